# Optimizing a Trainium2 kernel written in Bass

```python
import jax, jax.numpy as jnp
from jax import lax
import numpy as np

D_MODEL = 1024
BATCH = 2
SEQ = 8192
DEPTH = 4

CTX_LEN = 256
GRID_W = 64

BRANCH_W = D_MODEL // 2
N_BRANCH = 3
HEAD_DIM = 64
N_HEADS = BRANCH_W // HEAD_DIM
N_KV_HEADS = N_HEADS // 4
ATTN_W = N_HEADS * HEAD_DIM
KV_W = N_KV_HEADS * HEAD_DIM
WINDOW = 128
BLOCK = 128
ROPE_BASE = 10000.0
CONV_K = 31
LRU_BLOCKS = 8
LRU_BLOCK_DIM = BRANCH_W // LRU_BLOCKS
LRU_CONV_K = 4
LRU_C = 8.0
IN_SIZES = (ATTN_W, KV_W, KV_W, BRANCH_W, 2 * BRANCH_W, BRANCH_W, BRANCH_W, BRANCH_W)
IN_COLS = sum(IN_SIZES)
DEEPNORM_ALPHA = (2 * DEPTH) ** 0.25
DEEPNORM_BETA = (8 * DEPTH) ** -0.25
LN_EPS = 1e-6

kernel_name = "hybrid_gqa_conformer_rglru_diffusion"


def layer_norm(x):
    xf = x.astype(jnp.float32)
    mu = jnp.mean(xf, -1, keepdims=True)
    var = jnp.mean(jnp.square(xf - mu), -1, keepdims=True)
    return ((xf - mu) * lax.rsqrt(var + LN_EPS)).astype(x.dtype)


def split_in_proj(p):
    idx = np.cumsum(IN_SIZES)[:-1].tolist()
    return jnp.split(p, idx, axis=-1)


def axial_rope_tables(rows, dtype):
    quarter = HEAD_DIM // 4
    inv = ROPE_BASE ** (-jnp.arange(quarter, dtype=jnp.float32) / quarter)
    r = jnp.arange(rows, dtype=jnp.float32)
    col = jnp.arange(GRID_W, dtype=jnp.float32)
    n = rows * GRID_W
    ang_r = jnp.broadcast_to(r[:, None, None] * inv, (rows, GRID_W, quarter)).reshape(n, quarter)
    ang_c = jnp.broadcast_to(col[None, :, None] * inv, (rows, GRID_W, quarter)).reshape(n, quarter)
    ang = jnp.concatenate([ang_r, ang_r, ang_c, ang_c], -1)
    return jnp.cos(ang).astype(dtype), jnp.sin(ang).astype(dtype)


def apply_axial_rope(x, cos, sin):
    x1, x2, x3, x4 = jnp.split(x, 4, axis=-1)
    rot = jnp.concatenate([-x2, x1, -x4, x3], -1)
    return x * cos[:, None, :] + rot * sin[:, None, :]


def window_attention(q, k, v, k_ctx, v_ctx, sink):
    B, S = q.shape[:2]
    C = k_ctx.shape[1]
    nb = S // BLOCK
    G = N_HEADS // N_KV_HEADS
    scale = HEAD_DIM ** -0.5
    qb = q.reshape(B, nb, BLOCK, N_KV_HEADS, G, HEAD_DIM)
    pad = ((0, 0), (BLOCK, BLOCK), (0, 0), (0, 0))
    kp = jnp.pad(k, pad).reshape(B, nb + 2, BLOCK, N_KV_HEADS, HEAD_DIM)
    vp = jnp.pad(v, pad).reshape(B, nb + 2, BLOCK, N_KV_HEADS, HEAD_DIM)
    kb = jnp.concatenate([kp[:, :-2], kp[:, 1:-1], kp[:, 2:]], axis=2)
    vb = jnp.concatenate([vp[:, :-2], vp[:, 1:-1], vp[:, 2:]], axis=2)
    s_loc = jnp.einsum('bnqhgd,bnjhd->bnhgqj', qb, kb).astype(jnp.float32) * scale
    qpos = jnp.arange(S).reshape(nb, BLOCK)
    kpos = (jnp.arange(nb)[:, None] - 1) * BLOCK + jnp.arange(3 * BLOCK)[None, :]
    diff = kpos[:, None, :] - qpos[:, :, None]
    valid = (jnp.abs(diff) <= WINDOW) & (kpos[:, None, :] >= 0) & (kpos[:, None, :] < S)
    s_loc = jnp.where(valid[None, :, None, None], s_loc, -jnp.inf)
    s_ctx = jnp.einsum('bnqhgd,bchd->bnhgqc', qb, k_ctx).astype(jnp.float32) * scale
    s_sink = jnp.broadcast_to(sink.reshape(N_KV_HEADS, G)[None, None, :, :, None, None].astype(jnp.float32),
                              s_loc.shape[:-1] + (1,))
    p = jax.nn.softmax(jnp.concatenate([s_loc, s_ctx, s_sink], -1), axis=-1)
    p_loc = p[..., :3 * BLOCK].astype(v.dtype)
    p_ctx = p[..., 3 * BLOCK:3 * BLOCK + C].astype(v.dtype)
    out = jnp.einsum('bnhgqj,bnjhd->bnqhgd', p_loc, vb) + jnp.einsum('bnhgqc,bchd->bnqhgd', p_ctx, v_ctx)
    return out.reshape(B, S, ATTN_W)


def context_attention(q, k, v, sink):
    B, C = q.shape[:2]
    G = N_HEADS // N_KV_HEADS
    qg = q.reshape(B, C, N_KV_HEADS, G, HEAD_DIM)
    s = jnp.einsum('bqhgd,bchd->bhgqc', qg, k).astype(jnp.float32) * (HEAD_DIM ** -0.5)
    s_sink = jnp.broadcast_to(sink.reshape(N_KV_HEADS, G)[None, :, :, None, None].astype(jnp.float32),
                              s.shape[:-1] + (1,))
    p = jax.nn.softmax(jnp.concatenate([s, s_sink], -1), axis=-1)[..., :C].astype(v.dtype)
    return jnp.einsum('bhgqc,bchd->bqhgd', p, v).reshape(B, C, ATTN_W)


def depthwise_conv(x, w, b, pad):
    C = x.shape[-1]
    y = lax.conv_general_dilated(x, w[:, None, :].astype(x.dtype), window_strides=(1,), padding=[pad],
                                 dimension_numbers=('NWC', 'WIO', 'NWC'), feature_group_count=C)
    return y + b.astype(x.dtype)


def conformer_conv(u, dw_w, dw_b, ln_g, ln_b):
    a, g = jnp.split(u, 2, axis=-1)
    y = a * jax.nn.sigmoid(g)
    y = depthwise_conv(y, dw_w, dw_b, (CONV_K // 2, CONV_K // 2))
    y = layer_norm(y) * ln_g + ln_b
    return jax.nn.silu(y)


def block_diag_linear(x, w, b):
    xb = x.reshape(x.shape[:-1] + (LRU_BLOCKS, LRU_BLOCK_DIM))
    return jnp.einsum('btnd,nde->btne', xb, w).reshape(x.shape) + b


def linear_scan(a, u, h0):
    def combine(left, right):
        a_l, u_l = left
        a_r, u_r = right
        return a_l * a_r, a_r * u_l + u_r
    a_cum, h = lax.associative_scan(combine, (a, u), axis=1)
    return h + a_cum * h0[:, None, :]


def rglru_inputs(x, w_a, b_a, w_x, b_x, lam):
    r = jax.nn.sigmoid(block_diag_linear(x, w_a, b_a)).astype(jnp.float32)
    i = jax.nn.sigmoid(block_diag_linear(x, w_x, b_x)).astype(jnp.float32)
    log_a = -LRU_C * r * jax.nn.softplus(-lam.astype(jnp.float32))
    a = jnp.exp(log_a)
    u = jnp.sqrt(-jnp.expm1(2.0 * log_a)) * i * x.astype(jnp.float32)
    return a, u


def rglru_direction(x_ctx, x_lat, conv_w, conv_b, w_a, b_a, w_x, b_x, lam, reverse):
    pad = (0, LRU_CONV_K - 1) if reverse else (LRU_CONV_K - 1, 0)
    xc = depthwise_conv(x_ctx, conv_w, conv_b, pad)
    xl = depthwise_conv(x_lat, conv_w, conv_b, pad)
    if reverse:
        xc, xl = xc[:, ::-1], xl[:, ::-1]
    a_c, u_c = rglru_inputs(xc, w_a, b_a, w_x, b_x, lam)
    h_c = linear_scan(a_c, u_c, jnp.zeros((xc.shape[0], xc.shape[-1]), jnp.float32))
    a_l, u_l = rglru_inputs(xl, w_a, b_a, w_x, b_x, lam)
    h_l = linear_scan(a_l, u_l, h_c[:, -1])
    if reverse:
        h_c, h_l = h_c[:, ::-1], h_l[:, ::-1]
    return h_c, h_l


def merge_branches(h, ys, gs, w_branch, w_gate, b_gate, w_out):
    y = jnp.stack([yb.astype(h.dtype) * jax.nn.silu(gb) for yb, gb in zip(ys, gs)], axis=-2)
    proj = jnp.einsum('btnw,nwd->btnd', y, w_branch)
    g = jax.nn.sigmoid(h @ w_gate + b_gate).reshape(h.shape[:-1] + (N_BRANCH, D_MODEL))
    return jnp.sum(proj * g, axis=-2) @ w_out


def deepnorm_post(x, out, g, b):
    return layer_norm(DEEPNORM_ALPHA * x + out) * g + b


def setup_inputs(seed: int = 0) -> dict:
    key = jax.random.key(seed)
    ks = jax.random.split(key, 26)
    f32 = jnp.float32
    D = D_MODEL

    def nrm(k, shape, s):
        return jax.random.normal(k, shape, f32) * s

    u = jax.random.uniform(ks[18], (DEPTH, 2, BRANCH_W), f32, 0.9, 0.999)
    a0 = u ** (1.0 / LRU_C)
    lam = jnp.log(a0) - jnp.log1p(-a0)
    return {
        "x": nrm(ks[0], (BATCH, SEQ, D), 1.0),
        "c": nrm(ks[1], (BATCH, D), 1.0),
        "ctx": nrm(ks[2], (BATCH, CTX_LEN, D), 1.0),
        "c_ctx": nrm(ks[3], (D,), 1.0),
        "w_ada": nrm(ks[4], (DEPTH, D, 3 * D), D ** -0.5),
        "b_ada": nrm(ks[5], (DEPTH, 3 * D), 0.02),
        "w_in": nrm(ks[6], (DEPTH, D, IN_COLS), D ** -0.5),
        "attn_sink": nrm(ks[7], (DEPTH, N_HEADS), 0.5),
        "conv_dw_w": nrm(ks[8], (DEPTH, CONV_K, BRANCH_W), CONV_K ** -0.5),
        "conv_dw_b": nrm(ks[9], (DEPTH, BRANCH_W), 0.02),
        "conv_ln_g": 1.0 + nrm(ks[10], (DEPTH, BRANCH_W), 0.02),
        "conv_ln_b": nrm(ks[11], (DEPTH, BRANCH_W), 0.02),
        "lru_conv_w": nrm(ks[12], (DEPTH, 2, LRU_CONV_K, BRANCH_W), LRU_CONV_K ** -0.5),
        "lru_conv_b": nrm(ks[13], (DEPTH, 2, BRANCH_W), 0.02),
        "lru_w_a": nrm(ks[14], (DEPTH, 2, LRU_BLOCKS, LRU_BLOCK_DIM, LRU_BLOCK_DIM), LRU_BLOCK_DIM ** -0.5),
        "lru_b_a": nrm(ks[15], (DEPTH, 2, BRANCH_W), 0.02),
        "lru_w_x": nrm(ks[16], (DEPTH, 2, LRU_BLOCKS, LRU_BLOCK_DIM, LRU_BLOCK_DIM), LRU_BLOCK_DIM ** -0.5),
        "lru_b_x": nrm(ks[17], (DEPTH, 2, BRANCH_W), 0.02),
        "lru_lambda": lam,
        "w_branch": nrm(ks[19], (DEPTH, N_BRANCH, BRANCH_W, D), BRANCH_W ** -0.5 * DEEPNORM_BETA),
        "w_gate": nrm(ks[20], (DEPTH, D, N_BRANCH * D), D ** -0.5),
        "b_gate": nrm(ks[21], (DEPTH, N_BRANCH * D), 0.02),
        "w_out": nrm(ks[22], (DEPTH, D, D), D ** -0.5 * DEEPNORM_BETA),
        "ln_g": 1.0 + nrm(ks[23], (DEPTH, D), 0.02),
        "ln_b": nrm(ks[24], (DEPTH, D), 0.02),
    }


def reference(x, c, ctx, c_ctx, w_ada, b_ada, w_in, attn_sink, conv_dw_w, conv_dw_b, conv_ln_g, conv_ln_b,
              lru_conv_w, lru_conv_b, lru_w_a, lru_b_a, lru_w_x, lru_b_x, lru_lambda,
              w_branch, w_gate, b_gate, w_out, ln_g, ln_b):
    B, S = x.shape[:2]
    C = ctx.shape[1]
    rows = S // GRID_W
    cos, sin = axial_rope_tables(rows, x.dtype)
    for l in range(DEPTH):
        last = l == DEPTH - 1
        shift, scale, gate = jnp.split(jax.nn.silu(c) @ w_ada[l] + b_ada[l], 3, axis=-1)
        shift_c, scale_c, gate_c = jnp.split(jax.nn.silu(c_ctx) @ w_ada[l] + b_ada[l], 3, axis=-1)
        h = layer_norm(x) * (1.0 + scale[:, None]) + shift[:, None]
        hc = layer_norm(ctx) * (1.0 + scale_c) + shift_c
        q, k, v, g_att, u_conv, g_conv, x_lru, g_lru = split_in_proj(h @ w_in[l])
        qc, kc, vc, g_att_c, u_conv_c, g_conv_c, x_lru_c, g_lru_c = split_in_proj(hc @ w_in[l])
        q = apply_axial_rope(q.reshape(B, S, N_HEADS, HEAD_DIM), cos, sin)
        k = apply_axial_rope(k.reshape(B, S, N_KV_HEADS, HEAD_DIM), cos, sin)
        v = v.reshape(B, S, N_KV_HEADS, HEAD_DIM)
        kc = kc.reshape(B, C, N_KV_HEADS, HEAD_DIM)
        vc = vc.reshape(B, C, N_KV_HEADS, HEAD_DIM)
        y_att = window_attention(q, k, v, kc, vc, attn_sink[l])
        y_conv = conformer_conv(u_conv, conv_dw_w[l], conv_dw_b[l], conv_ln_g[l], conv_ln_b[l])
        hf_c, hf_l = rglru_direction(x_lru_c, x_lru, lru_conv_w[l, 0], lru_conv_b[l, 0], lru_w_a[l, 0],
                                     lru_b_a[l, 0], lru_w_x[l, 0], lru_b_x[l, 0], lru_lambda[l, 0], False)
        hb_c, hb_l = rglru_direction(x_lru_c, x_lru, lru_conv_w[l, 1], lru_conv_b[l, 1], lru_w_a[l, 1],
                                     lru_b_a[l, 1], lru_w_x[l, 1], lru_b_x[l, 1], lru_lambda[l, 1], True)
        y_lru = hf_l + hb_l
        out = merge_branches(h, (y_att, y_conv, y_lru), (g_att, g_conv, g_lru),
                             w_branch[l], w_gate[l], b_gate[l], w_out[l])
        x_new = deepnorm_post(x, gate[:, None] * out, ln_g[l], ln_b[l])
        if not last:
            y_att_c = context_attention(qc.reshape(B, C, N_HEADS, HEAD_DIM), kc, vc, attn_sink[l])
            y_conv_c = conformer_conv(u_conv_c, conv_dw_w[l], conv_dw_b[l], conv_ln_g[l], conv_ln_b[l])
            y_lru_c = hf_c + hb_c
            out_c = merge_branches(hc, (y_att_c, y_conv_c, y_lru_c), (g_att_c, g_conv_c, g_lru_c),
                                   w_branch[l], w_gate[l], b_gate[l], w_out[l])
            ctx = deepnorm_post(ctx, gate_c * out_c, ln_g[l], ln_b[l])
        x = x_new
    return x
```

```python
import numpy as np
import ml_dtypes
from contextlib import ExitStack
import concourse.bass as bass
import concourse.mybir as mybir
from concourse.bass_utils import run_bass_kernel_spmd

F32 = mybir.dt.float32
BF16 = mybir.dt.bfloat16
ALU = mybir.AluOpType
AF = mybir.ActivationFunctionType

DEPTH = 4
D = 1024
KC = 8
T_OWN = 2048
NT = 2560
NFULL = 2304
CTX0 = 2048
LH0 = 2304
RH0 = 2432
ALPHA = (2 * DEPTH) ** 0.25
EPS = 1e-6
C_Q, C_K, C_V, C_GATT, C_CA, C_CG, C_GCONV, C_XLRU, C_GLRU = 0, 512, 640, 768, 1280, 1792, 2304, 2816, 3328
FULL_R = [(0, 512), (512, 1024), (1024, 1536), (1536, 2048), (2048, 2304)]

ENG_NAMES = ("pe", "act", "dve", "pool", "sp")
SAME_ENG_SYNC = {"act", "dve", "pool"}


class Op:
    __slots__ = ("eng", "fn", "deps", "sig", "tok_sem", "tok_val", "dma", "idx", "cost", "lat", "seq", "alldeps")

    DEF_COST = {"pe": 1.5, "act": 0.5, "dve": 0.6, "pool": 1.0, "sp": 0.1}

    def __init__(self, eng, fn, dma, c=None):
        self.eng = eng
        self.fn = fn
        self.dma = dma
        if dma == "cc":
            self.cost, self.lat = 1.0, 100.0
        elif dma:
            self.cost = 0.1 if eng != "pool" else 0.8
            self.lat = self.cost + (c if c is not None else 3.0)
        else:
            self.cost = c if c is not None else self.DEF_COST[eng]
            self.lat = self.cost + 0.15
        if fn is None:
            self.cost = self.lat = 0.0
        self.deps = set()
        self.sig = False
        self.tok_sem = None
        self.tok_val = 0
        self.idx = -1


class Sched:
    NDMA = {"sp": 8, "pool": 6, "act": 4}

    def __init__(self, nc, stack):
        self.nc = nc
        self.ops = {e: [] for e in ENG_NAMES}
        self.all_ops = []
        self.reorder = True
        self.last_w = {}
        self.readers = {}
        self.esem = {}
        for e in ("pe", "act", "dve", "pool"):
            self.esem[e] = stack.enter_context(nc.semaphore("s_" + e))
        self.dsem = {}
        for e, n in self.NDMA.items():
            self.dsem[e] = [stack.enter_context(nc.semaphore("d_%s%d" % (e, i))) for i in range(n)]
        self.ccsem = stack.enter_context(nc.semaphore("s_cc"))
        self.ccval = 0
        self.dcount = {e: 0 for e in self.NDMA}
        self.dlast = {e: [None] * n for e, n in self.NDMA.items()}
        self.dval = {e: [0] * n for e, n in self.NDMA.items()}

    def add(self, eng, fn, reads=(), writes=(), dma=False, c=None, ne=False):
        if "EPOCH" not in writes and not ne:
            reads = list(reads) + ["EPOCH"]
        if dma == "cc":
            reads = list(reads) + ["CCORDER"]
            writes = list(writes) + ["CCORDER"]
        op = Op(eng, fn, dma, c)
        op.seq = len(self.all_ops)
        self.all_ops.append(op)
        deps = op.deps
        for k in reads:
            w = self.last_w.get(k)
            if w is not None:
                deps.add(w)
        for k in writes:
            w = self.last_w.get(k)
            if w is not None:
                deps.add(w)
            for r in self.readers.get(k, ()):
                deps.add(r)
        for k in reads:
            self.readers.setdefault(k, []).append(op)
        for k in writes:
            self.last_w[k] = op
            self.readers[k] = []
        if dma == "cc":
            self.ccval += 1
            op.tok_sem = self.ccsem
            op.tok_val = self.ccval
            op.sig = True
        elif dma:
            n = self.dcount[eng]
            self.dcount[eng] = n + 1
            slot = n % self.NDMA[eng]
            prev = self.dlast[eng][slot]
            if prev is not None:
                deps.add(prev)
            self.dlast[eng][slot] = op
            self.dval[eng][slot] += 16
            op.tok_sem = self.dsem[eng][slot]
            op.tok_val = self.dval[eng][slot]
            op.sig = True
        deps.discard(op)
        op.idx = len(self.ops[eng])
        self.ops[eng].append(op)
        return op

    def list_schedule(self):
        import heapq
        ops = self.all_ops
        ndep = [0] * len(ops)
        users = [[] for _ in ops]
        for op in ops:
            ndep[op.seq] = len(op.deps)
            for d in op.deps:
                users[d.seq].append(op)
        ready_t = [0.0] * len(ops)
        waiting = {e: [] for e in ENG_NAMES}
        avail = {e: [] for e in ENG_NAMES}
        free = {e: 0.0 for e in ENG_NAMES}
        for op in ops:
            if ndep[op.seq] == 0:
                heapq.heappush(waiting[op.eng], (0.0, op.seq))
        newq = {e: [] for e in ENG_NAMES}
        left = len(ops)
        while left:
            best = None
            for e in ENG_NAMES:
                w, a = waiting[e], avail[e]
                while w and w[0][0] <= free[e]:
                    heapq.heappush(a, heapq.heappop(w)[1])
                if a:
                    cand = (free[e], a[0], e, True)
                elif w:
                    cand = (w[0][0], w[0][1], e, False)
                else:
                    continue
                if best is None or cand[:2] < best[:2]:
                    best = cand
            t, sq, e, from_avail = best
            if from_avail:
                heapq.heappop(avail[e])
            else:
                heapq.heappop(waiting[e])
            op = ops[sq]
            newq[e].append(op)
            free[e] = t + op.cost
            done = t + op.lat
            left -= 1
            for u in users[sq]:
                if done > ready_t[u.seq]:
                    ready_t[u.seq] = done
                ndep[u.seq] -= 1
                if ndep[u.seq] == 0:
                    heapq.heappush(waiting[u.eng], (ready_t[u.seq], u.seq))
        for e in ENG_NAMES:
            assert len(newq[e]) == len(self.ops[e])
            self.ops[e] = newq[e]
            for i, op in enumerate(newq[e]):
                op.idx = i
        self.est_total = max(free.values())

    def finalize(self):
        if self.reorder:
            self.list_schedule()
        for e in ENG_NAMES:
            for op in self.ops[e]:
                best = {}
                keep = set()
                for d in op.deps:
                    if d.dma:
                        keep.add(d)
                    else:
                        if d.eng == op.eng and not op.dma and d.eng not in SAME_ENG_SYNC:
                            continue
                        b = best.get(d.eng)
                        if b is None or d.idx > b.idx:
                            best[d.eng] = d
                keep.update(best.values())
                op.deps = keep
                for d in keep:
                    d.sig = True
        for e in ("pe", "act", "dve", "pool"):
            c = 0
            for op in self.ops[e]:
                if op.dma:
                    continue
                if op.sig:
                    c += 1
                    op.tok_sem = self.esem[e]
                    op.tok_val = c

    def replay(self, e, eng):
        waited = {}
        for op in self.ops[e]:
            for d in sorted(op.deps, key=lambda d: (d.eng, d.idx)):
                key = id(d.tok_sem)
                if waited.get(key, 0) < d.tok_val:
                    eng.wait_ge(d.tok_sem, d.tok_val)
                    waited[key] = d.tok_val
            if op.fn is None:
                continue
            ins = op.fn(eng)
            if op.sig:
                if op.dma == "cc":
                    ins.then_inc(op.tok_sem)
                else:
                    ins.then_inc(op.tok_sem, 16 if op.dma else 1)

    def run(self):
        self.finalize()
        with self.nc.Block() as block:
            @block.tensor
            def _(eng):
                self.replay("pe", eng)

            @block.scalar
            def _(eng):
                self.replay("act", eng)

            @block.vector
            def _(eng):
                self.replay("dve", eng)

            @block.gpsimd
            def _(eng):
                self.replay("pool", eng)

            @block.sync
            def _(eng):
                self.replay("sp", eng)


class Alloc:
    def __init__(self, nc, base, limit):
        self.nc = nc
        self.off = base
        self.limit = limit
        self.n = 0

    def t(self, shape, dtype):
        esz = 2 if dtype == BF16 else 4
        per = esz
        for s in shape[1:]:
            per *= s
        per = (per + 63) // 64 * 64
        h = self.nc.alloc_sbuf_tensor_at("t%d_%d" % (self.n, self.off), list(shape), dtype, offset=self.off)
        self.n += 1
        self.off += per
        assert self.off <= self.limit, ("sbuf overflow", self.off, self.limit)
        return h


class _Stop(Exception):
    pass


def build_nc(nlayers=DEPTH, dbg=False, stop=None):
    def chk(name):
        if stop == name:
            raise _Stop()
    nc = bass.Bass("TRN2", target_bir_lowering=False)
    inp = lambda n, s, d=F32: nc.dram_tensor(n, list(s), d, kind="ExternalInput").ap()
    x_in = inp("x_in", [T_OWN, D])
    xh_in = inp("xh_in", [256, D])
    ctx_in = inp("ctx_in", [256, D])
    c_in = inp("c_in", [2, D])
    w_ada = inp("w_ada", [DEPTH, D, 3 * D]); b_ada = inp("b_ada", [DEPTH, 3 * D])
    w_in = inp("w_in", [DEPTH, D, 3840]); attn_sink = inp("attn_sink", [DEPTH, 8])
    conv_dw_w = inp("conv_dw_w", [DEPTH, 31, 512]); conv_dw_b = inp("conv_dw_b", [DEPTH, 512])
    conv_ln_g = inp("conv_ln_g", [DEPTH, 512]); conv_ln_b = inp("conv_ln_b", [DEPTH, 512])
    lru_conv_w = inp("lru_conv_w", [DEPTH, 2, 4, 512]); lru_conv_b = inp("lru_conv_b", [DEPTH, 2, 512])
    lru_w_a = inp("lru_w_a", [DEPTH, 2, 8, 64, 64]); lru_b_a = inp("lru_b_a", [DEPTH, 2, 512])
    lru_w_x = inp("lru_w_x", [DEPTH, 2, 8, 64, 64]); lru_b_x = inp("lru_b_x", [DEPTH, 2, 512])
    lru_lambda = inp("lru_lambda", [DEPTH, 2, 512])
    w_branch = inp("w_branch", [DEPTH, 3, 512, D]); w_gate = inp("w_gate", [DEPTH, D, 3 * D])
    b_gate = inp("b_gate", [DEPTH, 3 * D]); w_out = inp("w_out", [DEPTH, D, D])
    ln_g = inp("ln_g", [DEPTH, D]); ln_b = inp("ln_b", [DEPTH, D])
    cos_d = inp("cosT", [128, NT]); sin_d = inp("sinT", [128, NT])
    masks_d = inp("masks", [128, 512], BF16)
    perm_d = inp("perm", [128, 128], BF16)
    identb_d = inp("identb", [128, 128], BF16)
    identf_d = inp("identf", [128, 128])
    flags_d = inp("flags", [128, 8])
    sel_d = inp("sel", [2, 256])
    out = nc.dram_tensor("out", [T_OWN, D], F32, kind="ExternalOutput").ap()

    xs = nc.dram_tensor("xs", [NFULL, D], F32).ap()
    send = nc.dram_tensor("send", [256, D], F32).ap()
    gath = nc.dram_tensor("gath", [4 * 256, D], F32).ap()
    au = nc.dram_tensor("au", [16 * 128, 2048], F32).ap()
    csend = nc.dram_tensor("csend", [128, 16], F32).ap()
    cgath = nc.dram_tensor("cgath", [4 * 128, 16], F32).ap()
    RG = [[0, 1, 2, 3], [4, 5, 6, 7]]

    with ExitStack() as st:
        S = Sched(nc, st)
        A = S.add
        BASE = 16640
        LIM = 229376
        pa = Alloc(nc, BASE, LIM)
        hT = pa.t([128, KC, NT], BF16)
        Y_OFF = pa.off
        Y = pa.t([128, 12, NFULL], BF16)
        Y_END = pa.off
        identb = pa.t([128, 128], BF16); identf = pa.t([128, 128], F32); onesf = pa.t([128, 128], F32)
        onesb = pa.t([128, 128], BF16)
        perm = pa.t([128, 128], BF16); masks = pa.t([128, 4, 128], BF16)
        flags = pa.t([128, 8], F32); sel = pa.t([2, 256], F32)
        scT = pa.t([128, KC, 2], BF16); cT = pa.t([128, KC, 2], F32)
        grow = pa.t([2, D], F32)
        modc = pa.t([128, 16, 2], F32)
        pcols = pa.t([128, 256], F32)
        cA = pa.t([128, 8], F32)
        carry = pa.t([128, 8], F32)
        s0 = pa.t([128, 8], F32)
        ylru_ctx = pa.t([128, 4, 256], F32)
        WG = 256
        wb = [pa.t([128, KC, WG], BF16) for _ in range(2)]
        wk = pa.t([128, KC, 256], BF16)
        zeros = pa.t([128, 128], F32)
        bar_t = pa.t([128, 16], F32)
        csb = pa.t([128, 16], F32)
        cg = pa.t([128, 4, 16], F32)
        chain = pa.t([128, 2, 4, 4], F32)
        SCR = pa.off
        ps = [st.enter_context(nc.psum_tensor("ps%d" % i, [128, 512], F32)) for i in range(6)]
        psTs = [st.enter_context(nc.psum_tensor("psT%d" % i, [128, 1024], BF16)) for i in range(2)]
        bank = [0]
        dyn = {}

        def nb():
            bank[0] = (bank[0] + 1) % 5
            return bank[0]

        def barrier():
            A("pool", lambda e: e.memset(bar_t[:], 0.0), writes=["EPOCH"])

        bpool = {}

        def nbp(name, banks):
            i = bpool.get(name, 0)
            bpool[name] = i + 1
            return banks[i % len(banks)]

        def fence(keys):
            A("pe", lambda e: e.matmul(ps[5][:, 0:2], lhsT=onesb[:, 0:128], rhs=onesb[:, 0:2], start=True, stop=True),
              reads=list(keys) + ["onesb"], writes=list(keys))
        wbi = [0]

        for (t, d, k) in ((identb, identb_d, "identb"), (identf, identf_d, "identf"), (perm, perm_d, "perm"),
                          (flags, flags_d, "flags")):
            A("sp", lambda e, t=t, d=d: e.dma_start(out=t[:], in_=d[:, :]), writes=[k], dma=True)
        A("sp", lambda e: e.dma_start(out=masks[:], in_=masks_d.rearrange("p (m q) -> p m q", m=4)), writes=["masks"], dma=True)
        A("sp", lambda e: e.dma_start(out=sel[:], in_=sel_d[:, :]), writes=["sel"], dma=True)
        A("pool", lambda e: e.memset(onesf[:], 1.0), writes=["onesf"])
        A("pool", lambda e: e.memset(onesb[:], 1.0), writes=["onesb"])
        A("pool", lambda e: e.memset(zeros[:], 0.0), writes=["zeros"])

        for r_ in range(2):
            def f_cT(e, r_=r_):
                with nc.allow_non_contiguous_dma(reason="tiny one-off transpose load of c"):
                    return e.dma_start(out=cT[:, :, r_], in_=c_in[r_].rearrange("(kc p) -> p kc", p=128))
            A("sp", f_cT, writes=[("cT", r_)], dma=True)
        A("act", lambda e: e.activation(out=scT[:], in_=cT[:], func=AF.Silu), reads=[("cT", 0), ("cT", 1)], writes=["scT"])

        def load_w(src, c0, w):
            i = wbi[0] % 2
            wbi[0] += 1
            buf = wb[i]
            A("pool", lambda e: e.dma_start(out=buf[:, :, 0:w], in_=src[:, c0:c0 + w].rearrange("(kc p) c -> p kc c", p=128)),
              writes=[("wb", i)], dma=True, ne=True)
            return buf, ("wb", i)

        def proj_fm(buf, bkey, mc, n0, n1, extra_reads=(), pool=None):
            b = nb() if pool is None else nbp(*pool)

            def f(e):
                for kc in range(KC):
                    ins = e.matmul(ps[b][:, 0:n1 - n0], lhsT=buf[:, kc, mc * 128:(mc + 1) * 128], rhs=hT[:, kc, n0:n1],
                                   start=(kc == 0), stop=(kc == KC - 1))
                return ins
            A("pe", f, reads=[bkey, "hT"] + list(extra_reads), writes=[("ps", b)], c=0.1 + 8 * 0.27 * (n1 - n0) / 512.0, ne=True)
            return b

        def layer(l):
            last = (l == DEPTH - 1)
            full_r = FULL_R[:4] if last else FULL_R
            ntile_full = 16 if last else 18
            barrier()
            sa = Alloc(nc, SCR, LIM)
            modrows = sa.t([2, 3 * D], F32)
            brow = sa.t([2, 3 * D], F32)
            A("sp", lambda e: e.dma_start(out=brow[:], in_=b_ada[l:l + 1, :].partition_broadcast(2)), writes=["brow"], dma=True)
            for g in range(3 * D // WG):
                buf, bkey = load_w(w_ada[l], g * WG, WG)
                b = nb()

                def f(e, buf=buf, b=b):
                    for kc in range(KC):
                        ins = e.matmul(ps[b][0:2, 0:WG], lhsT=scT[:, kc, :], rhs=buf[:, kc, 0:WG], start=(kc == 0), stop=(kc == KC - 1))
                    return ins
                A("pe", f, reads=[bkey, "scT"], writes=[("ps", b)])
                A("dve", lambda e, b=b, g=g: e.tensor_tensor(out=modrows[:, g * WG:(g + 1) * WG], in0=ps[b][0:2, 0:WG],
                                                               in1=brow[:, g * WG:(g + 1) * WG], op=ALU.add),
                  reads=[("ps", b), "brow"], writes=[("modrows", g)])
            mr_all = [("modrows", g) for g in range(3 * D // WG)]
            b = nb()

            def f(e, b=b):
                for j in range(16):
                    ins = e.matmul(ps[b][:, 2 * j:2 * j + 2], lhsT=modrows[0:2, j * 128:(j + 1) * 128], rhs=identf[0:2, 0:2],
                                   start=True, stop=True)
                return ins
            A("pe", f, reads=mr_all + ["identf"], writes=[("ps", b)])
            fence([("ps", b)])
            A("dve", lambda e, b=b: e.tensor_copy(out=modc[:].rearrange("p a r -> p (a r)"), in_=ps[b][:, 0:32]),
              reads=[("ps", b)], writes=["modc"])
            A("dve", lambda e: e.tensor_scalar(out=modc[:, 8:16, :], in0=modc[:, 8:16, :], scalar1=1.0, scalar2=None, op0=ALU.add),
              reads=["modc"], writes=["modc"])
            A("act", lambda e: e.activation(out=grow[:], in_=modrows[:, 2 * D:3 * D], func=AF.Copy, scale=1.0 / ALPHA),
              reads=mr_all, writes=["grow"])
            prow = sa.t([128, 2, 128], F32)
            A("pool", lambda e: e.memset(prow[:], 0.0), writes=["prow"])
            plist = [
                (0, 0, conv_dw_w[l].rearrange("k (cc p) -> (k cc) p", p=128), 124),
                (0, 124, conv_dw_b[l].rearrange("(cc p) -> cc p", p=128), 4),
                (1, 0, conv_ln_g[l].rearrange("(cc p) -> cc p", p=128), 4),
                (1, 4, conv_ln_b[l].rearrange("(cc p) -> cc p", p=128), 4),
                (1, 8, lru_conv_w[l].rearrange("d k (cc p) -> (d k cc) p", p=128), 32),
                (1, 40, lru_conv_b[l].rearrange("d (cc p) -> (d cc) p", p=128), 8),
                (1, 48, lru_b_a[l].rearrange("d (cc p) -> (d cc) p", p=128), 8),
                (1, 56, lru_b_x[l].rearrange("d (cc p) -> (d cc) p", p=128), 8),
                (1, 64, lru_lambda[l].rearrange("d (cc p) -> (d cc) p", p=128), 8),
                (1, 72, b_gate[l].rearrange("(r p) -> r p", p=128), 24),
            ]
            for (s_, r0, src, n) in plist:
                A("sp", lambda e, s_=s_, r0=r0, src=src, n=n: e.dma_start(out=prow[r0:r0 + n, s_, :], in_=src),
                  reads=["prow"], writes=[("prow", s_, r0)], dma=True)
            b = nb()

            def f(e, b=b):
                for s_ in range(2):
                    ins = e.matmul(ps[b][:, s_ * 128:(s_ + 1) * 128], lhsT=prow[:, s_, :], rhs=identf[:], start=True, stop=True)
                return ins
            A("pe", f, reads=[("prow", s_, r0) for (s_, r0, _, _) in plist] + ["identf"], writes=[("ps", b)])
            fence([("ps", b)])
            A("dve", lambda e, b=b: e.tensor_copy(out=pcols[:], in_=ps[b][:, 0:256]), reads=[("ps", b)], writes=["pcols"])
            PB = 128
            A("act", lambda e: e.activation(out=cA[:], in_=pcols[:, PB + 64:PB + 72], func=AF.Exp, scale=-1.0), reads=["pcols"], writes=["cA"])
            A("act", lambda e: e.activation(out=cA[:], in_=cA[:], func=AF.Ln, bias=1.0, scale=1.0), reads=["cA"], writes=["cA"])
            A("dve", lambda e: e.tensor_scalar(out=cA[:], in0=cA[:], scalar1=-8.0, scalar2=None, op0=ALU.mult), reads=["cA"], writes=["cA"])

            if dbg and l == 0:
                dbg_mr = nc.dram_tensor("dbg_mr", [2, 3 * D], F32, kind="ExternalOutput").ap()
                dbg_mc = nc.dram_tensor("dbg_mc", [128, 32], F32, kind="ExternalOutput").ap()
                dbg_ct = nc.dram_tensor("dbg_ct", [128, 16], F32, kind="ExternalOutput").ap()
                dbg_pc = nc.dram_tensor("dbg_pc", [128, 256], F32, kind="ExternalOutput").ap()
                A("sp", lambda e: e.dma_start(out=dbg_mr[:, :], in_=modrows[:]), reads=mr_all, writes=["dbg_mr"], dma=True)
                A("sp", lambda e: e.dma_start(out=dbg_mc[:, :], in_=modc[:].rearrange("p a r -> p (a r)")), reads=["modc"], writes=["dbg_mc"], dma=True)
                A("sp", lambda e: e.dma_start(out=dbg_ct[:, :], in_=cT[:].rearrange("p a r -> p (a r)")), reads=["scT"], writes=["dbg_ct"], dma=True)
                A("sp", lambda e: e.dma_start(out=dbg_pc[:, :], in_=pcols[:]), reads=["pcols", "cA"], writes=["dbg_pc"], dma=True)
                A("sp", None, reads=["dbg_mr", "dbg_mc", "dbg_ct", "dbg_pc"])
            chk("adaln")
            ya = Alloc(nc, Y_OFF, Y_END)
            G = 5
            xt = [ya.t([128, D], F32) for _ in range(2 * G)]
            xn = [sa.t([128, D], BF16) for _ in range(2 * G)]
            stats = [sa.t([128, G, 2, 6], F32) for _ in range(2)]
            mv = [sa.t([128, G, 2], F32) for _ in range(2)]
            rstd = [sa.t([128, G], F32) for _ in range(2)]
            nbias = [sa.t([128, G], F32) for _ in range(2)]
            pTi = [0]
            for gi in range(4):
                s_ = gi % 2
                tiles = list(range(gi * G, gi * G + G))
                for j, ti in enumerate(tiles):
                    bi = s_ * G + j
                    if ti < 16:
                        src = (x_in if l == 0 else xs)[ti * 128:(ti + 1) * 128, :]
                        rk = [("xs", ti)]
                    elif ti < 18:
                        src = (ctx_in[(ti - 16) * 128:(ti - 15) * 128, :] if l == 0 else xs[ti * 128:(ti + 1) * 128, :])
                        rk = [("xs", ti)]
                    else:
                        rk = ["gath"]
                        src = xh_in[(ti - 18) * 128:(ti - 17) * 128, :] if l == 0 else None
                    if src is not None:
                        A("sp", lambda e, bi=bi, src=src: e.dma_start(out=xt[bi][:], in_=src), reads=rk, writes=[("xt", bi)], dma=True)
                    else:
                        def f(e, bi=bi, ti=ti):
                            if "L" not in dyn:
                                pid = e.partition_id()
                                dyn["L"] = ((pid + 3) % 4) * 256 + 128
                                dyn["R"] = ((pid + 1) % 4) * 256
                            row = dyn["L"] if ti == 18 else dyn["R"]
                            return e.dma_start(out=xt[bi][:], in_=gath[bass.ds(row, 128), :])
                        A("sp", f, reads=rk, writes=[("xt", bi)], dma=True)
                for j in range(G):
                    bi = s_ * G + j
                    for h_ in range(2):
                        A("dve", lambda e, bi=bi, j=j, h_=h_, s_=s_: e.bn_stats(out=stats[s_][:, j, h_, :], in_=xt[bi][:, h_ * 512:(h_ + 1) * 512]),
                          reads=[("xt", bi)], writes=[("st", s_, j, h_)])
                for j in range(G):
                    A("dve", lambda e, j=j, s_=s_: e.bn_aggr(out=mv[s_][:, j, :], in_=stats[s_][:, j, :, :]),
                      reads=[("st", s_, j, 0), ("st", s_, j, 1)], writes=[("mv", s_, j)])
                mvk = [("mv", s_, j) for j in range(G)]
                A("dve", lambda e, s_=s_: e.tensor_scalar(out=rstd[s_][:], in0=mv[s_][:, :, 1], scalar1=EPS, scalar2=None, op0=ALU.add),
                  reads=mvk, writes=[("rstd", s_)])
                A("act", lambda e, s_=s_: e.activation(out=rstd[s_][:], in_=rstd[s_][:], func=AF.Sqrt), reads=[("rstd", s_)], writes=[("rstd", s_)])
                A("dve", lambda e, s_=s_: e.reciprocal(out=rstd[s_][:], in_=rstd[s_][:]), reads=[("rstd", s_)], writes=[("rstd", s_)])
                A("dve", lambda e, s_=s_: e.scalar_tensor_tensor(out=nbias[s_][:], in0=mv[s_][:, :, 0], scalar=-1.0, in1=rstd[s_][:], op0=ALU.mult, op1=ALU.mult),
                  reads=mvk + [("rstd", s_)], writes=[("nbias", s_)], c=0.2)
                for j in range(G):
                    bi = s_ * G + j
                    A("act", lambda e, bi=bi, j=j, s_=s_: e.activation(out=xn[bi][:], in_=xt[bi][:], func=AF.Identity, scale=rstd[s_][:, j:j + 1], bias=nbias[s_][:, j:j + 1]),
                      reads=[("xt", bi), ("nbias", s_), ("rstd", s_)], writes=[("xn", bi)], c=1.0)
                for j, ti in enumerate(tiles):
                    bi = s_ * G + j
                    pTi[0] += 1
                    pi = pTi[0] % 2
                    psT = psTs[pi]

                    def f(e, bi=bi, psT=psT):
                        for kc in range(KC):
                            ins = e.transpose(out=psT[:, kc * 128:(kc + 1) * 128], in_=xn[bi][:, kc * 128:(kc + 1) * 128], identity=identb[:])
                        return ins
                    A("pe", f, reads=[("xn", bi), "identb"], writes=[("psT", pi)], c=0.9)
                    r = 1 if 16 <= ti < 18 else 0

                    def f(e, ti=ti, r=r, psT=psT):
                        for kc in range(0, 4):
                            ins = e.activation(out=hT[:, kc, ti * 128:(ti + 1) * 128], in_=psT[:, kc * 128:(kc + 1) * 128], func=AF.Identity,
                                               scale=modc[:, 8 + kc, r:r + 1], bias=modc[:, kc, r:r + 1])
                        return ins
                    A("act", f, reads=[("psT", pi), "modc"], writes=[("hTe", pi)])

                    def f(e, ti=ti, r=r, psT=psT):
                        for kc in range(4, 8):
                            ins = e.tensor_scalar(out=hT[:, kc, ti * 128:(ti + 1) * 128], in0=psT[:, kc * 128:(kc + 1) * 128],
                                                  scalar1=modc[:, 8 + kc, r:r + 1], scalar2=modc[:, kc, r:r + 1], op0=ALU.mult, op1=ALU.add)
                        return ins
                    A("dve", f, reads=[("psT", pi), "modc", ("hTe", pi)], writes=["hT"])

            chk("ln1")
            barrier()
            sa = Alloc(nc, SCR, LIM)
            ya = Alloc(nc, Y_OFF, Y_END)
            GW = 2364
            cv = sa.t([128, 4, NFULL], F32)
            glu = [sa.t([128, GW], BF16) for _ in range(2)]
            Dg = sa.t([128, 31, 128], BF16)
            sg_t = [sa.t([128, 512], F32) for _ in range(2)]
            conv1_end = sa.off
            XLW = 2316
            NO = 2310
            xl_pads = [sa.t([128, XLW], BF16) for _ in range(2)]
            al = [ya, sa]
            DgL = [al[d].t([128, 4, 128], BF16) for d in range(2)]
            xc = [al[d].t([128, NO + 2], F32) for d in range(2)]
            xcb = [al[d].t([128, NO + 2], BF16) for d in range(2)]
            rg = [ya.t([128, NO + 2], F32)] * 2
            ig = [ya.t([128, NO + 2], F32)] * 2
            tq = [ya.t([128, NO + 2], F32)] * 2
            hs = [ya.t([128, 2048], F32)] * 2
            hc = [ya.t([128, 256], F32)] * 2
            wbd = [[al[d].t([128, 128], BF16) for _ in range(2)] for d in range(2)]
            sumr = [ya.t([128, 1], F32)] * 2
            LB = ("lru", (3, 4))
            CB = ("cv1", (0, 1, 2))
            for i in range(2):
                A("pool", lambda e, i=i: e.memset(glu[i][:], 0.0), writes=[("glu", i)], c=2.0)
            sic = [0]
            CO_R = [(0, 512, 0), (512, 1024, 512), (1024, 1536, 1024), (1536, 2048, 1536), (2078, 2334, 2048)]

            def conv1(cc):
                gi = cc % 2
                bufa, ka = load_w(w_in[l], C_CA + cc * 128, 128)
                bufg, kg = load_w(w_in[l], C_CG + cc * 128, 128)
                ranges = [(n0, n1, (15 + n0 if n0 < CTX0 else 2093), None) for (n0, n1) in FULL_R]
                ranges.append((LH0 + 113, LH0 + 128, 0, 0))
                ranges.append((RH0, RH0 + 15, 2063, 1))
                for (n0, n1, p0, fl) in ranges:
                    n = n1 - n0
                    bg = proj_fm(bufg, kg, 0, n0, n1, pool=CB)
                    sic[0] += 1
                    si = sic[0] % 2
                    A("act", lambda e, bg=bg, si=si, n=n: e.activation(out=sg_t[si][:, 0:n], in_=ps[bg][:, 0:n], func=AF.Sigmoid),
                      reads=[("ps", bg)], writes=[("sg_t", si)])
                    ba = proj_fm(bufa, ka, 0, n0, n1, pool=CB)
                    if fl is None:
                        A("dve", lambda e, ba=ba, si=si, n=n, p0=p0, gi=gi: e.tensor_tensor(out=glu[gi][:, p0:p0 + n], in0=ps[ba][:, 0:n], in1=sg_t[si][:, 0:n], op=ALU.mult),
                          reads=[("ps", ba), ("sg_t", si)], writes=[("glu", gi)])
                    else:
                        A("dve", lambda e, ba=ba, si=si, n=n, p0=p0, gi=gi, fl=fl: e.scalar_tensor_tensor(out=glu[gi][:, p0:p0 + n], in0=ps[ba][:, 0:n], scalar=flags[:, fl:fl + 1],
                                                                                                 in1=sg_t[si][:, 0:n], op0=ALU.mult, op1=ALU.mult),
                          reads=[("ps", ba), ("sg_t", si), "flags"], writes=[("glu", gi)])

                def f(e, cc=cc):
                    for k in range(31):
                        ins = e.tensor_scalar(out=Dg[:, k, :], in0=identb[:], scalar1=pcols[:, k * 4 + cc:k * 4 + cc + 1], scalar2=None, op0=ALU.mult)
                    return ins
                A("dve", f, reads=["identb", "pcols"], writes=["Dg"], c=4.0)
                for (o0, o1, t0) in CO_R:
                    b = nbp(*CB)

                    def f(e, b=b, o0=o0, o1=o1, gi=gi):
                        for k in range(31):
                            ins = e.matmul(ps[b][:, 0:o1 - o0], lhsT=Dg[:, k, :], rhs=glu[gi][:, o0 + k:o1 + k], start=(k == 0), stop=(k == 30))
                        return ins
                    A("pe", f, reads=["Dg", ("glu", gi)], writes=[("ps", b)], c=0.1 + 31 * 0.27 * (o1 - o0) / 512.0)
                    A("act", lambda e, b=b, o0=o0, o1=o1, t0=t0, cc=cc: e.activation(out=cv[:, cc, t0:t0 + o1 - o0], in_=ps[b][:, 0:o1 - o0], func=AF.Identity,
                                                                              bias=pcols[:, 124 + cc:125 + cc], scale=1.0),
                      reads=[("ps", b), "pcols"], writes=[("cv", cc)])

            for i_ in range(2):
                A("pool", lambda e, i_=i_: e.memset(xl_pads[i_][:], 0.0), writes=[("xl_pad", i_)], c=4.0)
            O_R = [(0, 512), (512, 1024), (1024, 1536), (1536, 2048), (2054, 2310)]
            for cc in range(4):
                xi = cc % 2
                xl_pad = xl_pads[xi]
                xk = ("xl_pad", xi)
                buf, bkey = load_w(w_in[l], C_XLRU + cc * 128, 128)
                for (n0, n1) in FULL_R:
                    b = proj_fm(buf, bkey, 0, n0, n1, pool=LB)
                    dst = xl_pad[:, 3 + n0:3 + n1] if n0 < CTX0 else xl_pad[:, 2057:2313]
                    A("act", lambda e, b=b, dst=dst, n=n1 - n0: e.activation(out=dst, in_=ps[b][:, 0:n], func=AF.Copy),
                      reads=[("ps", b)], writes=[xk])
                b = proj_fm(buf, bkey, 0, LH0 + 125, LH0 + 131, pool=LB)
                A("dve", lambda e, b=b, xl_pad=xl_pad: e.tensor_scalar(out=xl_pad[:, 0:3], in0=ps[b][:, 0:3], scalar1=flags[:, 0:1], scalar2=None, op0=ALU.mult),
                  reads=[("ps", b), "flags"], writes=[xk], c=0.2)
                A("dve", lambda e, b=b, xl_pad=xl_pad: e.tensor_scalar(out=xl_pad[:, 2051:2054], in0=ps[b][:, 3:6], scalar1=flags[:, 1:2], scalar2=None, op0=ALU.mult),
                  reads=[("ps", b), "flags"], writes=[xk], c=0.2)
                for d in range(2):
                    sh = 0 if d == 0 else 3
                    wcols = [pcols[:, PB + 8 + d * 16 + k * 4 + cc:PB + 9 + d * 16 + k * 4 + cc] for k in range(4)]
                    bcol = pcols[:, PB + 40 + d * 4 + cc:PB + 41 + d * 4 + cc]
                    xc_, xcb_, rg_, ig_, tq_, hs_, hc_, sumr_ = xc[d], xcb[d], rg[d], ig[d], tq[d], hs[d], hc[d], sumr[d]
                    kxc, kxcb, krg, kig, ktq, khs, khc, ksr = ("xc", d), ("xcb", d), ("rg", 0), ("ig", 0), ("tq", 0), ("hs", 0), ("hc", 0), ("sumr", 0)
                    dg_ = DgL[d]

                    def f(e, dg_=dg_, wcols=wcols):
                        for k in range(4):
                            ins = e.tensor_scalar(out=dg_[:, k, :], in0=identb[:], scalar1=wcols[k], scalar2=None, op0=ALU.mult)
                        return ins
                    A("dve", f, reads=["identb", "pcols"], writes=[("DgL", d)], c=0.6)
                    for (o0, o1) in O_R:
                        b = nbp(*LB)

                        def f(e, b=b, o0=o0, o1=o1, sh=sh, dg_=dg_, xl_pad=xl_pad):
                            for k in range(4):
                                ins = e.matmul(ps[b][:, 0:o1 - o0], lhsT=dg_[:, k, :], rhs=xl_pad[:, sh + o0 + k:sh + o1 + k], start=(k == 0), stop=(k == 3))
                            return ins
                        A("pe", f, reads=[("DgL", d), xk], writes=[("ps", b)], c=0.1 + 4 * 0.27 * (o1 - o0) / 512.0)
                        A("act", lambda e, b=b, o0=o0, o1=o1, xc_=xc_, bcol=bcol: e.activation(out=xc_[:, o0:o1], in_=ps[b][:, 0:o1 - o0], func=AF.Identity, bias=bcol, scale=1.0),
                          reads=[("ps", b), "pcols"], writes=[kxc], c=0.6)
                    A("dve", lambda e, xc_=xc_, xcb_=xcb_: e.tensor_copy(out=xcb_[:, 0:NO], in_=xc_[:, 0:NO]), reads=[kxc], writes=[kxcb], c=1.5)
                    for wi, (wsrc, dstg, kdst, bo) in enumerate(((lru_w_a, rg_, krg, 48), (lru_w_x, ig_, kig, 56))):
                        wt = wbd[d][wi]
                        A("pool", lambda e, wt=wt: e.memset(wt[:], 0.0), writes=[("wbd", d, wi), ("wbdd", d, wi, 0), ("wbdd", d, wi, 1)], c=0.3)
                        for hb in range(2):
                            A("pool", lambda e, wt=wt, hb=hb, wsrc=wsrc, d=d, cc=cc: e.dma_start(out=wt[hb * 64:(hb + 1) * 64, hb * 64:(hb + 1) * 64],
                                                                                          in_=wsrc[l, d, 2 * cc + hb, :, :]),
                              reads=[("wbd", d, wi)], writes=[("wbdd", d, wi, hb)], dma=True)
                        bias = pcols[:, PB + bo + d * 4 + cc:PB + bo + 1 + d * 4 + cc]
                        for (o0, o1) in O_R:
                            b = nbp(*LB)
                            A("pe", lambda e, b=b, wt=wt, o0=o0, o1=o1, xcb_=xcb_: e.matmul(ps[b][:, 0:o1 - o0], lhsT=wt[:], rhs=xcb_[:, o0:o1], start=True, stop=True),
                              reads=[("wbdd", d, wi, 0), ("wbdd", d, wi, 1), kxcb], writes=[("ps", b)], c=0.3)
                            A("act", lambda e, b=b, dstg=dstg, o0=o0, o1=o1, bias=bias: e.activation(out=dstg[:, o0:o1], in_=ps[b][:, 0:o1 - o0], func=AF.Sigmoid,
                                                                                                  bias=bias, scale=1.0),
                              reads=[("ps", b), "pcols"], writes=[kdst], c=0.6)
                    cAc = cA[:, d * 4 + cc:d * 4 + cc + 1]
                    A("dve", lambda e, rg_=rg_, sumr_=sumr_: e.reduce_sum(out=sumr_[:], in_=rg_[:, 0:2048], axis=mybir.AxisListType.X), reads=[krg], writes=[ksr], c=2.2)
                    A("act", lambda e, cAc=cAc, d=d, cc=cc, sumr_=sumr_: e.activation(out=csb[:, d * 8 + cc * 2:d * 8 + cc * 2 + 1], in_=sumr_[:], func=AF.Exp, scale=cAc),
                      reads=[ksr, "cA"], writes=[("csb", d, cc, 0)], c=0.2)
                    A("act", lambda e, cAc=cAc, rg_=rg_: e.activation(out=rg_[:, 0:NO], in_=rg_[:, 0:NO], func=AF.Exp, scale=cAc), reads=[krg, "cA"], writes=[krg], c=2.0)
                    A("pool", lambda e, rg_=rg_, tq_=tq_: e.tensor_tensor(out=tq_[:, 0:NO], in0=rg_[:, 0:NO], in1=rg_[:, 0:NO], op=ALU.mult), reads=[krg], writes=[ktq], c=4.5)
                    A("act", lambda e, tq_=tq_: e.activation(out=tq_[:, 0:NO], in_=tq_[:, 0:NO], func=AF.Sqrt, scale=-1.0, bias=1.0), reads=[ktq], writes=[ktq], c=2.0)
                    A("pool", lambda e, ig_=ig_, xc_=xc_: e.tensor_tensor(out=ig_[:, 0:NO], in0=ig_[:, 0:NO], in1=xc_[:, 0:NO], op=ALU.mult), reads=[kig, kxc], writes=[kig], c=4.5)
                    A("dve", lambda e, ig_=ig_, tq_=tq_: e.tensor_tensor(out=ig_[:, 0:NO], in0=ig_[:, 0:NO], in1=tq_[:, 0:NO], op=ALU.mult), reads=[kig, ktq], writes=[kig], c=2.5)
                    if d == 0:
                        A("dve", lambda e, rg_=rg_, ig_=ig_, hc_=hc_: e.tensor_tensor_scan(out=hc_[:], data0=rg_[:, 2054:2310], data1=ig_[:, 2054:2310], initial=0.0,
                                                                 op0=ALU.mult, op1=ALU.add), reads=[krg, kig], writes=[khc], c=0.7)
                        A("dve", lambda e, rg_=rg_, ig_=ig_, hs_=hs_: e.tensor_tensor_scan(out=hs_[:], data0=rg_[:, 0:2048], data1=ig_[:, 0:2048], initial=0.0,
                                                                 op0=ALU.mult, op1=ALU.add), reads=[krg, kig], writes=[khs], c=4.4)
                        A("pool", lambda e, cc=cc, hc_=hc_: e.tensor_copy(out=ylru_ctx[:, cc, :], in_=hc_[:]), reads=[khc], writes=[("ylc", cc)], c=0.6)
                        A("act", lambda e, cc=cc, hc_=hc_: e.activation(out=s0[:, cc:cc + 1], in_=hc_[:, 255:256], func=AF.Copy), reads=[khc], writes=[("s0", 0, cc)], c=0.2)
                        A("act", lambda e, cc=cc, hs_=hs_: e.activation(out=csb[:, cc * 2 + 1:cc * 2 + 2], in_=hs_[:, 2047:2048], func=AF.Copy),
                          reads=[khs], writes=[("csb", 0, cc, 1)], c=0.2)
                    else:
                        A("dve", lambda e, rg_=rg_, ig_=ig_, hc_=hc_: e.tensor_tensor_scan(out=hc_[:, ::-1], data0=rg_[:, 2309:2053:-1], data1=ig_[:, 2309:2053:-1], initial=0.0,
                                                                 op0=ALU.mult, op1=ALU.add), reads=[krg, kig], writes=[khc], c=0.7)
                        A("dve", lambda e, rg_=rg_, ig_=ig_, hs_=hs_: e.tensor_tensor_scan(out=hs_[:, ::-1], data0=rg_[:, 2047::-1], data1=ig_[:, 2047::-1], initial=0.0,
                                                                 op0=ALU.mult, op1=ALU.add), reads=[krg, kig], writes=[khs], c=4.4)
                        A("pool", lambda e, cc=cc, hc_=hc_: e.tensor_tensor(out=ylru_ctx[:, cc, :], in0=ylru_ctx[:, cc, :], in1=hc_[:], op=ALU.add),
                          reads=[khc, ("ylc", cc)], writes=[("ylc", cc)], c=0.6)
                        A("act", lambda e, cc=cc, hc_=hc_: e.activation(out=s0[:, 4 + cc:5 + cc], in_=hc_[:, 0:1], func=AF.Copy), reads=[khc], writes=[("s0", 1, cc)], c=0.2)
                        A("act", lambda e, cc=cc, hs_=hs_: e.activation(out=csb[:, 8 + cc * 2 + 1:8 + cc * 2 + 2], in_=hs_[:, 0:1], func=AF.Copy),
                          reads=[khs], writes=[("csb", 1, cc, 1)], c=0.2)
                    cmb = d * 4 + cc
                    A("sp", lambda e, cmb=cmb, rg_=rg_: e.dma_start(out=au[(cmb * 2) * 128:(cmb * 2 + 1) * 128, :], in_=rg_[:, 0:2048]), reads=[krg], writes=[("au", cmb, 0)], dma=True, c=5.0)
                    A("sp", lambda e, cmb=cmb, ig_=ig_: e.dma_start(out=au[(cmb * 2 + 1) * 128:(cmb * 2 + 2) * 128, :], in_=ig_[:, 0:2048]), reads=[kig], writes=[("au", cmb, 1)], dma=True, c=5.0)
                conv1(cc)
            csb_keys = [("csb", d, cc, j) for d in range(2) for cc in range(4) for j in range(2)]
            A("sp", lambda e: e.dma_start(out=csend[:, :], in_=csb[:]), reads=csb_keys, writes=["csend"], dma=True, ne=True)
            A("pool", lambda e: e.collective_compute("AllGather", ALU.bypass, replica_groups=RG, ins=[csend.opt()], outs=[cgath.opt()]),
              reads=["csend"], writes=["cgath"], dma="cc", ne=True)
            A("sp", lambda e: e.dma_start(out=cg[:], in_=cgath.rearrange("(r p) f -> p r f", p=128)), reads=["cgath"], writes=["cg"], dma=True, ne=True)
            cgv = cg[:].rearrange("p r (d c j) -> p r d c j", d=2, c=4)
            A("dve", lambda e: e.tensor_copy(out=chain[:, 0, 0, :], in_=s0[:, 0:4]), reads=[("s0", 0, c_) for c_ in range(4)], writes=[("chain", 0, 0)], ne=True)
            for r in range(3):
                A("dve", lambda e, r=r: e.tensor_tensor(out=chain[:, 0, r + 1, :], in0=chain[:, 0, r, :], in1=cgv[:, r, 0, :, 0], op=ALU.mult),
                  reads=[("chain", 0, r), "cg"], writes=[("chain", 0, r + 1)], ne=True)
                A("dve", lambda e, r=r: e.tensor_tensor(out=chain[:, 0, r + 1, :], in0=chain[:, 0, r + 1, :], in1=cgv[:, r, 0, :, 1], op=ALU.add),
                  reads=[("chain", 0, r + 1), "cg"], writes=[("chain", 0, r + 1)], ne=True)
            A("dve", lambda e: e.tensor_copy(out=chain[:, 1, 3, :], in_=s0[:, 4:8]), reads=[("s0", 1, c_) for c_ in range(4)], writes=[("chain", 1, 3)], ne=True)
            for r in (3, 2, 1):
                A("dve", lambda e, r=r: e.tensor_tensor(out=chain[:, 1, r - 1, :], in0=chain[:, 1, r, :], in1=cgv[:, r, 1, :, 0], op=ALU.mult),
                  reads=[("chain", 1, r), "cg"], writes=[("chain", 1, r - 1)], ne=True)
                A("dve", lambda e, r=r: e.tensor_tensor(out=chain[:, 1, r - 1, :], in0=chain[:, 1, r - 1, :], in1=cgv[:, r, 1, :, 1], op=ALU.add),
                  reads=[("chain", 1, r - 1), "cg"], writes=[("chain", 1, r - 1)], ne=True)
            for d in range(2):
                ck = [("chain", d, r) for r in range(4)]
                A("dve", lambda e, d=d: e.tensor_scalar(out=carry[:, d * 4:d * 4 + 4], in0=chain[:, d, 0, :], scalar1=flags[:, 2:3], scalar2=None, op0=ALU.mult),
                  reads=ck + ["flags"], writes=[("carry", d)], ne=True)
                for r in range(1, 4):
                    A("dve", lambda e, d=d, r=r: e.scalar_tensor_tensor(out=carry[:, d * 4:d * 4 + 4], in0=chain[:, d, r, :], scalar=flags[:, 2 + r:3 + r],
                                                                    in1=carry[:, d * 4:d * 4 + 4], op0=ALU.mult, op1=ALU.add),
                      reads=ck + ["flags", ("carry", d)], writes=[("carry", d)], ne=True)

            chk("lru1")
            barrier()
            sa = Alloc(nc, conv1_end, LIM)
            sq = [sa.t([128, 512], F32) for _ in range(2)]
            mean_t = [sa.t([128, 512], F32) for _ in range(2)]
            msq = [sa.t([128, 512], F32) for _ in range(2)]
            var_t = [sa.t([128, 512], F32) for _ in range(2)]
            tt = [sa.t([128, 512], F32) for _ in range(4)]
            cvk = [("cv", cc) for cc in range(4)]
            for ri, (n0, n1) in enumerate(full_r):
                n = n1 - n0
                rp = ri % 2
                mean_r, msq_r, var_r = mean_t[rp], msq[rp], var_t[rp]
                kme, kms, kva = ("mean_t", rp), ("msq", rp), ("var_t", rp)
                b1 = nb()

                def f(e, b1=b1, n0=n0, n1=n1):
                    for cc in range(4):
                        ins = e.matmul(ps[b1][:, 0:n1 - n0], lhsT=onesf[:], rhs=cv[:, cc, n0:n1], start=(cc == 0), stop=(cc == 3))
                    return ins
                A("pe", f, reads=cvk + ["onesf"], writes=[("ps", b1)], c=4.5)
                b2 = nb()
                for cc in range(4):
                    qi = cc % 2
                    A("act", lambda e, qi=qi, cc=cc, n0=n0, n1=n1: e.activation(out=sq[qi][:, 0:n1 - n0], in_=cv[:, cc, n0:n1], func=AF.Square),
                      reads=[("cv", cc)], writes=[("sq", qi)])
                    A("pe", lambda e, qi=qi, cc=cc, b2=b2, n=n: e.matmul(ps[b2][:, 0:n], lhsT=onesf[:], rhs=sq[qi][:, 0:n], start=(cc == 0), stop=(cc == 3)),
                      reads=[("sq", qi), "onesf"], writes=[("ps", b2)], c=1.2)
                fence([("ps", b1), ("ps", b2)])
                A("act", lambda e, b1=b1, n=n, mean_r=mean_r: e.activation(out=mean_r[:, 0:n], in_=ps[b1][:, 0:n], func=AF.Copy, scale=1.0 / 512), reads=[("ps", b1)], writes=[kme])
                A("pool", lambda e, n=n, mean_r=mean_r, msq_r=msq_r: e.tensor_tensor(out=msq_r[:, 0:n], in0=mean_r[:, 0:n], in1=mean_r[:, 0:n], op=ALU.mult), reads=[kme], writes=[kms])
                A("dve", lambda e, b2=b2, n=n, var_r=var_r, msq_r=msq_r: e.scalar_tensor_tensor(out=var_r[:, 0:n], in0=ps[b2][:, 0:n], scalar=1.0 / 512, in1=msq_r[:, 0:n], op0=ALU.mult, op1=ALU.subtract),
                  reads=[("ps", b2), kms], writes=[kva])
                A("dve", lambda e, n=n, var_r=var_r: e.tensor_scalar(out=var_r[:, 0:n], in0=var_r[:, 0:n], scalar1=EPS, scalar2=None, op0=ALU.add), reads=[kva], writes=[kva])
                A("act", lambda e, n=n, var_r=var_r: e.activation(out=var_r[:, 0:n], in_=var_r[:, 0:n], func=AF.Sqrt), reads=[kva], writes=[kva])
                A("dve", lambda e, n=n, var_r=var_r: e.reciprocal(out=var_r[:, 0:n], in_=var_r[:, 0:n]), reads=[kva], writes=[kva])
                for cc in range(4):
                    ti_ = cc
                    A("dve", lambda e, ti_=ti_, cc=cc, n0=n0, n1=n1, n=n, mean_r=mean_r: e.tensor_tensor(out=tt[ti_][:, 0:n], in0=cv[:, cc, n0:n1], in1=mean_r[:, 0:n], op=ALU.subtract),
                      reads=[("cv", cc), kme], writes=[("tt", ti_)])
                    A("dve", lambda e, ti_=ti_, cc=cc, n=n, var_r=var_r: e.scalar_tensor_tensor(out=tt[ti_][:, 0:n], in0=tt[ti_][:, 0:n], scalar=pcols[:, PB + cc:PB + cc + 1], in1=var_r[:, 0:n],
                                                                               op0=ALU.mult, op1=ALU.mult),
                      reads=[("tt", ti_), kva, "pcols"], writes=[("tt", ti_)])
                    A("act", lambda e, ti_=ti_, cc=cc, n0=n0, n1=n1, n=n: e.activation(out=Y[:, 4 + cc, n0:n1], in_=tt[ti_][:, 0:n], func=AF.Silu, bias=pcols[:, PB + 4 + cc:PB + 5 + cc], scale=1.0),
                      reads=[("tt", ti_), "pcols"], writes=[("Y", 4 + cc)])
            gt = [sa.t([128, 512], F32) for _ in range(2)]

            def gate_branch(bidx, col0):
                for cc in range(4):
                    buf, bkey = load_w(w_in[l], col0 + cc * 128, 128)
                    for (n0, n1) in full_r:
                        n = n1 - n0
                        b = proj_fm(buf, bkey, 0, n0, n1)
                        gi_ = b % 2
                        A("act", lambda e, b=b, gi_=gi_, n=n: e.activation(out=gt[gi_][:, 0:n], in_=ps[b][:, 0:n], func=AF.Silu), reads=[("ps", b)], writes=[("gt", gi_)])
                        A("dve", lambda e, gi_=gi_, n0=n0, n1=n1, n=n, cc=cc: e.tensor_tensor(out=Y[:, bidx * 4 + cc, n0:n1], in0=Y[:, bidx * 4 + cc, n0:n1], in1=gt[gi_][:, 0:n], op=ALU.mult),
                          reads=[("gt", gi_), ("Y", bidx * 4 + cc)], writes=[("Y", bidx * 4 + cc)])
            gate_branch(1, C_GCONV)

            chk("conv")
            barrier()
            sa = Alloc(nc, SCR, LIM)
            gt = [sa.t([128, 512], F32) for _ in range(2)]
            qT = sa.t([128, 4, NFULL], BF16)
            kT = sa.t([128, 2, NT], BF16)
            vS = sa.t([128, 20, 2, 128], BF16)
            cs_off = [sa.off, sa.off + NT * 4]
            cosT = sa.t([128, NT], F32); sinT = sa.t([128, NT], F32)
            E = [nc.alloc_sbuf_tensor_at("E%d_%d" % (i_, l), [128, 5, 2, 2, 2, 128], BF16, offset=cs_off[i_]) for i_ in range(2)]
            ekeys_all = [[("E", i_, j0, hp, g) for j0 in (0, 2, 4) for hp in range(2) for g in range(2)] + [("Em", i_, j) for j in range(5)] for i_ in range(2)]
            t1 = [sa.t([128, 512], F32) for _ in range(3)]
            qb = [sa.t([128, 512], BF16) for _ in range(3)]
            rpi = [0]
            den = sa.t([128, 512], F32)
            sinkbc = sa.t([128, 4, 128], F32)
            sk = sa.t([128, 8], F32)
            A("sp", lambda e: e.dma_start(out=cosT[:], in_=cos_d[:, :]), writes=["cosT"] + ekeys_all[0], dma=True)
            A("sp", lambda e: e.dma_start(out=sinT[:], in_=sin_d[:, :]), writes=["sinT"] + ekeys_all[1], dma=True)
            A("sp", lambda e: e.dma_start(out=sk[:], in_=attn_sink[l:l + 1, :].partition_broadcast(128)), writes=["sk"], dma=True)
            A("act", lambda e: e.activation(out=sk[:], in_=sk[:], func=AF.Exp), reads=["sk"], writes=["sk"])

            def f(e):
                for c_ in range(4):
                    for hp in range(2):
                        ins = e.tensor_scalar(out=sinkbc[hp * 64:(hp + 1) * 64, c_, :], in0=zeros[hp * 64:(hp + 1) * 64, :],
                                              scalar1=sk[hp * 64:(hp + 1) * 64, 2 * c_ + hp:2 * c_ + hp + 1], scalar2=None, op0=ALU.add)
                return ins
            A("dve", f, reads=["sk", "zeros"], writes=["sinkbc"])
            A("pool", lambda e: e.memset(vS[:], 0.0), writes=["vS"])

            def rope(b, dst, n0, n1):
                n = n1 - n0
                rpi[0] += 1
                i = rpi[0] % 3
                A("act", lambda e: e.activation(out=qb[i][:, 0:n], in_=ps[b][:, 0:n], func=AF.Copy), reads=[("ps", b)], writes=[("qb", i)])
                A("dve", lambda e: e.tensor_tensor(out=t1[i][:, 0:n], in0=ps[b][:, 0:n], in1=cosT[:, n0:n1], op=ALU.mult), reads=[("ps", b), "cosT", ("qb", i)], writes=[("t1", i)])
                b2 = nb()
                A("pe", lambda e: e.matmul(ps[b2][:, 0:n], lhsT=perm[:], rhs=qb[i][:, 0:n], start=True, stop=True), reads=[("qb", i), "perm"], writes=[("ps", b2)])
                A("dve", lambda e: e.tensor_tensor(out=qb[i][:, 0:n], in0=ps[b2][:, 0:n], in1=sinT[:, n0:n1], op=ALU.mult), reads=[("ps", b2), "sinT"], writes=[("qb", i)])
                A("pool", lambda e: e.tensor_tensor(out=dst, in0=qb[i][:, 0:n], in1=t1[i][:, 0:n], op=ALU.add), reads=[("qb", i), ("t1", i)], writes=["qkT"])
            for c_ in range(4):
                buf, bkey = load_w(w_in[l], C_Q + c_ * 128, 128)
                for (n0, n1) in full_r:
                    b = proj_fm(buf, bkey, 0, n0, n1)
                    rope(b, qT[:, c_, n0:n1], n0, n1)
            for g in range(2):
                for dup in range(2):
                    A("pool", lambda e, g=g, dup=dup: e.dma_start(out=wk[:, :, g * 128 + dup * 64:g * 128 + dup * 64 + 64],
                                                                in_=w_in[l][:, C_K + g * 64:C_K + g * 64 + 64].rearrange("(kc p) c -> p kc c", p=128)),
                      writes=[("wk", g, dup)], dma=True)
            for g in range(2):
                for (n0, n1) in FULL_R + [(LH0, NT)]:
                    b = nb()

                    def f(e, b=b, g=g, n0=n0, n1=n1):
                        for kc in range(KC):
                            ins = e.matmul(ps[b][:, 0:n1 - n0], lhsT=wk[:, kc, g * 128:(g + 1) * 128], rhs=hT[:, kc, n0:n1], start=(kc == 0), stop=(kc == KC - 1))
                        return ins
                    A("pe", f, reads=[("wk", g, 0), ("wk", g, 1), "hT"], writes=[("ps", b)])
                    rope(b, kT[:, g, n0:n1], n0, n1)
            bufv, kv = load_w(w_in[l], C_V, 128)
            for ti in range(20):
                b = nb()

                def f(e, b=b, ti=ti):
                    for kc in range(KC):
                        ins = e.matmul(ps[b][:, 0:128], lhsT=hT[:, kc, ti * 128:(ti + 1) * 128], rhs=bufv[:, kc, 0:128], start=(kc == 0), stop=(kc == KC - 1))
                    return ins
                A("pe", f, reads=[kv, "hT"], writes=[("ps", b)])
                A("act", lambda e, b=b, ti=ti: e.activation(out=vS[:, ti, :, 0:64], in_=ps[b][:, 0:128].rearrange("p (g d) -> p g d", g=2), func=AF.Copy),
                  reads=[("ps", b), "vS"], writes=[("vS", ti)])
            vO = sa.t([128, 20, 2, 128], BF16)
            A("pool", lambda e: e.memset(vO[:], 0.0), writes=["vO"])
            for ti in range(20):
                A("pool", lambda e, ti=ti: e.tensor_copy(out=vO[:, ti, :, 64:128], in_=vS[:, ti, :, 0:64]), reads=[("vS", ti), "vO"], writes=[("vO", ti)])
            onesE = sa.t([128, 2, 128], BF16)
            A("pool", lambda e: e.memset(onesE[:], 0.0), writes=["onesE0"])
            A("pool", lambda e: e.memset(onesE[:, 0, 0:64], 1.0), reads=["onesE0"], writes=["onesE1"])
            A("pool", lambda e: e.memset(onesE[:, 1, 64:128], 1.0), reads=["onesE0"], writes=["onesE2"])
            onesEk = ["onesE1", "onesE2"]
            sbk = [0]
            for qt in range(ntile_full):
                ei = qt % 2
                if qt < 16:
                    kl = [(LH0 // 128 if qt == 0 else qt - 1, 2 if qt == 0 else 0), (qt, None),
                          (RH0 // 128 if qt == 15 else qt + 1, 3 if qt == 15 else 1), (16, None), (17, None)]
                else:
                    kl = [(16, None), (17, None)]
                q0 = qt * 128
                for g in range(2):
                    for hp in range(2):
                        for j0 in range(0, len(kl), 2):
                            js = list(range(j0, min(j0 + 2, len(kl))))
                            sbk[0] = (sbk[0] + 1) % 3
                            b = sbk[0]

                            def f(e, b=b, js=js, g=g, hp=hp, kl=kl, q0=q0):
                                for jj, j in enumerate(js):
                                    kt = kl[j][0]
                                    ins = e.matmul(ps[b][:, jj * 256:(jj + 1) * 256], lhsT=kT[hp * 64:(hp + 1) * 64, g, kt * 128:(kt + 1) * 128],
                                                   rhs=qT[hp * 64:(hp + 1) * 64, 2 * g:2 * g + 2, q0:q0 + 128], start=True, stop=True)
                                return ins
                            A("pe", f, reads=["qkT"], writes=[("ps", b)], c=0.1 + 0.14 * len(js))
                            A("act", lambda e, b=b, js=js, g=g, hp=hp, ei=ei, j0=j0: e.activation(
                                out=E[ei][:, j0:j0 + len(js), hp, g, :, :], in_=ps[b][:, 0:256 * len(js)].rearrange("p (j c q) -> p j c q", j=len(js), c=2),
                                func=AF.Exp, scale=0.125), reads=[("ps", b)], writes=[("E", ei, j0, hp, g)])
                ek = [("E", ei, j0, hp, g) for j0 in (0, 2, 4) for hp in range(2) for g in range(2)]
                for j, (kt, mi) in enumerate(kl):
                    if mi is not None:
                        A("pool", lambda e, ei=ei, j=j, mi=mi: e.tensor_tensor(out=E[ei][:, j].rearrange("p a b c q -> p (a b c) q"),
                                                                              in0=E[ei][:, j].rearrange("p a b c q -> p (a b c) q"),
                                                                              in1=masks[:, mi:mi + 1, :].to_broadcast([128, 8, 128]), op=ALU.mult),
                          reads=ek + ["masks"], writes=[("Em", ei, j)], c=1.6)
                emk = [("Em", ei, j) for j in range(5)]
                bn_ = 3
                bd_ = 4

                def f(e, ei=ei, kl=kl, bn_=bn_, bd_=bd_):
                    cnt = len(kl) * 2
                    for g in range(2):
                        i = 0
                        for j, (kt, mi) in enumerate(kl):
                            for hp in range(2):
                                vv = vS if hp == 0 else vO
                                e.matmul(ps[bn_][:, g * 256:(g + 1) * 256], lhsT=vv[:, kt, g, :], rhs=E[ei][:, j, hp, g, :, :],
                                         start=(i == 0), stop=(i == cnt - 1))
                                i += 1
                    for g in range(2):
                        i = 0
                        for j, (kt, mi) in enumerate(kl):
                            for hp in range(2):
                                ins = e.matmul(ps[bd_][:, g * 256:(g + 1) * 256], lhsT=onesE[:, hp, :], rhs=E[ei][:, j, hp, g, :, :],
                                               start=(i == 0), stop=(i == cnt - 1))
                                i += 1
                    return ins
                A("pe", f, reads=ek + emk + [("vS", kt) for kt, _ in kl] + [("vO", kt) for kt, _ in kl] + onesEk, writes=[("ps", bn_), ("ps", bd_)], c=0.1 + 0.14 * 8 * len(kl))
                A("dve", lambda e, bd_=bd_: e.tensor_tensor(out=den[:], in0=ps[bd_][:], in1=sinkbc[:].rearrange("p c q -> p (c q)"), op=ALU.add),
                  reads=[("ps", bd_), "sinkbc"], writes=["den"])
                A("dve", lambda e: e.reciprocal(out=den[:], in_=den[:]), reads=["den"], writes=["den"])
                A("dve", lambda e, bn_=bn_, q0=q0: e.tensor_tensor(out=Y[:, 0:4, q0:q0 + 128], in0=ps[bn_][:].rearrange("p (c q) -> p c q", c=4),
                                                                in1=den[:].rearrange("p (c q) -> p c q", c=4), op=ALU.mult),
                  reads=[("ps", bn_), "den"], writes=[("Y", c_) for c_ in range(4)])
            gate_branch(0, C_GATT)

            chk("att")
            barrier()
            sa = Alloc(nc, SCR, LIM)
            gt = [sa.t([128, 512], F32) for _ in range(2)]
            a2 = [sa.t([128, 2048], F32) for _ in range(2)]
            u2 = [sa.t([128, 2048], F32) for _ in range(2)]
            h2 = [sa.t([128, 2048], F32) for _ in range(2)]
            for cc in range(4):
                for d in range(2):
                    cmb = d * 4 + cc
                    A("sp", lambda e, cmb=cmb, d=d: e.dma_start(out=a2[d][:], in_=au[(cmb * 2) * 128:(cmb * 2 + 1) * 128, :]), reads=[("au", cmb, 0)], writes=[("a2", d)], dma=True)
                    A("sp", lambda e, cmb=cmb, d=d: e.dma_start(out=u2[d][:], in_=au[(cmb * 2 + 1) * 128:(cmb * 2 + 2) * 128, :]), reads=[("au", cmb, 1)], writes=[("u2", d)], dma=True)
                A("dve", lambda e, cc=cc: e.tensor_tensor_scan(out=h2[0][:], data0=a2[0][:], data1=u2[0][:], initial=carry[:, cc:cc + 1], op0=ALU.mult, op1=ALU.add),
                  reads=[("a2", 0), ("u2", 0), ("carry", 0)], writes=[("h2", 0)])
                A("dve", lambda e, cc=cc: e.tensor_tensor_scan(out=h2[1][:, ::-1], data0=a2[1][:, ::-1], data1=u2[1][:, ::-1], initial=carry[:, 4 + cc:5 + cc],
                                                             op0=ALU.mult, op1=ALU.add),
                  reads=[("a2", 1), ("u2", 1), ("carry", 1)], writes=[("h2", 1)])
                A("pool", lambda e, cc=cc: e.tensor_tensor(out=Y[:, 8 + cc, 0:2048], in0=h2[0][:], in1=h2[1][:], op=ALU.add), reads=[("h2", 0), ("h2", 1)], writes=[("Y", 8 + cc)])
                if not last:
                    A("pool", lambda e, cc=cc: e.tensor_copy(out=Y[:, 8 + cc, 2048:2304], in_=ylru_ctx[:, cc, :]), reads=[("ylc", cc)], writes=[("Y", 8 + cc)])
            gate_branch(2, C_GLRU)

            chk("lru2")
            barrier()
            sa = Alloc(nc, SCR, LIM)
            mT = sa.t([128, KC, NFULL], BF16)
            wout = sa.t([128, KC, D], BF16)
            mrg_off = sa.off
            wg = [sa.t([128, KC, 3, 128], BF16) for _ in range(2)]
            wbr = [sa.t([128, 3, 4, 128], BF16) for _ in range(2)]
            Gt = [sa.t([128, 512], F32) for _ in range(2)]
            acc = sa.t([128, 512], F32)
            tmp = sa.t([128, 512], F32)
            for kc in range(KC):
                A("pool", lambda e, kc=kc: e.dma_start(out=wout[:, kc, :], in_=w_out[l][kc * 128:(kc + 1) * 128, :]), writes=[("wout", kc)], dma=True)
            for dc in range(8):
                wi = dc % 2
                for b_ in range(3):
                    A("pool", lambda e, wi=wi, dc=dc, b_=b_: e.dma_start(out=wg[wi][:, :, b_, :],
                                                                     in_=w_gate[l][:, b_ * D + dc * 128:b_ * D + (dc + 1) * 128].rearrange("(kc p) c -> p kc c", p=128)),
                      writes=[("wg", wi, b_)], dma=True)
                    A("pool", lambda e, wi=wi, dc=dc, b_=b_: e.dma_start(out=wbr[wi][:, b_, :, :],
                                                                     in_=w_branch[l, b_][:, dc * 128:(dc + 1) * 128].rearrange("(kc p) c -> p kc c", p=128)),
                      writes=[("wbr", wi, b_)], dma=True)
                for (n0, n1) in full_r:
                    n = n1 - n0
                    for b_ in range(3):
                        bg = nb()

                        def f(e, bg=bg, wi=wi, b_=b_, n0=n0, n1=n1):
                            for kc in range(KC):
                                ins = e.matmul(ps[bg][:, 0:n1 - n0], lhsT=wg[wi][:, kc, b_, :], rhs=hT[:, kc, n0:n1], start=(kc == 0), stop=(kc == KC - 1))
                            return ins
                        A("pe", f, reads=[("wg", wi, b_), "hT"], writes=[("ps", bg)], c=0.1 + 8 * 0.27 * n / 512.0)
                        gi_ = bg % 2
                        A("act", lambda e, bg=bg, gi_=gi_, n=n, b_=b_, dc=dc: e.activation(out=Gt[gi_][:, 0:n], in_=ps[bg][:, 0:n], func=AF.Sigmoid,
                                                                                     bias=pcols[:, PB + 72 + b_ * 8 + dc:PB + 73 + b_ * 8 + dc], scale=1.0),
                          reads=[("ps", bg), "pcols"], writes=[("Gt", gi_)])
                        bp = nb()

                        def f(e, bp=bp, wi=wi, b_=b_, n0=n0, n1=n1):
                            for kc in range(4):
                                ins = e.matmul(ps[bp][:, 0:n1 - n0], lhsT=wbr[wi][:, b_, kc, :], rhs=Y[:, b_ * 4 + kc, n0:n1], start=(kc == 0), stop=(kc == 3))
                            return ins
                        A("pe", f, reads=[("wbr", wi, b_)] + [("Y", b_ * 4 + kc) for kc in range(4)], writes=[("ps", bp)], c=0.1 + 4 * 0.27 * n / 512.0)
                        if b_ == 0:
                            A("dve", lambda e, bp=bp, gi_=gi_, n=n: e.tensor_tensor(out=acc[:, 0:n], in0=ps[bp][:, 0:n], in1=Gt[gi_][:, 0:n], op=ALU.mult),
                              reads=[("ps", bp), ("Gt", gi_)], writes=["acc"])
                        else:
                            A("dve", lambda e, bp=bp, gi_=gi_, n=n: e.tensor_tensor(out=tmp[:, 0:n], in0=ps[bp][:, 0:n], in1=Gt[gi_][:, 0:n], op=ALU.mult),
                              reads=[("ps", bp), ("Gt", gi_)], writes=["tmp"])
                            if b_ == 1:
                                A("pool", lambda e, n=n: e.tensor_tensor(out=acc[:, 0:n], in0=acc[:, 0:n], in1=tmp[:, 0:n], op=ALU.add), reads=["acc", "tmp"], writes=["acc"])
                            else:
                                A("pool", lambda e, n=n, dc=dc, n0=n0, n1=n1: e.tensor_tensor(out=mT[:, dc, n0:n1], in0=acc[:, 0:n], in1=tmp[:, 0:n], op=ALU.add),
                                  reads=["acc", "tmp"], writes=[("mT", dc)])

            chk("merge")
            barrier()
            sa = Alloc(nc, mrg_off, LIM)
            gbc = [sa.t([128, D], F32) for _ in range(1 if last else 2)]
            lng = sa.t([128, D], F32); lnb = sa.t([128, D], F32)
            A("sp", lambda e: e.dma_start(out=lng[:], in_=ln_g[l:l + 1, :].partition_broadcast(128)), writes=["lng"], dma=True)
            A("sp", lambda e: e.dma_start(out=lnb[:], in_=ln_b[l:l + 1, :].partition_broadcast(128)), writes=["lnb"], dma=True)
            for r in range(len(gbc)):
                for h_ in range(2):
                    b = nb()
                    A("pe", lambda e, b=b, r=r, h_=h_: e.matmul(ps[b][:, 0:512], lhsT=sel[0:2, r * 128:(r + 1) * 128], rhs=grow[0:2, h_ * 512:(h_ + 1) * 512], start=True, stop=True),
                      reads=["sel", "grow"], writes=[("ps", b)])
                    fence([("ps", b)])
                    A("act", lambda e, b=b, r=r, h_=h_: e.activation(out=gbc[r][:, h_ * 512:(h_ + 1) * 512], in_=ps[b][:, 0:512], func=AF.Copy), reads=[("ps", b)], writes=[("gbc", r, h_)])
            mk = [("mT", dc) for dc in range(8)]
            wk_ = [("wout", kc) for kc in range(8)]
            wkc_ = [("woutc", kc) for kc in range(8)]
            if not last:
                wout_c = sa.t([128, KC, D], BF16)
                for dc in range(8):
                    A("pool", lambda e, dc=dc: e.tensor_tensor(out=wout_c[:, dc, :], in0=wout[:, dc, :], in1=gbc[1][:], op=ALU.mult),
                      reads=[("wout", dc), ("gbc", 1, 0), ("gbc", 1, 1)], writes=[("woutc", dc)], c=2.5)
            for dc in range(8):
                A("dve", lambda e, dc=dc: e.tensor_tensor(out=wout[:, dc, :], in0=wout[:, dc, :], in1=gbc[0][:], op=ALU.mult),
                  reads=[("wout", dc), ("woutc", dc), ("gbc", 0, 0), ("gbc", 0, 1)], writes=[("wout", dc)], c=1.1)
            ya = Alloc(nc, Y_OFF, Y_END)
            G4 = 3
            xr = [ya.t([128, D], F32) for _ in range(2 * G4)]
            zt = [ya.t([128, D], F32) for _ in range(2 * G4)]
            st2 = [sa.t([128, G4, 2, 6], F32) for _ in range(2)]
            mv2 = [sa.t([128, G4, 2], F32) for _ in range(2)]
            rs2 = [sa.t([128, G4], F32) for _ in range(2)]
            nb2 = [sa.t([128, G4], F32) for _ in range(2)]
            all_tiles = list(range(ntile_full))
            groups = [all_tiles[i:i + G4] for i in range(0, ntile_full, G4)]
            final = last or l == nlayers - 1
            for gi, grp in enumerate(groups):
                s_ = gi % 2
                ng = len(grp)
                for j, ti in enumerate(grp):
                    bi = s_ * G4 + j
                    r = 1 if ti >= 16 else 0
                    if l == 0:
                        src = x_in[ti * 128:(ti + 1) * 128, :] if ti < 16 else ctx_in[(ti - 16) * 128:(ti - 15) * 128, :]
                    else:
                        src = xs[ti * 128:(ti + 1) * 128, :]
                    A("sp", lambda e, bi=bi, src=src: e.dma_start(out=xr[bi][:], in_=src), reads=[("xs", ti)], writes=[("xr", bi)], dma=True)
                    for h_ in range(2):
                        b = nb()

                        wsel = wout_c if r == 1 else wout

                        def f(e, b=b, ti=ti, h_=h_, wsel=wsel):
                            for dc in range(8):
                                ins = e.matmul(ps[b][:, 0:512], lhsT=mT[:, dc, ti * 128:(ti + 1) * 128], rhs=wsel[:, dc, h_ * 512:(h_ + 1) * 512], start=(dc == 0), stop=(dc == 7))
                            return ins
                        A("pe", f, reads=mk + (wkc_ if r == 1 else wk_), writes=[("ps", b)], c=2.3)
                        A("dve", lambda e, b=b, bi=bi, h_=h_: e.tensor_tensor(out=zt[bi][:, h_ * 512:(h_ + 1) * 512], in0=ps[b][:, 0:512], in1=xr[bi][:, h_ * 512:(h_ + 1) * 512], op=ALU.add),
                          reads=[("ps", b), ("xr", bi)], writes=[("zt", bi, h_)])
                for j in range(ng):
                    bi = s_ * G4 + j
                    for h_ in range(2):
                        A("dve", lambda e, bi=bi, j=j, h_=h_, s_=s_: e.bn_stats(out=st2[s_][:, j, h_, :], in_=zt[bi][:, h_ * 512:(h_ + 1) * 512]),
                          reads=[("zt", bi, h_)], writes=[("st2", s_, j, h_)])
                for j in range(ng):
                    A("dve", lambda e, j=j, s_=s_: e.bn_aggr(out=mv2[s_][:, j, :], in_=st2[s_][:, j, :, :]),
                      reads=[("st2", s_, j, 0), ("st2", s_, j, 1)], writes=[("mv2", s_, j)])
                mvk = [("mv2", s_, j) for j in range(ng)]
                A("dve", lambda e, s_=s_, ng=ng: e.tensor_scalar(out=rs2[s_][:, 0:ng], in0=mv2[s_][:, 0:ng, 1], scalar1=EPS / (ALPHA * ALPHA), scalar2=None, op0=ALU.add),
                  reads=mvk, writes=[("rs2", s_)])
                A("act", lambda e, s_=s_, ng=ng: e.activation(out=rs2[s_][:, 0:ng], in_=rs2[s_][:, 0:ng], func=AF.Sqrt), reads=[("rs2", s_)], writes=[("rs2", s_)])
                A("dve", lambda e, s_=s_, ng=ng: e.reciprocal(out=rs2[s_][:, 0:ng], in_=rs2[s_][:, 0:ng]), reads=[("rs2", s_)], writes=[("rs2", s_)])
                A("dve", lambda e, s_=s_, ng=ng: e.scalar_tensor_tensor(out=nb2[s_][:, 0:ng], in0=mv2[s_][:, 0:ng, 0], scalar=-1.0, in1=rs2[s_][:, 0:ng], op0=ALU.mult, op1=ALU.mult),
                  reads=mvk + [("rs2", s_)], writes=[("nb2", s_)], c=0.2)
                for j in range(ng):
                    bi = s_ * G4 + j
                    zk = [("zt", bi, 0), ("zt", bi, 1)]
                    A("act", lambda e, bi=bi, j=j, s_=s_: e.activation(out=zt[bi][:], in_=zt[bi][:], func=AF.Identity, scale=rs2[s_][:, j:j + 1], bias=nb2[s_][:, j:j + 1]),
                      reads=zk + [("rs2", s_), ("nb2", s_)], writes=zk, c=1.0)
                for j in range(ng):
                    bi = s_ * G4 + j
                    zk = [("zt", bi, 0), ("zt", bi, 1)]
                    A("dve", lambda e, bi=bi: e.tensor_tensor(out=zt[bi][:], in0=zt[bi][:], in1=lng[:], op=ALU.mult), reads=zk + ["lng"], writes=zk, c=1.1)
                for j in range(ng):
                    bi = s_ * G4 + j
                    zk = [("zt", bi, 0), ("zt", bi, 1)]
                    A("pool", lambda e, bi=bi: e.tensor_tensor(out=zt[bi][:], in0=zt[bi][:], in1=lnb[:], op=ALU.add), reads=zk + ["lnb"], writes=zk, c=2.5)
                for j, ti in enumerate(grp):
                    bi = s_ * G4 + j
                    zk = [("zt", bi, 0), ("zt", bi, 1)]
                    if final:
                        if ti < 16:
                            A("sp", lambda e, bi=bi, ti=ti: e.dma_start(out=out[ti * 128:(ti + 1) * 128, :], in_=zt[bi][:]), reads=zk, writes=[("out", ti)], dma=True)
                    else:
                        A("sp", lambda e, bi=bi, ti=ti: e.dma_start(out=xs[ti * 128:(ti + 1) * 128, :], in_=zt[bi][:]), reads=zk, writes=[("xs", ti)], dma=True)
                        if ti == 0:
                            A("sp", lambda e, bi=bi: e.dma_start(out=send[0:128, :], in_=zt[bi][:]), reads=zk, writes=[("send", 0)], dma=True)
                        if ti == 15:
                            A("sp", lambda e, bi=bi: e.dma_start(out=send[128:256, :], in_=zt[bi][:]), reads=zk, writes=[("send", 1)], dma=True)
            if not (last or l == nlayers - 1):
                A("pool", lambda e: e.collective_compute("AllGather", ALU.bypass, replica_groups=RG, ins=[send.opt()], outs=[gath.opt()]),
                  reads=[("send", 0), ("send", 1)], writes=["gath"], dma="cc", ne=True)
        try:
            for l_ in range(nlayers):
                layer(l_)
        except _Stop:
            pass
        if dbg:
            dbg_h = nc.dram_tensor("dbg_h", [128, KC * NT], BF16, kind="ExternalOutput").ap()
            dbg_y = nc.dram_tensor("dbg_y", [128, 12 * NFULL], BF16, kind="ExternalOutput").ap()
            A("sp", lambda e: e.dma_start(out=dbg_h[:, :], in_=hT[:].rearrange("p a b -> p (a b)")), reads=["hT"], writes=["dbg_h"], dma=True)
            A("sp", lambda e: e.dma_start(out=dbg_y[:, :], in_=Y[:].rearrange("p a b -> p (a b)")), reads=[("Y", i) for i in range(12)], writes=["dbg_y"], dma=True)
            A("sp", None, reads=["dbg_h", "dbg_y"])
        A("sp", None, reads=[("out", ti) for ti in range(16)])
        S.run()
    return nc


def _rope_tables(pos):
    quarter = 16
    inv = (10000.0 ** (-np.arange(quarter, dtype=np.float32) / quarter)).astype(np.float32)
    row = (pos // 64).astype(np.float32)
    col = (pos % 64).astype(np.float32)
    ang_r = row[:, None] * inv[None, :]
    ang_c = col[:, None] * inv[None, :]
    ang = np.concatenate([ang_r, ang_r, ang_c, ang_c], -1).astype(np.float32)
    return np.cos(ang).astype(np.float32), np.sin(ang).astype(np.float32)


def make_in_maps(inputs):
    x = np.ascontiguousarray(inputs["x"], dtype=np.float32)
    B, Sq, _ = x.shape
    bf = ml_dtypes.bfloat16
    perm = np.zeros((128, 128), np.float32)
    for hd in range(2):
        o = hd * 64
        for m in range(64):
            seg, i = divmod(m, 16)
            if seg == 0:
                perm[o + m + 16, o + m] = -1.0
            elif seg == 1:
                perm[o + m - 16, o + m] = 1.0
            elif seg == 2:
                perm[o + m + 16, o + m] = -1.0
            else:
                perm[o + m - 16, o + m] = 1.0
    jj = np.arange(128)[:, None]
    qq = np.arange(128)[None, :]
    maskP = (jj >= qq).astype(np.float32)
    maskN = (jj <= qq).astype(np.float32)
    sel = np.zeros((2, 256), np.float32)
    sel[0, 0:128] = 1.0
    sel[1, 128:256] = 1.0
    wnames = ["w_ada", "b_ada", "w_in", "attn_sink", "conv_dw_w", "conv_dw_b", "conv_ln_g", "conv_ln_b", "lru_conv_w", "lru_conv_b",
              "lru_w_a", "lru_b_a", "lru_w_x", "lru_b_x", "lru_lambda", "w_branch", "w_gate", "b_gate", "w_out", "ln_g", "ln_b"]
    shared = {k: np.ascontiguousarray(inputs[k], dtype=np.float32) for k in wnames}
    shared["perm"] = perm.astype(bf)
    shared["identb"] = np.eye(128, dtype=np.float32).astype(bf)
    shared["identf"] = np.eye(128, dtype=np.float32)
    shared["sel"] = sel
    maps = []
    for c in range(8):
        b, r = divmod(c, 4)
        t0 = r * T_OWN
        m = dict(shared)
        m["x_in"] = x[b, t0:t0 + T_OWN]
        xh = np.zeros((256, D), np.float32)
        if r > 0:
            xh[0:128] = x[b, t0 - 128:t0]
        if r < 3:
            xh[128:256] = x[b, t0 + T_OWN:t0 + T_OWN + 128]
        m["xh_in"] = xh
        m["ctx_in"] = np.ascontiguousarray(inputs["ctx"][b], dtype=np.float32)
        m["c_in"] = np.stack([inputs["c"][b], inputs["c_ctx"]]).astype(np.float32)
        pos = np.zeros(NT, np.int64)
        pos[0:T_OWN] = t0 + np.arange(T_OWN)
        pos[LH0:LH0 + 128] = np.clip(t0 - 128 + np.arange(128), 0, Sq - 1)
        pos[RH0:RH0 + 128] = np.clip(t0 + T_OWN + np.arange(128), 0, Sq - 1)
        cos, sin = _rope_tables(pos)
        cos[CTX0:CTX0 + 256] = 1.0
        sin[CTX0:CTX0 + 256] = 0.0
        m["cosT"] = np.ascontiguousarray(np.concatenate([cos.T, cos.T], 0))
        m["sinT"] = np.ascontiguousarray(np.concatenate([sin.T, sin.T], 0))
        hl = 1.0 if r > 0 else 0.0
        hr = 1.0 if r < 3 else 0.0
        m["masks"] = np.concatenate([maskP, maskN, maskP * hl, maskN * hr], 1).astype(bf)
        fl = np.zeros((128, 8), np.float32)
        fl[:, 0] = hl
        fl[:, 1] = hr
        fl[:, 2 + r] = 1.0
        m["flags"] = fl
        maps.append(m)
    return maps


_NC_CACHE = {}


def kernel(**inputs):
    if "nc" not in _NC_CACHE:
        _NC_CACHE["nc"] = build_nc()
    nc = _NC_CACHE["nc"]
    maps = make_in_maps(inputs)
    res = run_bass_kernel_spmd(nc, maps, core_ids=list(range(8)))
    x = inputs["x"]
    outp = np.zeros(x.shape, np.float32)
    for c in range(8):
        b, r = divmod(c, 4)
        outp[b, r * T_OWN:(r + 1) * T_OWN] = res.results[c]["out"]
    return outp
```

```python
import numpy as np
import ml_dtypes
from contextlib import ExitStack
import concourse.bass as bass
import concourse.mybir as mybir
from concourse.bass_utils import run_bass_kernel_spmd

F32 = mybir.dt.float32
BF16 = mybir.dt.bfloat16
ALU = mybir.AluOpType
AF = mybir.ActivationFunctionType

DEPTH = 4
D = 1024
KC = 8
T_OWN = 2048
NT = 2560
NFULL = 2304
CTX0 = 2048
LH0 = 2304
RH0 = 2432
ALPHA = (2 * DEPTH) ** 0.25
EPS = 1e-6
C_Q, C_K, C_V, C_GATT, C_CA, C_CG, C_GCONV, C_XLRU, C_GLRU = 0, 512, 640, 768, 1280, 1792, 2304, 2816, 3328
FULL_R = [(0, 512), (512, 1024), (1024, 1536), (1536, 2048), (2048, 2304)]

ENG_NAMES = ("pe", "act", "dve", "pool", "sp")
SAME_ENG_SYNC = {"act", "dve", "pool"}


class Op:
    __slots__ = ("eng", "fn", "deps", "sig", "tok_sem", "tok_val", "dma", "idx", "cost", "lat", "seq", "alldeps")

    DEF_COST = {"pe": 1.5, "act": 0.5, "dve": 0.6, "pool": 1.0, "sp": 0.1}

    def __init__(self, eng, fn, dma, c=None):
        self.eng = eng
        self.fn = fn
        self.dma = dma
        if dma == "cc":
            self.cost, self.lat = 1.0, 100.0
        elif dma:
            self.cost = 0.1 if eng != "pool" else 0.8
            self.lat = self.cost + (c if c is not None else 3.0)
        else:
            self.cost = c if c is not None else self.DEF_COST[eng]
            self.lat = self.cost + 0.15
        if fn is None:
            self.cost = self.lat = 0.0
        self.deps = set()
        self.sig = False
        self.tok_sem = None
        self.tok_val = 0
        self.idx = -1


class Sched:
    NDMA = {"sp": 8, "pool": 6, "act": 4}

    def __init__(self, nc, stack):
        self.nc = nc
        self.ops = {e: [] for e in ENG_NAMES}
        self.all_ops = []
        self.reorder = True
        self.last_w = {}
        self.readers = {}
        self.esem = {}
        for e in ("pe", "act", "dve", "pool"):
            self.esem[e] = stack.enter_context(nc.semaphore("s_" + e))
        self.dsem = {}
        for e, n in self.NDMA.items():
            self.dsem[e] = [stack.enter_context(nc.semaphore("d_%s%d" % (e, i))) for i in range(n)]
        self.ccsem = stack.enter_context(nc.semaphore("s_cc"))
        self.ccval = 0
        self.dcount = {e: 0 for e in self.NDMA}
        self.dlast = {e: [None] * n for e, n in self.NDMA.items()}
        self.dval = {e: [0] * n for e, n in self.NDMA.items()}

    def add(self, eng, fn, reads=(), writes=(), dma=False, c=None, ne=False):
        if "EPOCH" not in writes and not ne:
            reads = list(reads) + ["EPOCH"]
        if dma == "cc":
            reads = list(reads) + ["CCORDER"]
            writes = list(writes) + ["CCORDER"]
        op = Op(eng, fn, dma, c)
        op.seq = len(self.all_ops)
        self.all_ops.append(op)
        deps = op.deps
        for k in reads:
            w = self.last_w.get(k)
            if w is not None:
                deps.add(w)
        for k in writes:
            w = self.last_w.get(k)
            if w is not None:
                deps.add(w)
            for r in self.readers.get(k, ()):
                deps.add(r)
        for k in reads:
            self.readers.setdefault(k, []).append(op)
        for k in writes:
            self.last_w[k] = op
            self.readers[k] = []
        if dma == "cc":
            self.ccval += 1
            op.tok_sem = self.ccsem
            op.tok_val = self.ccval
            op.sig = True
        elif dma:
            n = self.dcount[eng]
            self.dcount[eng] = n + 1
            slot = n % self.NDMA[eng]
            prev = self.dlast[eng][slot]
            if prev is not None:
                deps.add(prev)
            self.dlast[eng][slot] = op
            self.dval[eng][slot] += 16
            op.tok_sem = self.dsem[eng][slot]
            op.tok_val = self.dval[eng][slot]
            op.sig = True
        deps.discard(op)
        op.idx = len(self.ops[eng])
        self.ops[eng].append(op)
        return op

    def list_schedule(self):
        import heapq
        ops = self.all_ops
        ndep = [0] * len(ops)
        users = [[] for _ in ops]
        for op in ops:
            ndep[op.seq] = len(op.deps)
            for d in op.deps:
                users[d.seq].append(op)
        ready_t = [0.0] * len(ops)
        waiting = {e: [] for e in ENG_NAMES}
        avail = {e: [] for e in ENG_NAMES}
        free = {e: 0.0 for e in ENG_NAMES}
        for op in ops:
            if ndep[op.seq] == 0:
                heapq.heappush(waiting[op.eng], (0.0, op.seq))
        newq = {e: [] for e in ENG_NAMES}
        left = len(ops)
        while left:
            best = None
            for e in ENG_NAMES:
                w, a = waiting[e], avail[e]
                while w and w[0][0] <= free[e]:
                    heapq.heappush(a, heapq.heappop(w)[1])
                if a:
                    cand = (free[e], a[0], e, True)
                elif w:
                    cand = (w[0][0], w[0][1], e, False)
                else:
                    continue
                if best is None or cand[:2] < best[:2]:
                    best = cand
            t, sq, e, from_avail = best
            if from_avail:
                heapq.heappop(avail[e])
            else:
                heapq.heappop(waiting[e])
            op = ops[sq]
            newq[e].append(op)
            free[e] = t + op.cost
            done = t + op.lat
            left -= 1
            for u in users[sq]:
                if done > ready_t[u.seq]:
                    ready_t[u.seq] = done
                ndep[u.seq] -= 1
                if ndep[u.seq] == 0:
                    heapq.heappush(waiting[u.eng], (ready_t[u.seq], u.seq))
        for e in ENG_NAMES:
            assert len(newq[e]) == len(self.ops[e])
            self.ops[e] = newq[e]
            for i, op in enumerate(newq[e]):
                op.idx = i
        self.est_total = max(free.values())

    def finalize(self):
        if self.reorder:
            self.list_schedule()
        for e in ENG_NAMES:
            for op in self.ops[e]:
                best = {}
                keep = set()
                for d in op.deps:
                    if d.dma:
                        keep.add(d)
                    else:
                        if d.eng == op.eng and not op.dma and d.eng not in SAME_ENG_SYNC:
                            continue
                        b = best.get(d.eng)
                        if b is None or d.idx > b.idx:
                            best[d.eng] = d
                keep.update(best.values())
                op.deps = keep
                for d in keep:
                    d.sig = True
        for e in ("pe", "act", "dve", "pool"):
            c = 0
            for op in self.ops[e]:
                if op.dma:
                    continue
                if op.sig:
                    c += 1
                    op.tok_sem = self.esem[e]
                    op.tok_val = c

    def replay(self, e, eng):
        waited = {}
        for op in self.ops[e]:
            for d in sorted(op.deps, key=lambda d: (d.eng, d.idx)):
                key = id(d.tok_sem)
                if waited.get(key, 0) < d.tok_val:
                    eng.wait_ge(d.tok_sem, d.tok_val)
                    waited[key] = d.tok_val
            if op.fn is None:
                continue
            ins = op.fn(eng)
            if op.sig:
                if op.dma == "cc":
                    ins.then_inc(op.tok_sem)
                else:
                    ins.then_inc(op.tok_sem, 16 if op.dma else 1)

    def run(self):
        self.finalize()
        with self.nc.Block() as block:
            @block.tensor
            def _(eng):
                self.replay("pe", eng)

            @block.scalar
            def _(eng):
                self.replay("act", eng)

            @block.vector
            def _(eng):
                self.replay("dve", eng)

            @block.gpsimd
            def _(eng):
                self.replay("pool", eng)

            @block.sync
            def _(eng):
                self.replay("sp", eng)


class Alloc:
    def __init__(self, nc, base, limit):
        self.nc = nc
        self.off = base
        self.limit = limit
        self.n = 0

    def t(self, shape, dtype):
        esz = 2 if dtype == BF16 else 4
        per = esz
        for s in shape[1:]:
            per *= s
        per = (per + 63) // 64 * 64
        h = self.nc.alloc_sbuf_tensor_at("t%d_%d" % (self.n, self.off), list(shape), dtype, offset=self.off)
        self.n += 1
        self.off += per
        assert self.off <= self.limit, ("sbuf overflow", self.off, self.limit)
        return h


class _Stop(Exception):
    pass


def build_nc(nlayers=DEPTH, dbg=False, stop=None):
    def chk(name):
        if stop == name:
            raise _Stop()
    nc = bass.Bass("TRN2", target_bir_lowering=False)
    inp = lambda n, s, d=F32: nc.dram_tensor(n, list(s), d, kind="ExternalInput").ap()
    x_in = inp("x_in", [T_OWN, D])
    xh_in = inp("xh_in", [256, D])
    ctx_in = inp("ctx_in", [256, D])
    c_in = inp("c_in", [2, D])
    w_ada = inp("w_ada", [DEPTH, D, 3 * D]); b_ada = inp("b_ada", [DEPTH, 3 * D])
    w_in = inp("w_in", [DEPTH, D, 3840]); attn_sink = inp("attn_sink", [DEPTH, 8])
    conv_dw_w = inp("conv_dw_w", [DEPTH, 31, 512]); conv_dw_b = inp("conv_dw_b", [DEPTH, 512])
    conv_ln_g = inp("conv_ln_g", [DEPTH, 512]); conv_ln_b = inp("conv_ln_b", [DEPTH, 512])
    lru_conv_w = inp("lru_conv_w", [DEPTH, 2, 4, 512]); lru_conv_b = inp("lru_conv_b", [DEPTH, 2, 512])
    lru_w_a = inp("lru_w_a", [DEPTH, 2, 8, 64, 64]); lru_b_a = inp("lru_b_a", [DEPTH, 2, 512])
    lru_w_x = inp("lru_w_x", [DEPTH, 2, 8, 64, 64]); lru_b_x = inp("lru_b_x", [DEPTH, 2, 512])
    lru_lambda = inp("lru_lambda", [DEPTH, 2, 512])
    w_branch = inp("w_branch", [DEPTH, 3, 512, D]); w_gate = inp("w_gate", [DEPTH, D, 3 * D])
    b_gate = inp("b_gate", [DEPTH, 3 * D]); w_out = inp("w_out", [DEPTH, D, D])
    ln_g = inp("ln_g", [DEPTH, D]); ln_b = inp("ln_b", [DEPTH, D])
    cos_d = inp("cosT", [128, NT]); sin_d = inp("sinT", [128, NT])
    masks_d = inp("masks", [128, 512], BF16)
    perm_d = inp("perm", [128, 128], BF16)
    identb_d = inp("identb", [128, 128], BF16)
    identf_d = inp("identf", [128, 128])
    flags_d = inp("flags", [128, 8])
    sel_d = inp("sel", [2, 256])
    out = nc.dram_tensor("out", [T_OWN, D], F32, kind="ExternalOutput").ap()

    xs = nc.dram_tensor("xs", [NFULL, D], F32).ap()
    send = nc.dram_tensor("send", [256, D], F32).ap()
    gath = nc.dram_tensor("gath", [4 * 256, D], F32).ap()
    au = nc.dram_tensor("au", [16 * 128, 2048], F32).ap()
    csend = nc.dram_tensor("csend", [128, 16], F32).ap()
    cgath = nc.dram_tensor("cgath", [4 * 128, 16], F32).ap()
    RG = [[0, 1, 2, 3], [4, 5, 6, 7]]

    with ExitStack() as st:
        S = Sched(nc, st)
        A = S.add
        BASE = 16640
        LIM = 229376
        pa = Alloc(nc, BASE, LIM)
        hT = pa.t([128, KC, NT], BF16)
        Y_OFF = pa.off
        Y = pa.t([128, 12, NFULL], BF16)
        Y_END = pa.off
        identb = pa.t([128, 128], BF16); identf = pa.t([128, 128], F32); onesf = pa.t([128, 128], F32)
        onesb = pa.t([128, 128], BF16)
        perm = pa.t([128, 128], BF16); masks = pa.t([128, 4, 128], BF16)
        flags = pa.t([128, 8], F32); sel = pa.t([2, 256], F32)
        scT = pa.t([128, KC, 2], BF16); cT = pa.t([128, KC, 2], F32)
        grow = pa.t([2, D], F32)
        modc = pa.t([128, 16, 2], F32)
        pcols = pa.t([128, 256], F32)
        cA = pa.t([128, 8], F32)
        carry = pa.t([128, 8], F32)
        s0 = pa.t([128, 8], F32)
        ylru_ctx = pa.t([128, 4, 256], F32)
        WG = 256
        wb = [pa.t([128, KC, WG], BF16) for _ in range(2)]
        wk = pa.t([128, KC, 256], BF16)
        zeros = pa.t([128, 128], F32)
        bar_t = pa.t([128, 16], F32)
        csb = pa.t([128, 16], F32)
        cg = pa.t([128, 4, 16], F32)
        chain = pa.t([128, 2, 4, 4], F32)
        SCR = pa.off
        ps = [st.enter_context(nc.psum_tensor("ps%d" % i, [128, 512], F32)) for i in range(6)]
        psTs = [st.enter_context(nc.psum_tensor("psT%d" % i, [128, 1024], BF16)) for i in range(2)]
        bank = [0]
        dyn = {}

        def nb():
            bank[0] = (bank[0] + 1) % 5
            return bank[0]

        def barrier():
            A("pool", lambda e: e.memset(bar_t[:], 0.0), writes=["EPOCH"])

        bpool = {}

        def nbp(name, banks):
            i = bpool.get(name, 0)
            bpool[name] = i + 1
            return banks[i % len(banks)]

        def fence(keys):
            A("pe", lambda e: e.matmul(ps[5][:, 0:2], lhsT=onesb[:, 0:128], rhs=onesb[:, 0:2], start=True, stop=True),
              reads=list(keys) + ["onesb"], writes=list(keys))
        wbi = [0]

        for (t, d, k) in ((identb, identb_d, "identb"), (identf, identf_d, "identf"), (perm, perm_d, "perm"),
                          (flags, flags_d, "flags")):
            A("sp", lambda e, t=t, d=d: e.dma_start(out=t[:], in_=d[:, :]), writes=[k], dma=True)
        A("sp", lambda e: e.dma_start(out=masks[:], in_=masks_d.rearrange("p (m q) -> p m q", m=4)), writes=["masks"], dma=True)
        A("sp", lambda e: e.dma_start(out=sel[:], in_=sel_d[:, :]), writes=["sel"], dma=True)
        A("pool", lambda e: e.memset(onesf[:], 1.0), writes=["onesf"])
        A("pool", lambda e: e.memset(onesb[:], 1.0), writes=["onesb"])
        A("pool", lambda e: e.memset(zeros[:], 0.0), writes=["zeros"])

        for r_ in range(2):
            def f_cT(e, r_=r_):
                with nc.allow_non_contiguous_dma(reason="tiny one-off transpose load of c"):
                    return e.dma_start(out=cT[:, :, r_], in_=c_in[r_].rearrange("(kc p) -> p kc", p=128))
            A("sp", f_cT, writes=[("cT", r_)], dma=True)
        A("act", lambda e: e.activation(out=scT[:], in_=cT[:], func=AF.Silu), reads=[("cT", 0), ("cT", 1)], writes=["scT"])

        def load_w(src, c0, w):
            i = wbi[0] % 2
            wbi[0] += 1
            buf = wb[i]
            A("pool", lambda e: e.dma_start(out=buf[:, :, 0:w], in_=src[:, c0:c0 + w].rearrange("(kc p) c -> p kc c", p=128)),
              writes=[("wb", i)], dma=True, ne=True)
            return buf, ("wb", i)

        def proj_fm(buf, bkey, mc, n0, n1, extra_reads=(), pool=None):
            b = nb() if pool is None else nbp(*pool)

            def f(e):
                for kc in range(KC):
                    ins = e.matmul(ps[b][:, 0:n1 - n0], lhsT=buf[:, kc, mc * 128:(mc + 1) * 128], rhs=hT[:, kc, n0:n1],
                                   start=(kc == 0), stop=(kc == KC - 1))
                return ins
            A("pe", f, reads=[bkey, "hT"] + list(extra_reads), writes=[("ps", b)], c=0.1 + 8 * 0.27 * (n1 - n0) / 512.0, ne=True)
            return b

        def layer(l):
            last = (l == DEPTH - 1)
            full_r = FULL_R[:4] if last else FULL_R
            ntile_full = 16 if last else 18
            barrier()
            sa = Alloc(nc, SCR, LIM)
            modrows = sa.t([2, 2 * D], F32)
            brow = sa.t([2, 2 * D], F32)
            A("sp", lambda e: e.dma_start(out=brow[:], in_=b_ada[l:l + 1, 0:2 * D].partition_broadcast(2)), writes=["brow"], dma=True)
            for g in range(2 * D // WG):
                buf, bkey = load_w(w_ada[l], g * WG, WG)
                b = nb()

                def f(e, buf=buf, b=b):
                    for kc in range(KC):
                        ins = e.matmul(ps[b][0:2, 0:WG], lhsT=scT[:, kc, :], rhs=buf[:, kc, 0:WG], start=(kc == 0), stop=(kc == KC - 1))
                    return ins
                A("pe", f, reads=[bkey, "scT"], writes=[("ps", b)])
                A("dve", lambda e, b=b, g=g: e.tensor_tensor(out=modrows[:, g * WG:(g + 1) * WG], in0=ps[b][0:2, 0:WG],
                                                               in1=brow[:, g * WG:(g + 1) * WG], op=ALU.add),
                  reads=[("ps", b), "brow"], writes=[("modrows", g)])
            mr_all = [("modrows", g) for g in range(2 * D // WG)]
            b = nb()

            def f(e, b=b):
                for j in range(16):
                    ins = e.matmul(ps[b][:, 2 * j:2 * j + 2], lhsT=modrows[0:2, j * 128:(j + 1) * 128], rhs=identf[0:2, 0:2],
                                   start=True, stop=True)
                return ins
            A("pe", f, reads=mr_all + ["identf"], writes=[("ps", b)])
            fence([("ps", b)])
            A("dve", lambda e, b=b: e.tensor_copy(out=modc[:].rearrange("p a r -> p (a r)"), in_=ps[b][:, 0:32]),
              reads=[("ps", b)], writes=["modc"])
            A("dve", lambda e: e.tensor_scalar(out=modc[:, 8:16, :], in0=modc[:, 8:16, :], scalar1=1.0, scalar2=None, op0=ALU.add),
              reads=["modc"], writes=["modc"])
            prow = sa.t([128, 2, 128], F32)
            A("pool", lambda e: e.memset(prow[:], 0.0), writes=["prow"])
            plist = [
                (0, 0, conv_dw_w[l].rearrange("k (cc p) -> (k cc) p", p=128), 124),
                (0, 124, conv_dw_b[l].rearrange("(cc p) -> cc p", p=128), 4),
                (1, 0, conv_ln_g[l].rearrange("(cc p) -> cc p", p=128), 4),
                (1, 4, conv_ln_b[l].rearrange("(cc p) -> cc p", p=128), 4),
                (1, 8, lru_conv_w[l].rearrange("d k (cc p) -> (d k cc) p", p=128), 32),
                (1, 40, lru_conv_b[l].rearrange("d (cc p) -> (d cc) p", p=128), 8),
                (1, 48, lru_b_a[l].rearrange("d (cc p) -> (d cc) p", p=128), 8),
                (1, 56, lru_b_x[l].rearrange("d (cc p) -> (d cc) p", p=128), 8),
                (1, 64, lru_lambda[l].rearrange("d (cc p) -> (d cc) p", p=128), 8),
                (1, 72, b_gate[l].rearrange("(r p) -> r p", p=128), 24),
            ]
            for (s_, r0, src, n) in plist:
                A("sp", lambda e, s_=s_, r0=r0, src=src, n=n: e.dma_start(out=prow[r0:r0 + n, s_, :], in_=src),
                  reads=["prow"], writes=[("prow", s_, r0)], dma=True)
            b = nb()

            def f(e, b=b):
                for s_ in range(2):
                    ins = e.matmul(ps[b][:, s_ * 128:(s_ + 1) * 128], lhsT=prow[:, s_, :], rhs=identf[:], start=True, stop=True)
                return ins
            A("pe", f, reads=[("prow", s_, r0) for (s_, r0, _, _) in plist] + ["identf"], writes=[("ps", b)])
            fence([("ps", b)])
            A("dve", lambda e, b=b: e.tensor_copy(out=pcols[:], in_=ps[b][:, 0:256]), reads=[("ps", b)], writes=["pcols"])
            PB = 128
            A("act", lambda e: e.activation(out=cA[:], in_=pcols[:, PB + 64:PB + 72], func=AF.Exp, scale=-1.0), reads=["pcols"], writes=["cA"])
            A("act", lambda e: e.activation(out=cA[:], in_=cA[:], func=AF.Ln, bias=1.0, scale=1.0), reads=["cA"], writes=["cA"])
            A("dve", lambda e: e.tensor_scalar(out=cA[:], in0=cA[:], scalar1=-8.0, scalar2=None, op0=ALU.mult), reads=["cA"], writes=["cA"])

            if dbg and l == 0:
                dbg_mr = nc.dram_tensor("dbg_mr", [2, 3 * D], F32, kind="ExternalOutput").ap()
                dbg_mc = nc.dram_tensor("dbg_mc", [128, 32], F32, kind="ExternalOutput").ap()
                dbg_ct = nc.dram_tensor("dbg_ct", [128, 16], F32, kind="ExternalOutput").ap()
                dbg_pc = nc.dram_tensor("dbg_pc", [128, 256], F32, kind="ExternalOutput").ap()
                A("sp", lambda e: e.dma_start(out=dbg_mr[:, :], in_=modrows[:]), reads=mr_all, writes=["dbg_mr"], dma=True)
                A("sp", lambda e: e.dma_start(out=dbg_mc[:, :], in_=modc[:].rearrange("p a r -> p (a r)")), reads=["modc"], writes=["dbg_mc"], dma=True)
                A("sp", lambda e: e.dma_start(out=dbg_ct[:, :], in_=cT[:].rearrange("p a r -> p (a r)")), reads=["scT"], writes=["dbg_ct"], dma=True)
                A("sp", lambda e: e.dma_start(out=dbg_pc[:, :], in_=pcols[:]), reads=["pcols", "cA"], writes=["dbg_pc"], dma=True)
                A("sp", None, reads=["dbg_mr", "dbg_mc", "dbg_ct", "dbg_pc"])
            chk("adaln")
            ya = Alloc(nc, Y_OFF, Y_END)
            G = 5
            xt = [ya.t([128, D], F32) for _ in range(2 * G)]
            xn = [sa.t([128, D], BF16) for _ in range(2 * G)]
            stats = [sa.t([128, G, 2, 6], F32) for _ in range(2)]
            mv = [sa.t([128, G, 2], F32) for _ in range(2)]
            rstd = [sa.t([128, G], F32) for _ in range(2)]
            nbias = [sa.t([128, G], F32) for _ in range(2)]
            pTi = [0]
            for gi in range(4):
                s_ = gi % 2
                tiles = list(range(gi * G, gi * G + G))
                for j, ti in enumerate(tiles):
                    bi = s_ * G + j
                    if ti < 16:
                        src = (x_in if l == 0 else xs)[ti * 128:(ti + 1) * 128, :]
                        rk = [("xs", ti)]
                    elif ti < 18:
                        src = (ctx_in[(ti - 16) * 128:(ti - 15) * 128, :] if l == 0 else xs[ti * 128:(ti + 1) * 128, :])
                        rk = [("xs", ti)]
                    else:
                        rk = ["gath"]
                        src = xh_in[(ti - 18) * 128:(ti - 17) * 128, :] if l == 0 else None
                    if src is not None:
                        A("sp", lambda e, bi=bi, src=src: e.dma_start(out=xt[bi][:], in_=src), reads=rk, writes=[("xt", bi)], dma=True)
                    else:
                        def f(e, bi=bi, ti=ti):
                            if "L" not in dyn:
                                pid = e.partition_id()
                                dyn["L"] = ((pid + 3) % 4) * 256 + 128
                                dyn["R"] = ((pid + 1) % 4) * 256
                            row = dyn["L"] if ti == 18 else dyn["R"]
                            return e.dma_start(out=xt[bi][:], in_=gath[bass.ds(row, 128), :])
                        A("sp", f, reads=rk, writes=[("xt", bi)], dma=True)
                for j in range(G):
                    bi = s_ * G + j
                    for h_ in range(2):
                        A("dve", lambda e, bi=bi, j=j, h_=h_, s_=s_: e.bn_stats(out=stats[s_][:, j, h_, :], in_=xt[bi][:, h_ * 512:(h_ + 1) * 512]),
                          reads=[("xt", bi)], writes=[("st", s_, j, h_)])
                for j in range(G):
                    A("dve", lambda e, j=j, s_=s_: e.bn_aggr(out=mv[s_][:, j, :], in_=stats[s_][:, j, :, :]),
                      reads=[("st", s_, j, 0), ("st", s_, j, 1)], writes=[("mv", s_, j)])
                mvk = [("mv", s_, j) for j in range(G)]
                A("dve", lambda e, s_=s_: e.tensor_scalar(out=rstd[s_][:], in0=mv[s_][:, :, 1], scalar1=EPS, scalar2=None, op0=ALU.add),
                  reads=mvk, writes=[("rstd", s_)])
                A("act", lambda e, s_=s_: e.activation(out=rstd[s_][:], in_=rstd[s_][:], func=AF.Sqrt), reads=[("rstd", s_)], writes=[("rstd", s_)])
                A("dve", lambda e, s_=s_: e.reciprocal(out=rstd[s_][:], in_=rstd[s_][:]), reads=[("rstd", s_)], writes=[("rstd", s_)])
                A("dve", lambda e, s_=s_: e.scalar_tensor_tensor(out=nbias[s_][:], in0=mv[s_][:, :, 0], scalar=-1.0, in1=rstd[s_][:], op0=ALU.mult, op1=ALU.mult),
                  reads=mvk + [("rstd", s_)], writes=[("nbias", s_)], c=0.2)
                for j in range(G):
                    bi = s_ * G + j
                    A("act", lambda e, bi=bi, j=j, s_=s_: e.activation(out=xn[bi][:], in_=xt[bi][:], func=AF.Identity, scale=rstd[s_][:, j:j + 1], bias=nbias[s_][:, j:j + 1]),
                      reads=[("xt", bi), ("nbias", s_), ("rstd", s_)], writes=[("xn", bi)], c=1.0)
                for j, ti in enumerate(tiles):
                    bi = s_ * G + j
                    pTi[0] += 1
                    pi = pTi[0] % 2
                    psT = psTs[pi]

                    def f(e, bi=bi, psT=psT):
                        for kc in range(KC):
                            ins = e.transpose(out=psT[:, kc * 128:(kc + 1) * 128], in_=xn[bi][:, kc * 128:(kc + 1) * 128], identity=identb[:])
                        return ins
                    A("pe", f, reads=[("xn", bi), "identb"], writes=[("psT", pi)], c=0.9)
                    r = 1 if 16 <= ti < 18 else 0

                    def f(e, ti=ti, r=r, psT=psT):
                        for kc in range(0, 4):
                            ins = e.activation(out=hT[:, kc, ti * 128:(ti + 1) * 128], in_=psT[:, kc * 128:(kc + 1) * 128], func=AF.Identity,
                                               scale=modc[:, 8 + kc, r:r + 1], bias=modc[:, kc, r:r + 1])
                        return ins
                    A("act", f, reads=[("psT", pi), "modc"], writes=[("hTe", pi)])

                    def f(e, ti=ti, r=r, psT=psT):
                        for kc in range(4, 8):
                            ins = e.tensor_scalar(out=hT[:, kc, ti * 128:(ti + 1) * 128], in0=psT[:, kc * 128:(kc + 1) * 128],
                                                  scalar1=modc[:, 8 + kc, r:r + 1], scalar2=modc[:, kc, r:r + 1], op0=ALU.mult, op1=ALU.add)
                        return ins
                    A("dve", f, reads=[("psT", pi), "modc", ("hTe", pi)], writes=["hT"])

            A("sp", lambda e: e.dma_start(out=grow[:], in_=b_ada[l:l + 1, 2 * D:3 * D].partition_broadcast(2)), writes=["grow"], dma=True, ne=True)
            for g in range(2 * D // WG, 3 * D // WG):
                buf, bkey = load_w(w_ada[l], g * WG, WG)
                b = nb()

                def f(e, buf=buf, b=b):
                    for kc in range(KC):
                        ins = e.matmul(ps[b][0:2, 0:WG], lhsT=scT[:, kc, :], rhs=buf[:, kc, 0:WG], start=(kc == 0), stop=(kc == KC - 1))
                    return ins
                A("pe", f, reads=[bkey, "scT"], writes=[("ps", b)], ne=True, c=1.2)
                gs_ = (g - 2 * D // WG) * WG
                A("dve", lambda e, b=b, gs_=gs_: e.tensor_tensor(out=grow[:, gs_:gs_ + WG], in0=ps[b][0:2, 0:WG], in1=grow[:, gs_:gs_ + WG], op=ALU.add),
                  reads=[("ps", b), "grow"], writes=["grow"], ne=True, c=0.3)
            A("act", lambda e: e.activation(out=grow[:], in_=grow[:], func=AF.Copy, scale=1.0 / ALPHA), reads=["grow"], writes=["grow"], ne=True, c=0.5)
            chk("ln1")
            barrier()
            sa = Alloc(nc, SCR, LIM)
            ya = Alloc(nc, Y_OFF, Y_END)
            GW = 2364
            cv = sa.t([128, 4, NFULL], F32)
            glu = [sa.t([128, GW], BF16) for _ in range(2)]
            Dg = sa.t([128, 31, 128], BF16)
            sg_t = [sa.t([128, 512], F32) for _ in range(2)]
            conv1_end = sa.off
            XLW = 2316
            NO = 2310
            xl_pads = [sa.t([128, XLW], BF16) for _ in range(2)]
            al = [ya, sa]
            DgL = [al[d].t([128, 4, 128], BF16) for d in range(2)]
            xc = [al[d].t([128, NO + 2], F32) for d in range(2)]
            xcb = [al[d].t([128, NO + 2], BF16) for d in range(2)]
            rg = [ya.t([128, NO + 2], F32)] * 2
            ig = [ya.t([128, NO + 2], F32)] * 2
            tq = [ya.t([128, NO + 2], F32)] * 2
            hs = [ya.t([128, 2048], F32)] * 2
            hc = [ya.t([128, 256], F32)] * 2
            wbd = [[al[d].t([128, 128], BF16) for _ in range(2)] for d in range(2)]
            sumr = [ya.t([128, 1], F32)] * 2
            LB = ("lru", (3, 4))
            CB = ("cv1", (0, 1, 2))
            for i in range(2):
                A("pool", lambda e, i=i: e.memset(glu[i][:], 0.0), writes=[("glu", i)], c=2.0)
            sic = [0]
            CO_R = [(0, 512, 0), (512, 1024, 512), (1024, 1536, 1024), (1536, 2048, 1536), (2078, 2334, 2048)]

            def conv1(cc):
                gi = cc % 2
                bufa, ka = load_w(w_in[l], C_CA + cc * 128, 128)
                bufg, kg = load_w(w_in[l], C_CG + cc * 128, 128)
                ranges = [(n0, n1, (15 + n0 if n0 < CTX0 else 2093), None) for (n0, n1) in FULL_R]
                ranges.append((LH0 + 113, LH0 + 128, 0, 0))
                ranges.append((RH0, RH0 + 15, 2063, 1))
                for (n0, n1, p0, fl) in ranges:
                    n = n1 - n0
                    bg = proj_fm(bufg, kg, 0, n0, n1, pool=CB)
                    sic[0] += 1
                    si = sic[0] % 2
                    A("act", lambda e, bg=bg, si=si, n=n: e.activation(out=sg_t[si][:, 0:n], in_=ps[bg][:, 0:n], func=AF.Sigmoid),
                      reads=[("ps", bg)], writes=[("sg_t", si)])
                    ba = proj_fm(bufa, ka, 0, n0, n1, pool=CB)
                    if fl is None:
                        A("dve", lambda e, ba=ba, si=si, n=n, p0=p0, gi=gi: e.tensor_tensor(out=glu[gi][:, p0:p0 + n], in0=ps[ba][:, 0:n], in1=sg_t[si][:, 0:n], op=ALU.mult),
                          reads=[("ps", ba), ("sg_t", si)], writes=[("glu", gi)])
                    else:
                        A("dve", lambda e, ba=ba, si=si, n=n, p0=p0, gi=gi, fl=fl: e.scalar_tensor_tensor(out=glu[gi][:, p0:p0 + n], in0=ps[ba][:, 0:n], scalar=flags[:, fl:fl + 1],
                                                                                                 in1=sg_t[si][:, 0:n], op0=ALU.mult, op1=ALU.mult),
                          reads=[("ps", ba), ("sg_t", si), "flags"], writes=[("glu", gi)])

                def f(e, cc=cc):
                    for k in range(31):
                        ins = e.tensor_scalar(out=Dg[:, k, :], in0=identb[:], scalar1=pcols[:, k * 4 + cc:k * 4 + cc + 1], scalar2=None, op0=ALU.mult)
                    return ins
                A("dve", f, reads=["identb", "pcols"], writes=["Dg"], c=4.0)
                for (o0, o1, t0) in CO_R:
                    b = nbp(*CB)

                    def f(e, b=b, o0=o0, o1=o1, gi=gi):
                        for k in range(31):
                            ins = e.matmul(ps[b][:, 0:o1 - o0], lhsT=Dg[:, k, :], rhs=glu[gi][:, o0 + k:o1 + k], start=(k == 0), stop=(k == 30))
                        return ins
                    A("pe", f, reads=["Dg", ("glu", gi)], writes=[("ps", b)], c=0.1 + 31 * 0.27 * (o1 - o0) / 512.0)
                    A("act", lambda e, b=b, o0=o0, o1=o1, t0=t0, cc=cc: e.activation(out=cv[:, cc, t0:t0 + o1 - o0], in_=ps[b][:, 0:o1 - o0], func=AF.Identity,
                                                                              bias=pcols[:, 124 + cc:125 + cc], scale=1.0),
                      reads=[("ps", b), "pcols"], writes=[("cv", cc)])

            for i_ in range(2):
                A("pool", lambda e, i_=i_: e.memset(xl_pads[i_][:], 0.0), writes=[("xl_pad", i_)], c=4.0)
            O_R = [(0, 512), (512, 1024), (1024, 1536), (1536, 2048), (2054, 2310)]
            for cc in range(4):
                xi = cc % 2
                xl_pad = xl_pads[xi]
                xk = ("xl_pad", xi)
                buf, bkey = load_w(w_in[l], C_XLRU + cc * 128, 128)
                for (n0, n1) in FULL_R:
                    b = proj_fm(buf, bkey, 0, n0, n1, pool=LB)
                    dst = xl_pad[:, 3 + n0:3 + n1] if n0 < CTX0 else xl_pad[:, 2057:2313]
                    A("act", lambda e, b=b, dst=dst, n=n1 - n0: e.activation(out=dst, in_=ps[b][:, 0:n], func=AF.Copy),
                      reads=[("ps", b)], writes=[xk])
                b = proj_fm(buf, bkey, 0, LH0 + 125, LH0 + 131, pool=LB)
                A("dve", lambda e, b=b, xl_pad=xl_pad: e.tensor_scalar(out=xl_pad[:, 0:3], in0=ps[b][:, 0:3], scalar1=flags[:, 0:1], scalar2=None, op0=ALU.mult),
                  reads=[("ps", b), "flags"], writes=[xk], c=0.2)
                A("dve", lambda e, b=b, xl_pad=xl_pad: e.tensor_scalar(out=xl_pad[:, 2051:2054], in0=ps[b][:, 3:6], scalar1=flags[:, 1:2], scalar2=None, op0=ALU.mult),
                  reads=[("ps", b), "flags"], writes=[xk], c=0.2)
                for d in range(2):
                    sh = 0 if d == 0 else 3
                    wcols = [pcols[:, PB + 8 + d * 16 + k * 4 + cc:PB + 9 + d * 16 + k * 4 + cc] for k in range(4)]
                    bcol = pcols[:, PB + 40 + d * 4 + cc:PB + 41 + d * 4 + cc]
                    xc_, xcb_, rg_, ig_, tq_, hs_, hc_, sumr_ = xc[d], xcb[d], rg[d], ig[d], tq[d], hs[d], hc[d], sumr[d]
                    kxc, kxcb, krg, kig, ktq, khs, khc, ksr = ("xc", d), ("xcb", d), ("rg", 0), ("ig", 0), ("tq", 0), ("hs", 0), ("hc", 0), ("sumr", 0)
                    dg_ = DgL[d]

                    def f(e, dg_=dg_, wcols=wcols):
                        for k in range(4):
                            ins = e.tensor_scalar(out=dg_[:, k, :], in0=identb[:], scalar1=wcols[k], scalar2=None, op0=ALU.mult)
                        return ins
                    A("dve", f, reads=["identb", "pcols"], writes=[("DgL", d)], c=0.6)
                    for (o0, o1) in O_R:
                        b = nbp(*LB)

                        def f(e, b=b, o0=o0, o1=o1, sh=sh, dg_=dg_, xl_pad=xl_pad):
                            for k in range(4):
                                ins = e.matmul(ps[b][:, 0:o1 - o0], lhsT=dg_[:, k, :], rhs=xl_pad[:, sh + o0 + k:sh + o1 + k], start=(k == 0), stop=(k == 3))
                            return ins
                        A("pe", f, reads=[("DgL", d), xk], writes=[("ps", b)], c=0.1 + 4 * 0.27 * (o1 - o0) / 512.0)
                        A("act", lambda e, b=b, o0=o0, o1=o1, xc_=xc_, bcol=bcol: e.activation(out=xc_[:, o0:o1], in_=ps[b][:, 0:o1 - o0], func=AF.Identity, bias=bcol, scale=1.0),
                          reads=[("ps", b), "pcols"], writes=[kxc], c=0.6)
                    A("dve", lambda e, xc_=xc_, xcb_=xcb_: e.tensor_copy(out=xcb_[:, 0:NO], in_=xc_[:, 0:NO]), reads=[kxc], writes=[kxcb], c=1.5)
                    for wi, (wsrc, dstg, kdst, bo) in enumerate(((lru_w_a, rg_, krg, 48), (lru_w_x, ig_, kig, 56))):
                        wt = wbd[d][wi]
                        A("pool", lambda e, wt=wt: e.memset(wt[:], 0.0), writes=[("wbd", d, wi), ("wbdd", d, wi, 0), ("wbdd", d, wi, 1)], c=0.3)
                        for hb in range(2):
                            A("pool", lambda e, wt=wt, hb=hb, wsrc=wsrc, d=d, cc=cc: e.dma_start(out=wt[hb * 64:(hb + 1) * 64, hb * 64:(hb + 1) * 64],
                                                                                          in_=wsrc[l, d, 2 * cc + hb, :, :]),
                              reads=[("wbd", d, wi)], writes=[("wbdd", d, wi, hb)], dma=True)
                        bias = pcols[:, PB + bo + d * 4 + cc:PB + bo + 1 + d * 4 + cc]
                        for (o0, o1) in O_R:
                            b = nbp(*LB)
                            A("pe", lambda e, b=b, wt=wt, o0=o0, o1=o1, xcb_=xcb_: e.matmul(ps[b][:, 0:o1 - o0], lhsT=wt[:], rhs=xcb_[:, o0:o1], start=True, stop=True),
                              reads=[("wbdd", d, wi, 0), ("wbdd", d, wi, 1), kxcb], writes=[("ps", b)], c=0.3)
                            A("act", lambda e, b=b, dstg=dstg, o0=o0, o1=o1, bias=bias: e.activation(out=dstg[:, o0:o1], in_=ps[b][:, 0:o1 - o0], func=AF.Sigmoid,
                                                                                                  bias=bias, scale=1.0),
                              reads=[("ps", b), "pcols"], writes=[kdst], c=0.6)
                    cAc = cA[:, d * 4 + cc:d * 4 + cc + 1]
                    A("dve", lambda e, rg_=rg_, sumr_=sumr_: e.reduce_sum(out=sumr_[:], in_=rg_[:, 0:2048], axis=mybir.AxisListType.X), reads=[krg], writes=[ksr], c=2.2)
                    A("act", lambda e, cAc=cAc, d=d, cc=cc, sumr_=sumr_: e.activation(out=csb[:, d * 8 + cc * 2:d * 8 + cc * 2 + 1], in_=sumr_[:], func=AF.Exp, scale=cAc),
                      reads=[ksr, "cA"], writes=[("csb", d, cc, 0)], c=0.2)
                    A("act", lambda e, cAc=cAc, rg_=rg_: e.activation(out=rg_[:, 0:NO], in_=rg_[:, 0:NO], func=AF.Exp, scale=cAc), reads=[krg, "cA"], writes=[krg], c=2.0)
                    A("pool", lambda e, rg_=rg_, tq_=tq_: e.tensor_tensor(out=tq_[:, 0:NO], in0=rg_[:, 0:NO], in1=rg_[:, 0:NO], op=ALU.mult), reads=[krg], writes=[ktq], c=4.5)
                    A("act", lambda e, tq_=tq_: e.activation(out=tq_[:, 0:NO], in_=tq_[:, 0:NO], func=AF.Sqrt, scale=-1.0, bias=1.0), reads=[ktq], writes=[ktq], c=2.0)
                    A("pool", lambda e, ig_=ig_, xc_=xc_: e.tensor_tensor(out=ig_[:, 0:NO], in0=ig_[:, 0:NO], in1=xc_[:, 0:NO], op=ALU.mult), reads=[kig, kxc], writes=[kig], c=4.5)
                    A("dve", lambda e, ig_=ig_, tq_=tq_: e.tensor_tensor(out=ig_[:, 0:NO], in0=ig_[:, 0:NO], in1=tq_[:, 0:NO], op=ALU.mult), reads=[kig, ktq], writes=[kig], c=2.5)
                    if d == 0:
                        A("dve", lambda e, rg_=rg_, ig_=ig_, hc_=hc_: e.tensor_tensor_scan(out=hc_[:], data0=rg_[:, 2054:2310], data1=ig_[:, 2054:2310], initial=0.0,
                                                                 op0=ALU.mult, op1=ALU.add), reads=[krg, kig], writes=[khc], c=0.7)
                        A("dve", lambda e, rg_=rg_, ig_=ig_, hs_=hs_: e.tensor_tensor_scan(out=hs_[:], data0=rg_[:, 0:2048], data1=ig_[:, 0:2048], initial=0.0,
                                                                 op0=ALU.mult, op1=ALU.add), reads=[krg, kig], writes=[khs], c=4.4)
                        A("pool", lambda e, cc=cc, hc_=hc_: e.tensor_copy(out=ylru_ctx[:, cc, :], in_=hc_[:]), reads=[khc], writes=[("ylc", cc)], c=0.6)
                        A("act", lambda e, cc=cc, hc_=hc_: e.activation(out=s0[:, cc:cc + 1], in_=hc_[:, 255:256], func=AF.Copy), reads=[khc], writes=[("s0", 0, cc)], c=0.2)
                        A("act", lambda e, cc=cc, hs_=hs_: e.activation(out=csb[:, cc * 2 + 1:cc * 2 + 2], in_=hs_[:, 2047:2048], func=AF.Copy),
                          reads=[khs], writes=[("csb", 0, cc, 1)], c=0.2)
                    else:
                        A("dve", lambda e, rg_=rg_, ig_=ig_, hc_=hc_: e.tensor_tensor_scan(out=hc_[:, ::-1], data0=rg_[:, 2309:2053:-1], data1=ig_[:, 2309:2053:-1], initial=0.0,
                                                                 op0=ALU.mult, op1=ALU.add), reads=[krg, kig], writes=[khc], c=0.7)
                        A("dve", lambda e, rg_=rg_, ig_=ig_, hs_=hs_: e.tensor_tensor_scan(out=hs_[:, ::-1], data0=rg_[:, 2047::-1], data1=ig_[:, 2047::-1], initial=0.0,
                                                                 op0=ALU.mult, op1=ALU.add), reads=[krg, kig], writes=[khs], c=4.4)
                        A("pool", lambda e, cc=cc, hc_=hc_: e.tensor_tensor(out=ylru_ctx[:, cc, :], in0=ylru_ctx[:, cc, :], in1=hc_[:], op=ALU.add),
                          reads=[khc, ("ylc", cc)], writes=[("ylc", cc)], c=0.6)
                        A("act", lambda e, cc=cc, hc_=hc_: e.activation(out=s0[:, 4 + cc:5 + cc], in_=hc_[:, 0:1], func=AF.Copy), reads=[khc], writes=[("s0", 1, cc)], c=0.2)
                        A("act", lambda e, cc=cc, hs_=hs_: e.activation(out=csb[:, 8 + cc * 2 + 1:8 + cc * 2 + 2], in_=hs_[:, 0:1], func=AF.Copy),
                          reads=[khs], writes=[("csb", 1, cc, 1)], c=0.2)
                    cmb = d * 4 + cc
                    A("sp", lambda e, cmb=cmb, rg_=rg_: e.dma_start(out=au[(cmb * 2) * 128:(cmb * 2 + 1) * 128, :], in_=rg_[:, 0:2048]), reads=[krg], writes=[("au", cmb, 0)], dma=True, c=5.0)
                    A("sp", lambda e, cmb=cmb, ig_=ig_: e.dma_start(out=au[(cmb * 2 + 1) * 128:(cmb * 2 + 2) * 128, :], in_=ig_[:, 0:2048]), reads=[kig], writes=[("au", cmb, 1)], dma=True, c=5.0)
                conv1(cc)
            csb_keys = [("csb", d, cc, j) for d in range(2) for cc in range(4) for j in range(2)]
            A("sp", lambda e: e.dma_start(out=csend[:, :], in_=csb[:]), reads=csb_keys, writes=["csend"], dma=True, ne=True)
            A("pool", lambda e: e.collective_compute("AllGather", ALU.bypass, replica_groups=RG, ins=[csend.opt()], outs=[cgath.opt()]),
              reads=["csend"], writes=["cgath"], dma="cc", ne=True)
            A("sp", lambda e: e.dma_start(out=cg[:], in_=cgath.rearrange("(r p) f -> p r f", p=128)), reads=["cgath"], writes=["cg"], dma=True, ne=True)
            cgv = cg[:].rearrange("p r (d c j) -> p r d c j", d=2, c=4)
            A("dve", lambda e: e.tensor_copy(out=chain[:, 0, 0, :], in_=s0[:, 0:4]), reads=[("s0", 0, c_) for c_ in range(4)], writes=[("chain", 0, 0)], ne=True)
            for r in range(3):
                A("dve", lambda e, r=r: e.tensor_tensor(out=chain[:, 0, r + 1, :], in0=chain[:, 0, r, :], in1=cgv[:, r, 0, :, 0], op=ALU.mult),
                  reads=[("chain", 0, r), "cg"], writes=[("chain", 0, r + 1)], ne=True)
                A("dve", lambda e, r=r: e.tensor_tensor(out=chain[:, 0, r + 1, :], in0=chain[:, 0, r + 1, :], in1=cgv[:, r, 0, :, 1], op=ALU.add),
                  reads=[("chain", 0, r + 1), "cg"], writes=[("chain", 0, r + 1)], ne=True)
            A("dve", lambda e: e.tensor_copy(out=chain[:, 1, 3, :], in_=s0[:, 4:8]), reads=[("s0", 1, c_) for c_ in range(4)], writes=[("chain", 1, 3)], ne=True)
            for r in (3, 2, 1):
                A("dve", lambda e, r=r: e.tensor_tensor(out=chain[:, 1, r - 1, :], in0=chain[:, 1, r, :], in1=cgv[:, r, 1, :, 0], op=ALU.mult),
                  reads=[("chain", 1, r), "cg"], writes=[("chain", 1, r - 1)], ne=True)
                A("dve", lambda e, r=r: e.tensor_tensor(out=chain[:, 1, r - 1, :], in0=chain[:, 1, r - 1, :], in1=cgv[:, r, 1, :, 1], op=ALU.add),
                  reads=[("chain", 1, r - 1), "cg"], writes=[("chain", 1, r - 1)], ne=True)
            for d in range(2):
                ck = [("chain", d, r) for r in range(4)]
                A("dve", lambda e, d=d: e.tensor_scalar(out=carry[:, d * 4:d * 4 + 4], in0=chain[:, d, 0, :], scalar1=flags[:, 2:3], scalar2=None, op0=ALU.mult),
                  reads=ck + ["flags"], writes=[("carry", d)], ne=True)
                for r in range(1, 4):
                    A("dve", lambda e, d=d, r=r: e.scalar_tensor_tensor(out=carry[:, d * 4:d * 4 + 4], in0=chain[:, d, r, :], scalar=flags[:, 2 + r:3 + r],
                                                                    in1=carry[:, d * 4:d * 4 + 4], op0=ALU.mult, op1=ALU.add),
                      reads=ck + ["flags", ("carry", d)], writes=[("carry", d)], ne=True)

            chk("lru1")
            barrier()
            sa = Alloc(nc, conv1_end, LIM)
            sq = [sa.t([128, 512], F32) for _ in range(2)]
            mean_t = [sa.t([128, 512], F32) for _ in range(2)]
            msq = [sa.t([128, 512], F32) for _ in range(2)]
            var_t = [sa.t([128, 512], F32) for _ in range(2)]
            tt = [sa.t([128, 512], F32) for _ in range(4)]
            cvk = [("cv", cc) for cc in range(4)]
            for ri, (n0, n1) in enumerate(full_r):
                n = n1 - n0
                rp = ri % 2
                mean_r, msq_r, var_r = mean_t[rp], msq[rp], var_t[rp]
                kme, kms, kva = ("mean_t", rp), ("msq", rp), ("var_t", rp)
                b1 = nb()

                def f(e, b1=b1, n0=n0, n1=n1):
                    for cc in range(4):
                        ins = e.matmul(ps[b1][:, 0:n1 - n0], lhsT=onesf[:], rhs=cv[:, cc, n0:n1], start=(cc == 0), stop=(cc == 3))
                    return ins
                A("pe", f, reads=cvk + ["onesf"], writes=[("ps", b1)], c=4.5)
                b2 = nb()
                for cc in range(4):
                    qi = cc % 2
                    A("act", lambda e, qi=qi, cc=cc, n0=n0, n1=n1: e.activation(out=sq[qi][:, 0:n1 - n0], in_=cv[:, cc, n0:n1], func=AF.Square),
                      reads=[("cv", cc)], writes=[("sq", qi)])
                    A("pe", lambda e, qi=qi, cc=cc, b2=b2, n=n: e.matmul(ps[b2][:, 0:n], lhsT=onesf[:], rhs=sq[qi][:, 0:n], start=(cc == 0), stop=(cc == 3)),
                      reads=[("sq", qi), "onesf"], writes=[("ps", b2)], c=1.2)
                fence([("ps", b1), ("ps", b2)])
                A("act", lambda e, b1=b1, n=n, mean_r=mean_r: e.activation(out=mean_r[:, 0:n], in_=ps[b1][:, 0:n], func=AF.Copy, scale=1.0 / 512), reads=[("ps", b1)], writes=[kme])
                A("pool", lambda e, n=n, mean_r=mean_r, msq_r=msq_r: e.tensor_tensor(out=msq_r[:, 0:n], in0=mean_r[:, 0:n], in1=mean_r[:, 0:n], op=ALU.mult), reads=[kme], writes=[kms])
                A("dve", lambda e, b2=b2, n=n, var_r=var_r, msq_r=msq_r: e.scalar_tensor_tensor(out=var_r[:, 0:n], in0=ps[b2][:, 0:n], scalar=1.0 / 512, in1=msq_r[:, 0:n], op0=ALU.mult, op1=ALU.subtract),
                  reads=[("ps", b2), kms], writes=[kva])
                A("dve", lambda e, n=n, var_r=var_r: e.tensor_scalar(out=var_r[:, 0:n], in0=var_r[:, 0:n], scalar1=EPS, scalar2=None, op0=ALU.add), reads=[kva], writes=[kva])
                A("act", lambda e, n=n, var_r=var_r: e.activation(out=var_r[:, 0:n], in_=var_r[:, 0:n], func=AF.Sqrt), reads=[kva], writes=[kva])
                A("dve", lambda e, n=n, var_r=var_r: e.reciprocal(out=var_r[:, 0:n], in_=var_r[:, 0:n]), reads=[kva], writes=[kva])
                for cc in range(4):
                    ti_ = cc
                    A("dve", lambda e, ti_=ti_, cc=cc, n0=n0, n1=n1, n=n, mean_r=mean_r: e.tensor_tensor(out=tt[ti_][:, 0:n], in0=cv[:, cc, n0:n1], in1=mean_r[:, 0:n], op=ALU.subtract),
                      reads=[("cv", cc), kme], writes=[("tt", ti_)])
                    A("dve", lambda e, ti_=ti_, cc=cc, n=n, var_r=var_r: e.scalar_tensor_tensor(out=tt[ti_][:, 0:n], in0=tt[ti_][:, 0:n], scalar=pcols[:, PB + cc:PB + cc + 1], in1=var_r[:, 0:n],
                                                                               op0=ALU.mult, op1=ALU.mult),
                      reads=[("tt", ti_), kva, "pcols"], writes=[("tt", ti_)])
                    A("act", lambda e, ti_=ti_, cc=cc, n0=n0, n1=n1, n=n: e.activation(out=Y[:, 4 + cc, n0:n1], in_=tt[ti_][:, 0:n], func=AF.Silu, bias=pcols[:, PB + 4 + cc:PB + 5 + cc], scale=1.0),
                      reads=[("tt", ti_), "pcols"], writes=[("Y", 4 + cc)])
            gt = [sa.t([128, 512], F32) for _ in range(2)]

            def gate_branch(bidx, col0):
                for cc in range(4):
                    buf, bkey = load_w(w_in[l], col0 + cc * 128, 128)
                    for (n0, n1) in full_r:
                        n = n1 - n0
                        b = proj_fm(buf, bkey, 0, n0, n1)
                        gi_ = b % 2
                        A("act", lambda e, b=b, gi_=gi_, n=n: e.activation(out=gt[gi_][:, 0:n], in_=ps[b][:, 0:n], func=AF.Silu), reads=[("ps", b)], writes=[("gt", gi_)])
                        A("dve", lambda e, gi_=gi_, n0=n0, n1=n1, n=n, cc=cc: e.tensor_tensor(out=Y[:, bidx * 4 + cc, n0:n1], in0=Y[:, bidx * 4 + cc, n0:n1], in1=gt[gi_][:, 0:n], op=ALU.mult),
                          reads=[("gt", gi_), ("Y", bidx * 4 + cc)], writes=[("Y", bidx * 4 + cc)])
            gate_branch(1, C_GCONV)

            chk("conv")
            barrier()
            sa = Alloc(nc, SCR, LIM)
            gt = [sa.t([128, 512], F32) for _ in range(2)]
            qT = sa.t([128, 4, NFULL], BF16)
            kT = sa.t([128, 2, NT], BF16)
            vS = sa.t([128, 20, 2, 128], BF16)
            cs_off = [sa.off, sa.off + NT * 4]
            cosT = sa.t([128, NT], F32); sinT = sa.t([128, NT], F32)
            E = [nc.alloc_sbuf_tensor_at("E%d_%d" % (i_, l), [128, 5, 2, 2, 2, 128], BF16, offset=cs_off[i_]) for i_ in range(2)]
            ekeys_all = [[("E", i_, j0, hp, g) for j0 in (0, 2, 4) for hp in range(2) for g in range(2)] + [("Em", i_, j) for j in range(5)] for i_ in range(2)]
            t1 = [sa.t([128, 512], F32) for _ in range(3)]
            qb = [sa.t([128, 512], BF16) for _ in range(3)]
            rpi = [0]
            den = sa.t([128, 512], F32)
            sinkbc = sa.t([128, 4, 128], F32)
            sk = sa.t([128, 8], F32)
            A("sp", lambda e: e.dma_start(out=cosT[:], in_=cos_d[:, :]), writes=["cosT"] + ekeys_all[0], dma=True)
            A("sp", lambda e: e.dma_start(out=sinT[:], in_=sin_d[:, :]), writes=["sinT"] + ekeys_all[1], dma=True)
            A("sp", lambda e: e.dma_start(out=sk[:], in_=attn_sink[l:l + 1, :].partition_broadcast(128)), writes=["sk"], dma=True)
            A("act", lambda e: e.activation(out=sk[:], in_=sk[:], func=AF.Exp), reads=["sk"], writes=["sk"])

            def f(e):
                for c_ in range(4):
                    for hp in range(2):
                        ins = e.tensor_scalar(out=sinkbc[hp * 64:(hp + 1) * 64, c_, :], in0=zeros[hp * 64:(hp + 1) * 64, :],
                                              scalar1=sk[hp * 64:(hp + 1) * 64, 2 * c_ + hp:2 * c_ + hp + 1], scalar2=None, op0=ALU.add)
                return ins
            A("dve", f, reads=["sk", "zeros"], writes=["sinkbc"])
            A("pool", lambda e: e.memset(vS[:], 0.0), writes=["vS"])

            def rope(b, dst, n0, n1):
                n = n1 - n0
                rpi[0] += 1
                i = rpi[0] % 3
                A("act", lambda e: e.activation(out=qb[i][:, 0:n], in_=ps[b][:, 0:n], func=AF.Copy), reads=[("ps", b)], writes=[("qb", i)])
                A("dve", lambda e: e.tensor_tensor(out=t1[i][:, 0:n], in0=ps[b][:, 0:n], in1=cosT[:, n0:n1], op=ALU.mult), reads=[("ps", b), "cosT", ("qb", i)], writes=[("t1", i)])
                b2 = nb()
                A("pe", lambda e: e.matmul(ps[b2][:, 0:n], lhsT=perm[:], rhs=qb[i][:, 0:n], start=True, stop=True), reads=[("qb", i), "perm"], writes=[("ps", b2)])
                A("dve", lambda e: e.tensor_tensor(out=qb[i][:, 0:n], in0=ps[b2][:, 0:n], in1=sinT[:, n0:n1], op=ALU.mult), reads=[("ps", b2), "sinT"], writes=[("qb", i)])
                A("pool", lambda e: e.tensor_tensor(out=dst, in0=qb[i][:, 0:n], in1=t1[i][:, 0:n], op=ALU.add), reads=[("qb", i), ("t1", i)], writes=["qkT"])
            for c_ in range(4):
                buf, bkey = load_w(w_in[l], C_Q + c_ * 128, 128)
                for (n0, n1) in full_r:
                    b = proj_fm(buf, bkey, 0, n0, n1)
                    rope(b, qT[:, c_, n0:n1], n0, n1)
            for g in range(2):
                for dup in range(2):
                    A("pool", lambda e, g=g, dup=dup: e.dma_start(out=wk[:, :, g * 128 + dup * 64:g * 128 + dup * 64 + 64],
                                                                in_=w_in[l][:, C_K + g * 64:C_K + g * 64 + 64].rearrange("(kc p) c -> p kc c", p=128)),
                      writes=[("wk", g, dup)], dma=True)
            for g in range(2):
                for (n0, n1) in FULL_R + [(LH0, NT)]:
                    b = nb()

                    def f(e, b=b, g=g, n0=n0, n1=n1):
                        for kc in range(KC):
                            ins = e.matmul(ps[b][:, 0:n1 - n0], lhsT=wk[:, kc, g * 128:(g + 1) * 128], rhs=hT[:, kc, n0:n1], start=(kc == 0), stop=(kc == KC - 1))
                        return ins
                    A("pe", f, reads=[("wk", g, 0), ("wk", g, 1), "hT"], writes=[("ps", b)])
                    rope(b, kT[:, g, n0:n1], n0, n1)
            bufv, kv = load_w(w_in[l], C_V, 128)
            for ti in range(20):
                b = nb()

                def f(e, b=b, ti=ti):
                    for kc in range(KC):
                        ins = e.matmul(ps[b][:, 0:128], lhsT=hT[:, kc, ti * 128:(ti + 1) * 128], rhs=bufv[:, kc, 0:128], start=(kc == 0), stop=(kc == KC - 1))
                    return ins
                A("pe", f, reads=[kv, "hT"], writes=[("ps", b)])
                A("act", lambda e, b=b, ti=ti: e.activation(out=vS[:, ti, :, 0:64], in_=ps[b][:, 0:128].rearrange("p (g d) -> p g d", g=2), func=AF.Copy),
                  reads=[("ps", b), "vS"], writes=[("vS", ti)])
            vO = sa.t([128, 20, 2, 128], BF16)
            A("pool", lambda e: e.memset(vO[:], 0.0), writes=["vO"])
            for ti in range(20):
                A("pool", lambda e, ti=ti: e.tensor_copy(out=vO[:, ti, :, 64:128], in_=vS[:, ti, :, 0:64]), reads=[("vS", ti), "vO"], writes=[("vO", ti)])
            onesE = sa.t([128, 2, 128], BF16)
            A("pool", lambda e: e.memset(onesE[:], 0.0), writes=["onesE0"])
            A("pool", lambda e: e.memset(onesE[:, 0, 0:64], 1.0), reads=["onesE0"], writes=["onesE1"])
            A("pool", lambda e: e.memset(onesE[:, 1, 64:128], 1.0), reads=["onesE0"], writes=["onesE2"])
            onesEk = ["onesE1", "onesE2"]
            sbk = [0]
            for qt in range(ntile_full):
                ei = qt % 2
                if qt < 16:
                    kl = [(LH0 // 128 if qt == 0 else qt - 1, 2 if qt == 0 else 0), (qt, None),
                          (RH0 // 128 if qt == 15 else qt + 1, 3 if qt == 15 else 1), (16, None), (17, None)]
                else:
                    kl = [(16, None), (17, None)]
                q0 = qt * 128
                for g in range(2):
                    for hp in range(2):
                        for j0 in range(0, len(kl), 2):
                            js = list(range(j0, min(j0 + 2, len(kl))))
                            sbk[0] = (sbk[0] + 1) % 3
                            b = sbk[0]

                            def f(e, b=b, js=js, g=g, hp=hp, kl=kl, q0=q0):
                                for jj, j in enumerate(js):
                                    kt = kl[j][0]
                                    ins = e.matmul(ps[b][:, jj * 256:(jj + 1) * 256], lhsT=kT[hp * 64:(hp + 1) * 64, g, kt * 128:(kt + 1) * 128],
                                                   rhs=qT[hp * 64:(hp + 1) * 64, 2 * g:2 * g + 2, q0:q0 + 128], start=True, stop=True)
                                return ins
                            A("pe", f, reads=["qkT"], writes=[("ps", b)], c=0.1 + 0.14 * len(js))
                            A("act", lambda e, b=b, js=js, g=g, hp=hp, ei=ei, j0=j0: e.activation(
                                out=E[ei][:, j0:j0 + len(js), hp, g, :, :], in_=ps[b][:, 0:256 * len(js)].rearrange("p (j c q) -> p j c q", j=len(js), c=2),
                                func=AF.Exp, scale=0.125), reads=[("ps", b)], writes=[("E", ei, j0, hp, g)])
                ek = [("E", ei, j0, hp, g) for j0 in (0, 2, 4) for hp in range(2) for g in range(2)]
                for j, (kt, mi) in enumerate(kl):
                    if mi is not None:
                        A("pool", lambda e, ei=ei, j=j, mi=mi: e.tensor_tensor(out=E[ei][:, j].rearrange("p a b c q -> p (a b c) q"),
                                                                              in0=E[ei][:, j].rearrange("p a b c q -> p (a b c) q"),
                                                                              in1=masks[:, mi:mi + 1, :].to_broadcast([128, 8, 128]), op=ALU.mult),
                          reads=ek + ["masks"], writes=[("Em", ei, j)], c=1.6)
                emk = [("Em", ei, j) for j in range(5)]
                bn_ = 3
                bd_ = 4

                def f(e, ei=ei, kl=kl, bn_=bn_, bd_=bd_):
                    cnt = len(kl) * 2
                    for g in range(2):
                        i = 0
                        for j, (kt, mi) in enumerate(kl):
                            for hp in range(2):
                                vv = vS if hp == 0 else vO
                                e.matmul(ps[bn_][:, g * 256:(g + 1) * 256], lhsT=vv[:, kt, g, :], rhs=E[ei][:, j, hp, g, :, :],
                                         start=(i == 0), stop=(i == cnt - 1))
                                i += 1
                    for g in range(2):
                        i = 0
                        for j, (kt, mi) in enumerate(kl):
                            for hp in range(2):
                                ins = e.matmul(ps[bd_][:, g * 256:(g + 1) * 256], lhsT=onesE[:, hp, :], rhs=E[ei][:, j, hp, g, :, :],
                                               start=(i == 0), stop=(i == cnt - 1))
                                i += 1
                    return ins
                A("pe", f, reads=ek + emk + [("vS", kt) for kt, _ in kl] + [("vO", kt) for kt, _ in kl] + onesEk, writes=[("ps", bn_), ("ps", bd_)], c=0.1 + 0.14 * 8 * len(kl))
                A("dve", lambda e, bd_=bd_: e.tensor_tensor(out=den[:], in0=ps[bd_][:], in1=sinkbc[:].rearrange("p c q -> p (c q)"), op=ALU.add),
                  reads=[("ps", bd_), "sinkbc"], writes=["den"])
                A("dve", lambda e: e.reciprocal(out=den[:], in_=den[:]), reads=["den"], writes=["den"])
                A("dve", lambda e, bn_=bn_, q0=q0: e.tensor_tensor(out=Y[:, 0:4, q0:q0 + 128], in0=ps[bn_][:].rearrange("p (c q) -> p c q", c=4),
                                                                in1=den[:].rearrange("p (c q) -> p c q", c=4), op=ALU.mult),
                  reads=[("ps", bn_), "den"], writes=[("Y", c_) for c_ in range(4)])
            gate_branch(0, C_GATT)

            chk("att")
            barrier()
            sa = Alloc(nc, SCR, LIM)
            gt = [sa.t([128, 512], F32) for _ in range(2)]
            a2 = [sa.t([128, 2048], F32) for _ in range(2)]
            u2 = [sa.t([128, 2048], F32) for _ in range(2)]
            h2 = [sa.t([128, 2048], F32) for _ in range(2)]
            for cc in range(4):
                for d in range(2):
                    cmb = d * 4 + cc
                    A("sp", lambda e, cmb=cmb, d=d: e.dma_start(out=a2[d][:], in_=au[(cmb * 2) * 128:(cmb * 2 + 1) * 128, :]), reads=[("au", cmb, 0)], writes=[("a2", d)], dma=True)
                    A("sp", lambda e, cmb=cmb, d=d: e.dma_start(out=u2[d][:], in_=au[(cmb * 2 + 1) * 128:(cmb * 2 + 2) * 128, :]), reads=[("au", cmb, 1)], writes=[("u2", d)], dma=True)
                A("dve", lambda e, cc=cc: e.tensor_tensor_scan(out=h2[0][:], data0=a2[0][:], data1=u2[0][:], initial=carry[:, cc:cc + 1], op0=ALU.mult, op1=ALU.add),
                  reads=[("a2", 0), ("u2", 0), ("carry", 0)], writes=[("h2", 0)])
                A("dve", lambda e, cc=cc: e.tensor_tensor_scan(out=h2[1][:, ::-1], data0=a2[1][:, ::-1], data1=u2[1][:, ::-1], initial=carry[:, 4 + cc:5 + cc],
                                                             op0=ALU.mult, op1=ALU.add),
                  reads=[("a2", 1), ("u2", 1), ("carry", 1)], writes=[("h2", 1)])
                A("pool", lambda e, cc=cc: e.tensor_tensor(out=Y[:, 8 + cc, 0:2048], in0=h2[0][:], in1=h2[1][:], op=ALU.add), reads=[("h2", 0), ("h2", 1)], writes=[("Y", 8 + cc)])
                if not last:
                    A("pool", lambda e, cc=cc: e.tensor_copy(out=Y[:, 8 + cc, 2048:2304], in_=ylru_ctx[:, cc, :]), reads=[("ylc", cc)], writes=[("Y", 8 + cc)])
            gate_branch(2, C_GLRU)

            chk("lru2")
            barrier()
            sa = Alloc(nc, SCR, LIM)
            mT = sa.t([128, KC, NFULL], BF16)
            wout = sa.t([128, KC, D], BF16)
            mrg_off = sa.off
            wg = [sa.t([128, KC, 3, 128], BF16) for _ in range(2)]
            wbr = [sa.t([128, 3, 4, 128], BF16) for _ in range(2)]
            Gt = [sa.t([128, 512], F32) for _ in range(2)]
            acc = sa.t([128, 512], F32)
            tmp = sa.t([128, 512], F32)
            for kc in range(KC):
                A("pool", lambda e, kc=kc: e.dma_start(out=wout[:, kc, :], in_=w_out[l][kc * 128:(kc + 1) * 128, :]), writes=[("wout", kc)], dma=True)
            for dc in range(8):
                wi = dc % 2
                for b_ in range(3):
                    A("pool", lambda e, wi=wi, dc=dc, b_=b_: e.dma_start(out=wg[wi][:, :, b_, :],
                                                                     in_=w_gate[l][:, b_ * D + dc * 128:b_ * D + (dc + 1) * 128].rearrange("(kc p) c -> p kc c", p=128)),
                      writes=[("wg", wi, b_)], dma=True)
                    A("pool", lambda e, wi=wi, dc=dc, b_=b_: e.dma_start(out=wbr[wi][:, b_, :, :],
                                                                     in_=w_branch[l, b_][:, dc * 128:(dc + 1) * 128].rearrange("(kc p) c -> p kc c", p=128)),
                      writes=[("wbr", wi, b_)], dma=True)
                for (n0, n1) in full_r:
                    n = n1 - n0
                    for b_ in range(3):
                        bg = nb()

                        def f(e, bg=bg, wi=wi, b_=b_, n0=n0, n1=n1):
                            for kc in range(KC):
                                ins = e.matmul(ps[bg][:, 0:n1 - n0], lhsT=wg[wi][:, kc, b_, :], rhs=hT[:, kc, n0:n1], start=(kc == 0), stop=(kc == KC - 1))
                            return ins
                        A("pe", f, reads=[("wg", wi, b_), "hT"], writes=[("ps", bg)], c=0.1 + 8 * 0.27 * n / 512.0)
                        gi_ = bg % 2
                        A("act", lambda e, bg=bg, gi_=gi_, n=n, b_=b_, dc=dc: e.activation(out=Gt[gi_][:, 0:n], in_=ps[bg][:, 0:n], func=AF.Sigmoid,
                                                                                     bias=pcols[:, PB + 72 + b_ * 8 + dc:PB + 73 + b_ * 8 + dc], scale=1.0),
                          reads=[("ps", bg), "pcols"], writes=[("Gt", gi_)])
                        bp = nb()

                        def f(e, bp=bp, wi=wi, b_=b_, n0=n0, n1=n1):
                            for kc in range(4):
                                ins = e.matmul(ps[bp][:, 0:n1 - n0], lhsT=wbr[wi][:, b_, kc, :], rhs=Y[:, b_ * 4 + kc, n0:n1], start=(kc == 0), stop=(kc == 3))
                            return ins
                        A("pe", f, reads=[("wbr", wi, b_)] + [("Y", b_ * 4 + kc) for kc in range(4)], writes=[("ps", bp)], c=0.1 + 4 * 0.27 * n / 512.0)
                        if b_ == 0:
                            A("dve", lambda e, bp=bp, gi_=gi_, n=n: e.tensor_tensor(out=acc[:, 0:n], in0=ps[bp][:, 0:n], in1=Gt[gi_][:, 0:n], op=ALU.mult),
                              reads=[("ps", bp), ("Gt", gi_)], writes=["acc"])
                        else:
                            A("dve", lambda e, bp=bp, gi_=gi_, n=n: e.tensor_tensor(out=tmp[:, 0:n], in0=ps[bp][:, 0:n], in1=Gt[gi_][:, 0:n], op=ALU.mult),
                              reads=[("ps", bp), ("Gt", gi_)], writes=["tmp"])
                            if b_ == 1:
                                A("pool", lambda e, n=n: e.tensor_tensor(out=acc[:, 0:n], in0=acc[:, 0:n], in1=tmp[:, 0:n], op=ALU.add), reads=["acc", "tmp"], writes=["acc"])
                            else:
                                A("pool", lambda e, n=n, dc=dc, n0=n0, n1=n1: e.tensor_tensor(out=mT[:, dc, n0:n1], in0=acc[:, 0:n], in1=tmp[:, 0:n], op=ALU.add),
                                  reads=["acc", "tmp"], writes=[("mT", dc)])

            chk("merge")
            barrier()
            sa = Alloc(nc, mrg_off, LIM)
            gbc = [sa.t([128, D], F32) for _ in range(1 if last else 2)]
            lng = sa.t([128, D], F32); lnb = sa.t([128, D], F32)
            A("sp", lambda e: e.dma_start(out=lng[:], in_=ln_g[l:l + 1, :].partition_broadcast(128)), writes=["lng"], dma=True)
            A("sp", lambda e: e.dma_start(out=lnb[:], in_=ln_b[l:l + 1, :].partition_broadcast(128)), writes=["lnb"], dma=True)
            for r in range(len(gbc)):
                for h_ in range(2):
                    b = nb()
                    A("pe", lambda e, b=b, r=r, h_=h_: e.matmul(ps[b][:, 0:512], lhsT=sel[0:2, r * 128:(r + 1) * 128], rhs=grow[0:2, h_ * 512:(h_ + 1) * 512], start=True, stop=True),
                      reads=["sel", "grow"], writes=[("ps", b)])
                    fence([("ps", b)])
                    A("act", lambda e, b=b, r=r, h_=h_: e.activation(out=gbc[r][:, h_ * 512:(h_ + 1) * 512], in_=ps[b][:, 0:512], func=AF.Copy), reads=[("ps", b)], writes=[("gbc", r, h_)])
            mk = [("mT", dc) for dc in range(8)]
            wk_ = [("wout", kc) for kc in range(8)]
            ya = Alloc(nc, Y_OFF, Y_END)
            G4 = 3
            xr = [ya.t([128, D], F32) for _ in range(2 * G4)]
            zt = [ya.t([128, D], F32) for _ in range(2 * G4)]
            st2 = [sa.t([128, G4, 2, 6], F32) for _ in range(2)]
            mv2 = [sa.t([128, G4, 2], F32) for _ in range(2)]
            rs2 = [sa.t([128, G4], F32) for _ in range(2)]
            all_tiles = list(range(ntile_full))
            groups = [all_tiles[i:i + G4] for i in range(0, ntile_full, G4)]
            final = last or l == nlayers - 1
            for gi, grp in enumerate(groups):
                s_ = gi % 2
                ng = len(grp)
                for j, ti in enumerate(grp):
                    bi = s_ * G4 + j
                    r = 1 if ti >= 16 else 0
                    if l == 0:
                        src = x_in[ti * 128:(ti + 1) * 128, :] if ti < 16 else ctx_in[(ti - 16) * 128:(ti - 15) * 128, :]
                    else:
                        src = xs[ti * 128:(ti + 1) * 128, :]
                    A("sp", lambda e, bi=bi, src=src: e.dma_start(out=xr[bi][:], in_=src), reads=[("xs", ti)], writes=[("xr", bi)], dma=True)
                    for h_ in range(2):
                        b = nb()

                        def f(e, b=b, ti=ti, h_=h_):
                            for dc in range(8):
                                ins = e.matmul(ps[b][:, 0:512], lhsT=mT[:, dc, ti * 128:(ti + 1) * 128], rhs=wout[:, dc, h_ * 512:(h_ + 1) * 512], start=(dc == 0), stop=(dc == 7))
                            return ins
                        A("pe", f, reads=mk + wk_, writes=[("ps", b)], c=2.3)
                        A("dve", lambda e, b=b, bi=bi, r=r, h_=h_: e.tensor_tensor(out=zt[bi][:, h_ * 512:(h_ + 1) * 512], in0=ps[b][:, 0:512], in1=gbc[r][:, h_ * 512:(h_ + 1) * 512], op=ALU.mult),
                          reads=[("ps", b), ("gbc", r, h_)], writes=[("zt", bi, h_)])
                        A("pool", lambda e, bi=bi, h_=h_: e.tensor_tensor(out=zt[bi][:, h_ * 512:(h_ + 1) * 512], in0=zt[bi][:, h_ * 512:(h_ + 1) * 512], in1=xr[bi][:, h_ * 512:(h_ + 1) * 512], op=ALU.add),
                          reads=[("zt", bi, h_), ("xr", bi)], writes=[("zt", bi, h_)])
                for j in range(ng):
                    bi = s_ * G4 + j
                    for h_ in range(2):
                        A("dve", lambda e, bi=bi, j=j, h_=h_, s_=s_: e.bn_stats(out=st2[s_][:, j, h_, :], in_=zt[bi][:, h_ * 512:(h_ + 1) * 512]),
                          reads=[("zt", bi, h_)], writes=[("st2", s_, j, h_)])
                for j in range(ng):
                    A("dve", lambda e, j=j, s_=s_: e.bn_aggr(out=mv2[s_][:, j, :], in_=st2[s_][:, j, :, :]),
                      reads=[("st2", s_, j, 0), ("st2", s_, j, 1)], writes=[("mv2", s_, j)])
                mvk = [("mv2", s_, j) for j in range(ng)]
                A("dve", lambda e, s_=s_, ng=ng: e.tensor_scalar(out=rs2[s_][:, 0:ng], in0=mv2[s_][:, 0:ng, 1], scalar1=EPS / (ALPHA * ALPHA), scalar2=None, op0=ALU.add),
                  reads=mvk, writes=[("rs2", s_)])
                A("act", lambda e, s_=s_, ng=ng: e.activation(out=rs2[s_][:, 0:ng], in_=rs2[s_][:, 0:ng], func=AF.Sqrt), reads=[("rs2", s_)], writes=[("rs2", s_)])
                A("dve", lambda e, s_=s_, ng=ng: e.reciprocal(out=rs2[s_][:, 0:ng], in_=rs2[s_][:, 0:ng]), reads=[("rs2", s_)], writes=[("rs2", s_)])
                for j in range(ng):
                    bi = s_ * G4 + j
                    zk = [("zt", bi, 0), ("zt", bi, 1)]
                    A("dve", lambda e, bi=bi, j=j, s_=s_: e.scalar_tensor_tensor(out=zt[bi][:], in0=zt[bi][:], scalar=mv2[s_][:, j, 0:1], in1=lng[:], op0=ALU.subtract, op1=ALU.mult),
                      reads=zk + [("mv2", s_, j), "lng"], writes=zk)
                for j in range(ng):
                    bi = s_ * G4 + j
                    zk = [("zt", bi, 0), ("zt", bi, 1)]
                    A("dve", lambda e, bi=bi, j=j, s_=s_: e.scalar_tensor_tensor(out=zt[bi][:], in0=zt[bi][:], scalar=rs2[s_][:, j:j + 1], in1=lnb[:], op0=ALU.mult, op1=ALU.add),
                      reads=zk + [("rs2", s_), "lnb"], writes=zk)
                for j, ti in enumerate(grp):
                    bi = s_ * G4 + j
                    zk = [("zt", bi, 0), ("zt", bi, 1)]
                    if final:
                        if ti < 16:
                            A("sp", lambda e, bi=bi, ti=ti: e.dma_start(out=out[ti * 128:(ti + 1) * 128, :], in_=zt[bi][:]), reads=zk, writes=[("out", ti)], dma=True)
                    else:
                        A("sp", lambda e, bi=bi, ti=ti: e.dma_start(out=xs[ti * 128:(ti + 1) * 128, :], in_=zt[bi][:]), reads=zk, writes=[("xs", ti)], dma=True)
                        if ti == 0:
                            A("sp", lambda e, bi=bi: e.dma_start(out=send[0:128, :], in_=zt[bi][:]), reads=zk, writes=[("send", 0)], dma=True)
                        if ti == 15:
                            A("sp", lambda e, bi=bi: e.dma_start(out=send[128:256, :], in_=zt[bi][:]), reads=zk, writes=[("send", 1)], dma=True)
            if not (last or l == nlayers - 1):
                A("pool", lambda e: e.collective_compute("AllGather", ALU.bypass, replica_groups=RG, ins=[send.opt()], outs=[gath.opt()]),
                  reads=[("send", 0), ("send", 1)], writes=["gath"], dma="cc", ne=True)
        try:
            for l_ in range(nlayers):
                layer(l_)
        except _Stop:
            pass
        if dbg:
            dbg_h = nc.dram_tensor("dbg_h", [128, KC * NT], BF16, kind="ExternalOutput").ap()
            dbg_y = nc.dram_tensor("dbg_y", [128, 12 * NFULL], BF16, kind="ExternalOutput").ap()
            A("sp", lambda e: e.dma_start(out=dbg_h[:, :], in_=hT[:].rearrange("p a b -> p (a b)")), reads=["hT"], writes=["dbg_h"], dma=True)
            A("sp", lambda e: e.dma_start(out=dbg_y[:, :], in_=Y[:].rearrange("p a b -> p (a b)")), reads=[("Y", i) for i in range(12)], writes=["dbg_y"], dma=True)
            A("sp", None, reads=["dbg_h", "dbg_y"])
        A("sp", None, reads=[("out", ti) for ti in range(16)])
        S.run()
    return nc


def _rope_tables(pos):
    quarter = 16
    inv = (10000.0 ** (-np.arange(quarter, dtype=np.float32) / quarter)).astype(np.float32)
    row = (pos // 64).astype(np.float32)
    col = (pos % 64).astype(np.float32)
    ang_r = row[:, None] * inv[None, :]
    ang_c = col[:, None] * inv[None, :]
    ang = np.concatenate([ang_r, ang_r, ang_c, ang_c], -1).astype(np.float32)
    return np.cos(ang).astype(np.float32), np.sin(ang).astype(np.float32)


def make_in_maps(inputs):
    x = np.ascontiguousarray(inputs["x"], dtype=np.float32)
    B, Sq, _ = x.shape
    bf = ml_dtypes.bfloat16
    perm = np.zeros((128, 128), np.float32)
    for hd in range(2):
        o = hd * 64
        for m in range(64):
            seg, i = divmod(m, 16)
            if seg == 0:
                perm[o + m + 16, o + m] = -1.0
            elif seg == 1:
                perm[o + m - 16, o + m] = 1.0
            elif seg == 2:
                perm[o + m + 16, o + m] = -1.0
            else:
                perm[o + m - 16, o + m] = 1.0
    jj = np.arange(128)[:, None]
    qq = np.arange(128)[None, :]
    maskP = (jj >= qq).astype(np.float32)
    maskN = (jj <= qq).astype(np.float32)
    sel = np.zeros((2, 256), np.float32)
    sel[0, 0:128] = 1.0
    sel[1, 128:256] = 1.0
    wnames = ["w_ada", "b_ada", "w_in", "attn_sink", "conv_dw_w", "conv_dw_b", "conv_ln_g", "conv_ln_b", "lru_conv_w", "lru_conv_b",
              "lru_w_a", "lru_b_a", "lru_w_x", "lru_b_x", "lru_lambda", "w_branch", "w_gate", "b_gate", "w_out", "ln_g", "ln_b"]
    shared = {k: np.ascontiguousarray(inputs[k], dtype=np.float32) for k in wnames}
    shared["perm"] = perm.astype(bf)
    shared["identb"] = np.eye(128, dtype=np.float32).astype(bf)
    shared["identf"] = np.eye(128, dtype=np.float32)
    shared["sel"] = sel
    maps = []
    for c in range(8):
        b, r = divmod(c, 4)
        t0 = r * T_OWN
        m = dict(shared)
        m["x_in"] = x[b, t0:t0 + T_OWN]
        xh = np.zeros((256, D), np.float32)
        if r > 0:
            xh[0:128] = x[b, t0 - 128:t0]
        if r < 3:
            xh[128:256] = x[b, t0 + T_OWN:t0 + T_OWN + 128]
        m["xh_in"] = xh
        m["ctx_in"] = np.ascontiguousarray(inputs["ctx"][b], dtype=np.float32)
        m["c_in"] = np.stack([inputs["c"][b], inputs["c_ctx"]]).astype(np.float32)
        pos = np.zeros(NT, np.int64)
        pos[0:T_OWN] = t0 + np.arange(T_OWN)
        pos[LH0:LH0 + 128] = np.clip(t0 - 128 + np.arange(128), 0, Sq - 1)
        pos[RH0:RH0 + 128] = np.clip(t0 + T_OWN + np.arange(128), 0, Sq - 1)
        cos, sin = _rope_tables(pos)
        cos[CTX0:CTX0 + 256] = 1.0
        sin[CTX0:CTX0 + 256] = 0.0
        m["cosT"] = np.ascontiguousarray(np.concatenate([cos.T, cos.T], 0))
        m["sinT"] = np.ascontiguousarray(np.concatenate([sin.T, sin.T], 0))
        hl = 1.0 if r > 0 else 0.0
        hr = 1.0 if r < 3 else 0.0
        m["masks"] = np.concatenate([maskP, maskN, maskP * hl, maskN * hr], 1).astype(bf)
        fl = np.zeros((128, 8), np.float32)
        fl[:, 0] = hl
        fl[:, 1] = hr
        fl[:, 2 + r] = 1.0
        m["flags"] = fl
        maps.append(m)
    return maps


_NC_CACHE = {}


def kernel(**inputs):
    if "nc" not in _NC_CACHE:
        _NC_CACHE["nc"] = build_nc()
    nc = _NC_CACHE["nc"]
    maps = make_in_maps(inputs)
    res = run_bass_kernel_spmd(nc, maps, core_ids=list(range(8)))
    x = inputs["x"]
    outp = np.zeros(x.shape, np.float32)
    for c in range(8):
        b, r = divmod(c, 4)
        outp[b, r * T_OWN:(r + 1) * T_OWN] = res.results[c]["out"]
    return outp
```

```python
import numpy as np
import ml_dtypes
from contextlib import ExitStack
import concourse.bass as bass
import concourse.mybir as mybir
from concourse.bass_utils import run_bass_kernel_spmd

F32 = mybir.dt.float32
BF16 = mybir.dt.bfloat16
ALU = mybir.AluOpType
AF = mybir.ActivationFunctionType

DEPTH = 4
D = 1024
KC = 8
T_OWN = 2048
NT = 2560
NFULL = 2304
CTX0 = 2048
LH0 = 2304
RH0 = 2432
ALPHA = (2 * DEPTH) ** 0.25
EPS = 1e-6
C_Q, C_K, C_V, C_GATT, C_CA, C_CG, C_GCONV, C_XLRU, C_GLRU = 0, 512, 640, 768, 1280, 1792, 2304, 2816, 3328
FULL_R = [(0, 512), (512, 1024), (1024, 1536), (1536, 2048), (2048, 2304)]

ENG_NAMES = ("pe", "act", "dve", "pool", "sp")
SAME_ENG_SYNC = {"act", "dve", "pool"}


class Op:
    __slots__ = ("eng", "fn", "deps", "sig", "tok_sem", "tok_val", "dma", "idx", "cost", "lat", "seq", "alldeps")

    DEF_COST = {"pe": 1.5, "act": 0.5, "dve": 0.6, "pool": 1.0, "sp": 0.1}

    def __init__(self, eng, fn, dma, c=None):
        self.eng = eng
        self.fn = fn
        self.dma = dma
        if dma == "cc":
            self.cost, self.lat = 1.0, 100.0
        elif dma:
            self.cost = 0.1 if eng != "pool" else 0.8
            self.lat = self.cost + (c if c is not None else 3.0)
        else:
            self.cost = c if c is not None else self.DEF_COST[eng]
            self.lat = self.cost + 0.15
        if fn is None:
            self.cost = self.lat = 0.0
        self.deps = set()
        self.sig = False
        self.tok_sem = None
        self.tok_val = 0
        self.idx = -1


class Sched:
    NDMA = {"sp": 8, "pool": 6, "act": 4}

    def __init__(self, nc, stack):
        self.nc = nc
        self.ops = {e: [] for e in ENG_NAMES}
        self.all_ops = []
        self.reorder = True
        self.last_w = {}
        self.readers = {}
        self.esem = {}
        for e in ("pe", "act", "dve", "pool"):
            self.esem[e] = stack.enter_context(nc.semaphore("s_" + e))
        self.dsem = {}
        for e, n in self.NDMA.items():
            self.dsem[e] = [stack.enter_context(nc.semaphore("d_%s%d" % (e, i))) for i in range(n)]
        self.ccsem = stack.enter_context(nc.semaphore("s_cc"))
        self.ccval = 0
        self.dcount = {e: 0 for e in self.NDMA}
        self.dlast = {e: [None] * n for e, n in self.NDMA.items()}
        self.dval = {e: [0] * n for e, n in self.NDMA.items()}

    def add(self, eng, fn, reads=(), writes=(), dma=False, c=None, ne=False):
        if "EPOCH" not in writes and not ne:
            reads = list(reads) + ["EPOCH"]
        if dma == "cc":
            reads = list(reads) + ["CCORDER"]
            writes = list(writes) + ["CCORDER"]
        op = Op(eng, fn, dma, c)
        op.seq = len(self.all_ops)
        self.all_ops.append(op)
        deps = op.deps
        for k in reads:
            w = self.last_w.get(k)
            if w is not None:
                deps.add(w)
        for k in writes:
            w = self.last_w.get(k)
            if w is not None:
                deps.add(w)
            for r in self.readers.get(k, ()):
                deps.add(r)
        for k in reads:
            self.readers.setdefault(k, []).append(op)
        for k in writes:
            self.last_w[k] = op
            self.readers[k] = []
        if dma == "cc":
            self.ccval += 1
            op.tok_sem = self.ccsem
            op.tok_val = self.ccval
            op.sig = True
        elif dma:
            n = self.dcount[eng]
            self.dcount[eng] = n + 1
            slot = n % self.NDMA[eng]
            prev = self.dlast[eng][slot]
            if prev is not None:
                deps.add(prev)
            self.dlast[eng][slot] = op
            self.dval[eng][slot] += 16
            op.tok_sem = self.dsem[eng][slot]
            op.tok_val = self.dval[eng][slot]
            op.sig = True
        deps.discard(op)
        op.idx = len(self.ops[eng])
        self.ops[eng].append(op)
        return op

    def list_schedule(self):
        import heapq
        ops = self.all_ops
        ndep = [0] * len(ops)
        users = [[] for _ in ops]
        for op in ops:
            ndep[op.seq] = len(op.deps)
            for d in op.deps:
                users[d.seq].append(op)
        ready_t = [0.0] * len(ops)
        waiting = {e: [] for e in ENG_NAMES}
        avail = {e: [] for e in ENG_NAMES}
        free = {e: 0.0 for e in ENG_NAMES}
        for op in ops:
            if ndep[op.seq] == 0:
                heapq.heappush(waiting[op.eng], (0.0, op.seq))
        newq = {e: [] for e in ENG_NAMES}
        left = len(ops)
        while left:
            best = None
            for e in ENG_NAMES:
                w, a = waiting[e], avail[e]
                while w and w[0][0] <= free[e]:
                    heapq.heappush(a, heapq.heappop(w)[1])
                if a:
                    cand = (free[e], a[0], e, True)
                elif w:
                    cand = (w[0][0], w[0][1], e, False)
                else:
                    continue
                if best is None or cand[:2] < best[:2]:
                    best = cand
            t, sq, e, from_avail = best
            if from_avail:
                heapq.heappop(avail[e])
            else:
                heapq.heappop(waiting[e])
            op = ops[sq]
            newq[e].append(op)
            free[e] = t + op.cost
            done = t + op.lat
            left -= 1
            for u in users[sq]:
                if done > ready_t[u.seq]:
                    ready_t[u.seq] = done
                ndep[u.seq] -= 1
                if ndep[u.seq] == 0:
                    heapq.heappush(waiting[u.eng], (ready_t[u.seq], u.seq))
        for e in ENG_NAMES:
            assert len(newq[e]) == len(self.ops[e])
            self.ops[e] = newq[e]
            for i, op in enumerate(newq[e]):
                op.idx = i
        self.est_total = max(free.values())

    def finalize(self):
        if self.reorder:
            self.list_schedule()
        for e in ENG_NAMES:
            for op in self.ops[e]:
                best = {}
                keep = set()
                for d in op.deps:
                    if d.dma:
                        keep.add(d)
                    else:
                        if d.eng == op.eng and not op.dma and d.eng not in SAME_ENG_SYNC:
                            continue
                        b = best.get(d.eng)
                        if b is None or d.idx > b.idx:
                            best[d.eng] = d
                keep.update(best.values())
                op.deps = keep
                for d in keep:
                    d.sig = True
        for e in ("pe", "act", "dve", "pool"):
            c = 0
            for op in self.ops[e]:
                if op.dma:
                    continue
                if op.sig:
                    c += 1
                    op.tok_sem = self.esem[e]
                    op.tok_val = c

    def replay(self, e, eng):
        waited = {}
        for op in self.ops[e]:
            for d in sorted(op.deps, key=lambda d: (d.eng, d.idx)):
                key = id(d.tok_sem)
                if waited.get(key, 0) < d.tok_val:
                    eng.wait_ge(d.tok_sem, d.tok_val)
                    waited[key] = d.tok_val
            if op.fn is None:
                continue
            ins = op.fn(eng)
            if op.sig:
                if op.dma == "cc":
                    ins.then_inc(op.tok_sem)
                else:
                    ins.then_inc(op.tok_sem, 16 if op.dma else 1)

    def run(self):
        self.finalize()
        with self.nc.Block() as block:
            @block.tensor
            def _(eng):
                self.replay("pe", eng)

            @block.scalar
            def _(eng):
                self.replay("act", eng)

            @block.vector
            def _(eng):
                self.replay("dve", eng)

            @block.gpsimd
            def _(eng):
                self.replay("pool", eng)

            @block.sync
            def _(eng):
                self.replay("sp", eng)


class Alloc:
    def __init__(self, nc, base, limit):
        self.nc = nc
        self.off = base
        self.limit = limit
        self.n = 0

    def t(self, shape, dtype):
        esz = 2 if dtype == BF16 else 4
        per = esz
        for s in shape[1:]:
            per *= s
        per = (per + 63) // 64 * 64
        h = self.nc.alloc_sbuf_tensor_at("t%d_%d" % (self.n, self.off), list(shape), dtype, offset=self.off)
        self.n += 1
        self.off += per
        assert self.off <= self.limit, ("sbuf overflow", self.off, self.limit)
        return h


class _Stop(Exception):
    pass


def build_nc(nlayers=DEPTH, dbg=False, stop=None):
    def chk(name):
        if stop == name:
            raise _Stop()
    nc = bass.Bass("TRN2", target_bir_lowering=False)
    inp = lambda n, s, d=F32: nc.dram_tensor(n, list(s), d, kind="ExternalInput").ap()
    x_in = inp("x_in", [T_OWN, D])
    xh_in = inp("xh_in", [256, D])
    ctx_in = inp("ctx_in", [256, D])
    c_in = inp("c_in", [2, D])
    w_ada = inp("w_ada", [DEPTH, D, 3 * D]); b_ada = inp("b_ada", [DEPTH, 3 * D])
    w_in = inp("w_in", [DEPTH, D, 3840]); attn_sink = inp("attn_sink", [DEPTH, 8])
    conv_dw_w = inp("conv_dw_w", [DEPTH, 31, 512]); conv_dw_b = inp("conv_dw_b", [DEPTH, 512])
    conv_ln_g = inp("conv_ln_g", [DEPTH, 512]); conv_ln_b = inp("conv_ln_b", [DEPTH, 512])
    lru_conv_w = inp("lru_conv_w", [DEPTH, 2, 4, 512]); lru_conv_b = inp("lru_conv_b", [DEPTH, 2, 512])
    lru_w_a = inp("lru_w_a", [DEPTH, 2, 8, 64, 64]); lru_b_a = inp("lru_b_a", [DEPTH, 2, 512])
    lru_w_x = inp("lru_w_x", [DEPTH, 2, 8, 64, 64]); lru_b_x = inp("lru_b_x", [DEPTH, 2, 512])
    lru_lambda = inp("lru_lambda", [DEPTH, 2, 512])
    w_branch = inp("w_branch", [DEPTH, 3, 512, D]); w_gate = inp("w_gate", [DEPTH, D, 3 * D])
    b_gate = inp("b_gate", [DEPTH, 3 * D]); w_out = inp("w_out", [DEPTH, D, D])
    ln_g = inp("ln_g", [DEPTH, D]); ln_b = inp("ln_b", [DEPTH, D])
    cos_d = inp("cosT", [128, NT]); sin_d = inp("sinT", [128, NT])
    masks_d = inp("masks", [128, 512], BF16)
    perm_d = inp("perm", [128, 128], BF16)
    identb_d = inp("identb", [128, 128], BF16)
    identf_d = inp("identf", [128, 128])
    flags_d = inp("flags", [128, 8])
    sel_d = inp("sel", [2, 256])
    out = nc.dram_tensor("out", [T_OWN, D], F32, kind="ExternalOutput").ap()

    xs = nc.dram_tensor("xs", [NFULL, D], F32).ap()
    send = nc.dram_tensor("send", [256, D], F32).ap()
    gath = nc.dram_tensor("gath", [4 * 256, D], F32).ap()
    au = nc.dram_tensor("au", [16 * 128, 2048], F32).ap()
    csend = nc.dram_tensor("csend", [128, 16], F32).ap()
    cgath = nc.dram_tensor("cgath", [4 * 128, 16], F32).ap()
    RG = [[0, 1, 2, 3], [4, 5, 6, 7]]

    with ExitStack() as st:
        S = Sched(nc, st)
        A = S.add
        BASE = 16640
        LIM = 229376
        pa = Alloc(nc, BASE, LIM)
        hT = pa.t([128, KC, NT], BF16)
        Y_OFF = pa.off
        Y = pa.t([128, 12, NFULL], BF16)
        Y_END = pa.off
        identb = pa.t([128, 128], BF16); identf = pa.t([128, 128], F32); onesf = pa.t([128, 128], F32)
        onesb = pa.t([128, 128], BF16)
        perm = pa.t([128, 128], BF16); masks = pa.t([128, 4, 128], BF16)
        flags = pa.t([128, 8], F32); sel = pa.t([2, 256], F32)
        scT = pa.t([128, KC, 2], BF16); cT = pa.t([128, KC, 2], F32)
        grow = pa.t([2, D], F32)
        modc = pa.t([128, 16, 2], F32)
        pcols = pa.t([128, 256], F32)
        cA = pa.t([128, 8], F32)
        carry = pa.t([128, 8], F32)
        s0 = pa.t([128, 8], F32)
        ylru_ctx = pa.t([128, 4, 256], F32)
        WG = 256
        wb = [pa.t([128, KC, WG], BF16) for _ in range(2)]
        wk = pa.t([128, KC, 256], BF16)
        zeros = pa.t([128, 128], F32)
        jn_t = pa.t([128, 16], F32)
        bar_t = pa.t([128, 16], F32)
        csb = pa.t([128, 16], F32)
        cg = pa.t([128, 4, 16], F32)
        chain = pa.t([128, 2, 4, 4], F32)
        SCR = pa.off
        ps = [st.enter_context(nc.psum_tensor("ps%d" % i, [128, 512], F32)) for i in range(6)]
        psTs = [st.enter_context(nc.psum_tensor("psT%d" % i, [128, 1024], BF16)) for i in range(2)]
        bank = [0]
        dyn = {}

        def nb():
            bank[0] = (bank[0] + 1) % 5
            return bank[0]

        def barrier():
            A("pool", lambda e: e.memset(bar_t[:], 0.0), writes=["EPOCH"])

        bpool = {}

        def nbp(name, banks):
            i = bpool.get(name, 0)
            bpool[name] = i + 1
            return banks[i % len(banks)]

        def fence(keys):
            A("pe", lambda e: e.matmul(ps[5][:, 0:2], lhsT=onesb[:, 0:128], rhs=onesb[:, 0:2], start=True, stop=True),
              reads=list(keys) + ["onesb"], writes=list(keys))
        wbi = [0]

        for (t, d, k) in ((identb, identb_d, "identb"), (identf, identf_d, "identf"), (perm, perm_d, "perm"),
                          (flags, flags_d, "flags")):
            A("sp", lambda e, t=t, d=d: e.dma_start(out=t[:], in_=d[:, :]), writes=[k], dma=True)
        A("sp", lambda e: e.dma_start(out=masks[:], in_=masks_d.rearrange("p (m q) -> p m q", m=4)), writes=["masks"], dma=True)
        A("sp", lambda e: e.dma_start(out=sel[:], in_=sel_d[:, :]), writes=["sel"], dma=True)
        A("pool", lambda e: e.memset(onesf[:], 1.0), writes=["onesf"])
        A("pool", lambda e: e.memset(onesb[:], 1.0), writes=["onesb"])
        A("pool", lambda e: e.memset(zeros[:], 0.0), writes=["zeros"])

        for r_ in range(2):
            def f_cT(e, r_=r_):
                with nc.allow_non_contiguous_dma(reason="tiny one-off transpose load of c"):
                    return e.dma_start(out=cT[:, :, r_], in_=c_in[r_].rearrange("(kc p) -> p kc", p=128))
            A("sp", f_cT, writes=[("cT", r_)], dma=True)
        A("act", lambda e: e.activation(out=scT[:], in_=cT[:], func=AF.Silu), reads=[("cT", 0), ("cT", 1)], writes=["scT"])

        def load_w(src, c0, w):
            i = wbi[0] % 2
            wbi[0] += 1
            buf = wb[i]
            A("pool", lambda e: e.dma_start(out=buf[:, :, 0:w], in_=src[:, c0:c0 + w].rearrange("(kc p) c -> p kc c", p=128)),
              writes=[("wb", i)], dma=True, ne=True)
            return buf, ("wb", i)

        def proj_fm(buf, bkey, mc, n0, n1, extra_reads=(), pool=None):
            b = nb() if pool is None else nbp(*pool)

            def f(e):
                for kc in range(KC):
                    ins = e.matmul(ps[b][:, 0:n1 - n0], lhsT=buf[:, kc, mc * 128:(mc + 1) * 128], rhs=hT[:, kc, n0:n1],
                                   start=(kc == 0), stop=(kc == KC - 1))
                return ins
            A("pe", f, reads=[bkey, "hT"] + list(extra_reads), writes=[("ps", b)], c=0.1 + 8 * 0.27 * (n1 - n0) / 512.0, ne=True)
            return b

        def layer(l):
            last = (l == DEPTH - 1)
            full_r = FULL_R[:4] if last else FULL_R
            ntile_full = 16 if last else 18
            barrier()
            sa = Alloc(nc, SCR, LIM)
            modrows = sa.t([2, 2 * D], F32)
            brow = sa.t([2, 2 * D], F32)
            A("sp", lambda e: e.dma_start(out=brow[:], in_=b_ada[l:l + 1, 0:2 * D].partition_broadcast(2)), writes=["brow"], dma=True)
            for g in range(2 * D // WG):
                buf, bkey = load_w(w_ada[l], g * WG, WG)
                b = nb()

                def f(e, buf=buf, b=b):
                    for kc in range(KC):
                        ins = e.matmul(ps[b][0:2, 0:WG], lhsT=scT[:, kc, :], rhs=buf[:, kc, 0:WG], start=(kc == 0), stop=(kc == KC - 1))
                    return ins
                A("pe", f, reads=[bkey, "scT"], writes=[("ps", b)])
                A("dve", lambda e, b=b, g=g: e.tensor_tensor(out=modrows[:, g * WG:(g + 1) * WG], in0=ps[b][0:2, 0:WG],
                                                               in1=brow[:, g * WG:(g + 1) * WG], op=ALU.add),
                  reads=[("ps", b), "brow"], writes=[("modrows", g)])
            mr_all = [("modrows", g) for g in range(2 * D // WG)]
            b = nb()

            def f(e, b=b):
                for j in range(16):
                    ins = e.matmul(ps[b][:, 2 * j:2 * j + 2], lhsT=modrows[0:2, j * 128:(j + 1) * 128], rhs=identf[0:2, 0:2],
                                   start=True, stop=True)
                return ins
            A("pe", f, reads=mr_all + ["identf"], writes=[("ps", b)])
            fence([("ps", b)])
            A("dve", lambda e, b=b: e.tensor_copy(out=modc[:].rearrange("p a r -> p (a r)"), in_=ps[b][:, 0:32]),
              reads=[("ps", b)], writes=["modc"])
            A("dve", lambda e: e.tensor_scalar(out=modc[:, 8:16, :], in0=modc[:, 8:16, :], scalar1=1.0, scalar2=None, op0=ALU.add),
              reads=["modc"], writes=["modc"])
            prow = sa.t([128, 2, 128], F32)
            A("pool", lambda e: e.memset(prow[:], 0.0), writes=["prow"])
            plist = [
                (0, 0, conv_dw_w[l].rearrange("k (cc p) -> (k cc) p", p=128), 124),
                (0, 124, conv_dw_b[l].rearrange("(cc p) -> cc p", p=128), 4),
                (1, 0, conv_ln_g[l].rearrange("(cc p) -> cc p", p=128), 4),
                (1, 4, conv_ln_b[l].rearrange("(cc p) -> cc p", p=128), 4),
                (1, 8, lru_conv_w[l].rearrange("d k (cc p) -> (d k cc) p", p=128), 32),
                (1, 40, lru_conv_b[l].rearrange("d (cc p) -> (d cc) p", p=128), 8),
                (1, 48, lru_b_a[l].rearrange("d (cc p) -> (d cc) p", p=128), 8),
                (1, 56, lru_b_x[l].rearrange("d (cc p) -> (d cc) p", p=128), 8),
                (1, 64, lru_lambda[l].rearrange("d (cc p) -> (d cc) p", p=128), 8),
                (1, 72, b_gate[l].rearrange("(r p) -> r p", p=128), 24),
            ]
            for (s_, r0, src, n) in plist:
                A("sp", lambda e, s_=s_, r0=r0, src=src, n=n: e.dma_start(out=prow[r0:r0 + n, s_, :], in_=src),
                  reads=["prow"], writes=[("prow", s_, r0)], dma=True)
            b = nb()

            def f(e, b=b):
                for s_ in range(2):
                    ins = e.matmul(ps[b][:, s_ * 128:(s_ + 1) * 128], lhsT=prow[:, s_, :], rhs=identf[:], start=True, stop=True)
                return ins
            A("pe", f, reads=[("prow", s_, r0) for (s_, r0, _, _) in plist] + ["identf"], writes=[("ps", b)])
            fence([("ps", b)])
            A("dve", lambda e, b=b: e.tensor_copy(out=pcols[:], in_=ps[b][:, 0:256]), reads=[("ps", b)], writes=["pcols"])
            PB = 128
            A("act", lambda e: e.activation(out=cA[:], in_=pcols[:, PB + 64:PB + 72], func=AF.Exp, scale=-1.0), reads=["pcols"], writes=["cA"])
            A("act", lambda e: e.activation(out=cA[:], in_=cA[:], func=AF.Ln, bias=1.0, scale=1.0), reads=["cA"], writes=["cA"])
            A("dve", lambda e: e.tensor_scalar(out=cA[:], in0=cA[:], scalar1=-8.0, scalar2=None, op0=ALU.mult), reads=["cA"], writes=["cA"])

            if dbg and l == 0:
                dbg_mr = nc.dram_tensor("dbg_mr", [2, 3 * D], F32, kind="ExternalOutput").ap()
                dbg_mc = nc.dram_tensor("dbg_mc", [128, 32], F32, kind="ExternalOutput").ap()
                dbg_ct = nc.dram_tensor("dbg_ct", [128, 16], F32, kind="ExternalOutput").ap()
                dbg_pc = nc.dram_tensor("dbg_pc", [128, 256], F32, kind="ExternalOutput").ap()
                A("sp", lambda e: e.dma_start(out=dbg_mr[:, :], in_=modrows[:]), reads=mr_all, writes=["dbg_mr"], dma=True)
                A("sp", lambda e: e.dma_start(out=dbg_mc[:, :], in_=modc[:].rearrange("p a r -> p (a r)")), reads=["modc"], writes=["dbg_mc"], dma=True)
                A("sp", lambda e: e.dma_start(out=dbg_ct[:, :], in_=cT[:].rearrange("p a r -> p (a r)")), reads=["scT"], writes=["dbg_ct"], dma=True)
                A("sp", lambda e: e.dma_start(out=dbg_pc[:, :], in_=pcols[:]), reads=["pcols", "cA"], writes=["dbg_pc"], dma=True)
                A("sp", None, reads=["dbg_mr", "dbg_mc", "dbg_ct", "dbg_pc"])
            chk("adaln")
            ya = Alloc(nc, Y_OFF, Y_END)
            G = 5
            xt = [ya.t([128, D], F32) for _ in range(2 * G)]
            xn = [sa.t([128, D], BF16) for _ in range(2 * G)]
            stats = [sa.t([128, G, 2, 6], F32) for _ in range(2)]
            mv = [sa.t([128, G, 2], F32) for _ in range(2)]
            rstd = [sa.t([128, G], F32) for _ in range(2)]
            nbias = [sa.t([128, G], F32) for _ in range(2)]
            pTi = [0]
            act_tiles = []
            for gi in range(4):
                s_ = gi % 2
                tiles = list(range(gi * G, gi * G + G))
                for j, ti in enumerate(tiles):
                    bi = s_ * G + j
                    if ti < 16:
                        src = (x_in if l == 0 else xs)[ti * 128:(ti + 1) * 128, :]
                        rk = [("xs", ti)]
                    elif ti < 18:
                        src = (ctx_in[(ti - 16) * 128:(ti - 15) * 128, :] if l == 0 else xs[ti * 128:(ti + 1) * 128, :])
                        rk = [("xs", ti)]
                    else:
                        rk = ["gath"]
                        src = xh_in[(ti - 18) * 128:(ti - 17) * 128, :] if l == 0 else None
                    if src is not None:
                        A("sp", lambda e, bi=bi, src=src: e.dma_start(out=xt[bi][:], in_=src), reads=rk, writes=[("xt", bi)], dma=True)
                    else:
                        def f(e, bi=bi, ti=ti):
                            if "L" not in dyn:
                                pid = e.partition_id()
                                dyn["L"] = ((pid + 3) % 4) * 256 + 128
                                dyn["R"] = ((pid + 1) % 4) * 256
                            row = dyn["L"] if ti == 18 else dyn["R"]
                            return e.dma_start(out=xt[bi][:], in_=gath[bass.ds(row, 128), :])
                        A("sp", f, reads=rk, writes=[("xt", bi)], dma=True)
                for j in range(G):
                    bi = s_ * G + j
                    for h_ in range(2):
                        A("dve", lambda e, bi=bi, j=j, h_=h_, s_=s_: e.bn_stats(out=stats[s_][:, j, h_, :], in_=xt[bi][:, h_ * 512:(h_ + 1) * 512]),
                          reads=[("xt", bi)], writes=[("st", s_, j, h_)])
                for j in range(G):
                    A("dve", lambda e, j=j, s_=s_: e.bn_aggr(out=mv[s_][:, j, :], in_=stats[s_][:, j, :, :]),
                      reads=[("st", s_, j, 0), ("st", s_, j, 1)], writes=[("mv", s_, j)])
                mvk = [("mv", s_, j) for j in range(G)]
                A("dve", lambda e, s_=s_: e.tensor_scalar(out=rstd[s_][:], in0=mv[s_][:, :, 1], scalar1=EPS, scalar2=None, op0=ALU.add),
                  reads=mvk, writes=[("rstd", s_)])
                A("act", lambda e, s_=s_: e.activation(out=rstd[s_][:], in_=rstd[s_][:], func=AF.Sqrt), reads=[("rstd", s_)], writes=[("rstd", s_)])
                A("dve", lambda e, s_=s_: e.reciprocal(out=rstd[s_][:], in_=rstd[s_][:]), reads=[("rstd", s_)], writes=[("rstd", s_)])
                A("dve", lambda e, s_=s_: e.scalar_tensor_tensor(out=nbias[s_][:], in0=mv[s_][:, :, 0], scalar=-1.0, in1=rstd[s_][:], op0=ALU.mult, op1=ALU.mult),
                  reads=mvk + [("rstd", s_)], writes=[("nbias", s_)], c=0.2)
                for j in range(G):
                    bi = s_ * G + j
                    A("act", lambda e, bi=bi, j=j, s_=s_: e.activation(out=xn[bi][:], in_=xt[bi][:], func=AF.Identity, scale=rstd[s_][:, j:j + 1], bias=nbias[s_][:, j:j + 1]),
                      reads=[("xt", bi), ("nbias", s_), ("rstd", s_)], writes=[("xn", bi)], c=1.0)
                for j, ti in enumerate(tiles):
                    bi = s_ * G + j
                    pTi[0] += 1
                    pi = pTi[0] % 2
                    psT = psTs[pi]

                    def f(e, bi=bi, psT=psT):
                        for kc in range(KC):
                            ins = e.transpose(out=psT[:, kc * 128:(kc + 1) * 128], in_=xn[bi][:, kc * 128:(kc + 1) * 128], identity=identb[:])
                        return ins
                    A("pe", f, reads=[("xn", bi), "identb"], writes=[("psT", pi)], c=0.9)
                    r = 1 if 16 <= ti < 18 else 0

                    if pi == 0:
                        def f(e, ti=ti, r=r, psT=psT):
                            for kc in range(KC):
                                ins = e.activation(out=hT[:, kc, ti * 128:(ti + 1) * 128], in_=psT[:, kc * 128:(kc + 1) * 128], func=AF.Identity,
                                                   scale=modc[:, 8 + kc, r:r + 1], bias=modc[:, kc, r:r + 1])
                            return ins
                        A("act", f, reads=[("psT", pi), "modc"], writes=[("hTe", ti)], c=2.3)
                        act_tiles.append(ti)
                    else:
                        def f(e, ti=ti, r=r, psT=psT):
                            for kc in range(KC):
                                ins = e.tensor_scalar(out=hT[:, kc, ti * 128:(ti + 1) * 128], in0=psT[:, kc * 128:(kc + 1) * 128],
                                                      scalar1=modc[:, 8 + kc, r:r + 1], scalar2=modc[:, kc, r:r + 1], op0=ALU.mult, op1=ALU.add)
                            return ins
                        A("dve", f, reads=[("psT", pi), "modc"], writes=["hT"], c=1.7)

            A("dve", lambda e: e.memset(jn_t[:], 0.0), reads=[("hTe", t_) for t_ in act_tiles], writes=["hT"], c=0.1)
            A("sp", lambda e: e.dma_start(out=grow[:], in_=b_ada[l:l + 1, 2 * D:3 * D].partition_broadcast(2)), writes=["grow"], dma=True, ne=True)
            for g in range(2 * D // WG, 3 * D // WG):
                buf, bkey = load_w(w_ada[l], g * WG, WG)
                b = nb()

                def f(e, buf=buf, b=b):
                    for kc in range(KC):
                        ins = e.matmul(ps[b][0:2, 0:WG], lhsT=scT[:, kc, :], rhs=buf[:, kc, 0:WG], start=(kc == 0), stop=(kc == KC - 1))
                    return ins
                A("pe", f, reads=[bkey, "scT"], writes=[("ps", b)], ne=True, c=1.2)
                gs_ = (g - 2 * D // WG) * WG
                A("dve", lambda e, b=b, gs_=gs_: e.tensor_tensor(out=grow[:, gs_:gs_ + WG], in0=ps[b][0:2, 0:WG], in1=grow[:, gs_:gs_ + WG], op=ALU.add),
                  reads=[("ps", b), "grow"], writes=["grow"], ne=True, c=0.3)
            A("act", lambda e: e.activation(out=grow[:], in_=grow[:], func=AF.Copy, scale=1.0 / ALPHA), reads=["grow"], writes=["grow"], ne=True, c=0.5)
            chk("ln1")
            barrier()
            sa = Alloc(nc, SCR, LIM)
            ya = Alloc(nc, Y_OFF, Y_END)
            GW = 2364
            cv = sa.t([128, 4, NFULL], F32)
            glu = [sa.t([128, GW], BF16) for _ in range(2)]
            Dg = sa.t([128, 31, 128], BF16)
            sg_t = [sa.t([128, 512], F32) for _ in range(2)]
            conv1_end = sa.off
            XLW = 2316
            NO = 2310
            xl_pads = [sa.t([128, XLW], BF16) for _ in range(2)]
            al = [ya, sa]
            DgL = [al[d].t([128, 4, 128], BF16) for d in range(2)]
            xc = [al[d].t([128, NO + 2], F32) for d in range(2)]
            xcb = [al[d].t([128, NO + 2], BF16) for d in range(2)]
            rg = [ya.t([128, NO + 2], F32)] * 2
            ig = [ya.t([128, NO + 2], F32)] * 2
            tq = [ya.t([128, NO + 2], F32)] * 2
            hs = [ya.t([128, 2048], F32)] * 2
            hc = [ya.t([128, 256], F32)] * 2
            wbd = [[al[d].t([128, 128], BF16) for _ in range(2)] for d in range(2)]
            sumr = [ya.t([128, 1], F32)] * 2
            LB = ("lru", (3, 4))
            CB = ("cv1", (0, 1, 2))
            for i in range(2):
                A("pool", lambda e, i=i: e.memset(glu[i][:], 0.0), writes=[("glu", i)], c=2.0)
            sic = [0]
            CO_R = [(0, 512, 0), (512, 1024, 512), (1024, 1536, 1024), (1536, 2048, 1536), (2078, 2334, 2048)]

            def conv1(cc):
                gi = cc % 2
                bufa, ka = load_w(w_in[l], C_CA + cc * 128, 128)
                bufg, kg = load_w(w_in[l], C_CG + cc * 128, 128)
                ranges = [(n0, n1, (15 + n0 if n0 < CTX0 else 2093), None) for (n0, n1) in FULL_R]
                ranges.append((LH0 + 113, LH0 + 128, 0, 0))
                ranges.append((RH0, RH0 + 15, 2063, 1))
                for (n0, n1, p0, fl) in ranges:
                    n = n1 - n0
                    bg = proj_fm(bufg, kg, 0, n0, n1, pool=CB)
                    sic[0] += 1
                    si = sic[0] % 2
                    A("act", lambda e, bg=bg, si=si, n=n: e.activation(out=sg_t[si][:, 0:n], in_=ps[bg][:, 0:n], func=AF.Sigmoid),
                      reads=[("ps", bg)], writes=[("sg_t", si)])
                    ba = proj_fm(bufa, ka, 0, n0, n1, pool=CB)
                    if fl is None:
                        A("dve", lambda e, ba=ba, si=si, n=n, p0=p0, gi=gi: e.tensor_tensor(out=glu[gi][:, p0:p0 + n], in0=ps[ba][:, 0:n], in1=sg_t[si][:, 0:n], op=ALU.mult),
                          reads=[("ps", ba), ("sg_t", si)], writes=[("glu", gi)])
                    else:
                        A("dve", lambda e, ba=ba, si=si, n=n, p0=p0, gi=gi, fl=fl: e.scalar_tensor_tensor(out=glu[gi][:, p0:p0 + n], in0=ps[ba][:, 0:n], scalar=flags[:, fl:fl + 1],
                                                                                                 in1=sg_t[si][:, 0:n], op0=ALU.mult, op1=ALU.mult),
                          reads=[("ps", ba), ("sg_t", si), "flags"], writes=[("glu", gi)])

                def f(e, cc=cc):
                    for k in range(31):
                        ins = e.tensor_scalar(out=Dg[:, k, :], in0=identb[:], scalar1=pcols[:, k * 4 + cc:k * 4 + cc + 1], scalar2=None, op0=ALU.mult)
                    return ins
                A("dve", f, reads=["identb", "pcols"], writes=["Dg"], c=4.0)
                for (o0, o1, t0) in CO_R:
                    b = nbp(*CB)

                    def f(e, b=b, o0=o0, o1=o1, gi=gi):
                        for k in range(31):
                            ins = e.matmul(ps[b][:, 0:o1 - o0], lhsT=Dg[:, k, :], rhs=glu[gi][:, o0 + k:o1 + k], start=(k == 0), stop=(k == 30))
                        return ins
                    A("pe", f, reads=["Dg", ("glu", gi)], writes=[("ps", b)], c=0.1 + 31 * 0.27 * (o1 - o0) / 512.0)
                    A("act", lambda e, b=b, o0=o0, o1=o1, t0=t0, cc=cc: e.activation(out=cv[:, cc, t0:t0 + o1 - o0], in_=ps[b][:, 0:o1 - o0], func=AF.Identity,
                                                                              bias=pcols[:, 124 + cc:125 + cc], scale=1.0),
                      reads=[("ps", b), "pcols"], writes=[("cv", cc)])

            for i_ in range(2):
                A("pool", lambda e, i_=i_: e.memset(xl_pads[i_][:], 0.0), writes=[("xl_pad", i_)], c=4.0)
            O_R = [(0, 512), (512, 1024), (1024, 1536), (1536, 2048), (2054, 2310)]
            for cc in range(4):
                xi = cc % 2
                xl_pad = xl_pads[xi]
                xk = ("xl_pad", xi)
                buf, bkey = load_w(w_in[l], C_XLRU + cc * 128, 128)
                for (n0, n1) in FULL_R:
                    b = proj_fm(buf, bkey, 0, n0, n1, pool=LB)
                    dst = xl_pad[:, 3 + n0:3 + n1] if n0 < CTX0 else xl_pad[:, 2057:2313]
                    A("act", lambda e, b=b, dst=dst, n=n1 - n0: e.activation(out=dst, in_=ps[b][:, 0:n], func=AF.Copy),
                      reads=[("ps", b)], writes=[xk])
                b = proj_fm(buf, bkey, 0, LH0 + 125, LH0 + 131, pool=LB)
                A("dve", lambda e, b=b, xl_pad=xl_pad: e.tensor_scalar(out=xl_pad[:, 0:3], in0=ps[b][:, 0:3], scalar1=flags[:, 0:1], scalar2=None, op0=ALU.mult),
                  reads=[("ps", b), "flags"], writes=[xk], c=0.2)
                A("dve", lambda e, b=b, xl_pad=xl_pad: e.tensor_scalar(out=xl_pad[:, 2051:2054], in0=ps[b][:, 3:6], scalar1=flags[:, 1:2], scalar2=None, op0=ALU.mult),
                  reads=[("ps", b), "flags"], writes=[xk], c=0.2)
                for d in range(2):
                    sh = 0 if d == 0 else 3
                    wcols = [pcols[:, PB + 8 + d * 16 + k * 4 + cc:PB + 9 + d * 16 + k * 4 + cc] for k in range(4)]
                    bcol = pcols[:, PB + 40 + d * 4 + cc:PB + 41 + d * 4 + cc]
                    xc_, xcb_, rg_, ig_, tq_, hs_, hc_, sumr_ = xc[d], xcb[d], rg[d], ig[d], tq[d], hs[d], hc[d], sumr[d]
                    kxc, kxcb, krg, kig, ktq, khs, khc, ksr = ("xc", d), ("xcb", d), ("rg", 0), ("ig", 0), ("tq", 0), ("hs", 0), ("hc", 0), ("sumr", 0)
                    dg_ = DgL[d]

                    def f(e, dg_=dg_, wcols=wcols):
                        for k in range(4):
                            ins = e.tensor_scalar(out=dg_[:, k, :], in0=identb[:], scalar1=wcols[k], scalar2=None, op0=ALU.mult)
                        return ins
                    A("dve", f, reads=["identb", "pcols"], writes=[("DgL", d)], c=0.6)
                    for (o0, o1) in O_R:
                        b = nbp(*LB)

                        def f(e, b=b, o0=o0, o1=o1, sh=sh, dg_=dg_, xl_pad=xl_pad):
                            for k in range(4):
                                ins = e.matmul(ps[b][:, 0:o1 - o0], lhsT=dg_[:, k, :], rhs=xl_pad[:, sh + o0 + k:sh + o1 + k], start=(k == 0), stop=(k == 3))
                            return ins
                        A("pe", f, reads=[("DgL", d), xk], writes=[("ps", b)], c=0.1 + 4 * 0.27 * (o1 - o0) / 512.0)
                        A("act", lambda e, b=b, o0=o0, o1=o1, xc_=xc_, bcol=bcol: e.activation(out=xc_[:, o0:o1], in_=ps[b][:, 0:o1 - o0], func=AF.Identity, bias=bcol, scale=1.0),
                          reads=[("ps", b), "pcols"], writes=[kxc], c=0.6)
                    A("dve", lambda e, xc_=xc_, xcb_=xcb_: e.tensor_copy(out=xcb_[:, 0:NO], in_=xc_[:, 0:NO]), reads=[kxc], writes=[kxcb], c=1.5)
                    for wi, (wsrc, dstg, kdst, bo) in enumerate(((lru_w_a, rg_, krg, 48), (lru_w_x, ig_, kig, 56))):
                        wt = wbd[d][wi]
                        A("pool", lambda e, wt=wt: e.memset(wt[:], 0.0), writes=[("wbd", d, wi), ("wbdd", d, wi, 0), ("wbdd", d, wi, 1)], c=0.3)
                        for hb in range(2):
                            A("pool", lambda e, wt=wt, hb=hb, wsrc=wsrc, d=d, cc=cc: e.dma_start(out=wt[hb * 64:(hb + 1) * 64, hb * 64:(hb + 1) * 64],
                                                                                          in_=wsrc[l, d, 2 * cc + hb, :, :]),
                              reads=[("wbd", d, wi)], writes=[("wbdd", d, wi, hb)], dma=True)
                        bias = pcols[:, PB + bo + d * 4 + cc:PB + bo + 1 + d * 4 + cc]
                        for (o0, o1) in O_R:
                            b = nbp(*LB)
                            A("pe", lambda e, b=b, wt=wt, o0=o0, o1=o1, xcb_=xcb_: e.matmul(ps[b][:, 0:o1 - o0], lhsT=wt[:], rhs=xcb_[:, o0:o1], start=True, stop=True),
                              reads=[("wbdd", d, wi, 0), ("wbdd", d, wi, 1), kxcb], writes=[("ps", b)], c=0.3)
                            A("act", lambda e, b=b, dstg=dstg, o0=o0, o1=o1, bias=bias: e.activation(out=dstg[:, o0:o1], in_=ps[b][:, 0:o1 - o0], func=AF.Sigmoid,
                                                                                                  bias=bias, scale=1.0),
                              reads=[("ps", b), "pcols"], writes=[kdst], c=0.6)
                    cAc = cA[:, d * 4 + cc:d * 4 + cc + 1]
                    A("dve", lambda e, rg_=rg_, sumr_=sumr_: e.reduce_sum(out=sumr_[:], in_=rg_[:, 0:2048], axis=mybir.AxisListType.X), reads=[krg], writes=[ksr], c=2.2)
                    A("act", lambda e, cAc=cAc, d=d, cc=cc, sumr_=sumr_: e.activation(out=csb[:, d * 8 + cc * 2:d * 8 + cc * 2 + 1], in_=sumr_[:], func=AF.Exp, scale=cAc),
                      reads=[ksr, "cA"], writes=[("csb", d, cc, 0)], c=0.2)
                    A("act", lambda e, cAc=cAc, rg_=rg_: e.activation(out=rg_[:, 0:NO], in_=rg_[:, 0:NO], func=AF.Exp, scale=cAc), reads=[krg, "cA"], writes=[krg], c=2.0)
                    A("pool", lambda e, rg_=rg_, tq_=tq_: e.tensor_tensor(out=tq_[:, 0:NO], in0=rg_[:, 0:NO], in1=rg_[:, 0:NO], op=ALU.mult), reads=[krg], writes=[ktq], c=4.5)
                    A("act", lambda e, tq_=tq_: e.activation(out=tq_[:, 0:NO], in_=tq_[:, 0:NO], func=AF.Sqrt, scale=-1.0, bias=1.0), reads=[ktq], writes=[ktq], c=2.0)
                    A("pool", lambda e, ig_=ig_, xc_=xc_: e.tensor_tensor(out=ig_[:, 0:NO], in0=ig_[:, 0:NO], in1=xc_[:, 0:NO], op=ALU.mult), reads=[kig, kxc], writes=[kig], c=4.5)
                    A("dve", lambda e, ig_=ig_, tq_=tq_: e.tensor_tensor(out=ig_[:, 0:NO], in0=ig_[:, 0:NO], in1=tq_[:, 0:NO], op=ALU.mult), reads=[kig, ktq], writes=[kig], c=2.5)
                    if d == 0:
                        A("dve", lambda e, rg_=rg_, ig_=ig_, hc_=hc_: e.tensor_tensor_scan(out=hc_[:], data0=rg_[:, 2054:2310], data1=ig_[:, 2054:2310], initial=0.0,
                                                                 op0=ALU.mult, op1=ALU.add), reads=[krg, kig], writes=[khc], c=0.7)
                        A("dve", lambda e, rg_=rg_, ig_=ig_, hs_=hs_: e.tensor_tensor_scan(out=hs_[:], data0=rg_[:, 0:2048], data1=ig_[:, 0:2048], initial=0.0,
                                                                 op0=ALU.mult, op1=ALU.add), reads=[krg, kig], writes=[khs], c=4.4)
                        A("pool", lambda e, cc=cc, hc_=hc_: e.tensor_copy(out=ylru_ctx[:, cc, :], in_=hc_[:]), reads=[khc], writes=[("ylc", cc)], c=0.6)
                        A("act", lambda e, cc=cc, hc_=hc_: e.activation(out=s0[:, cc:cc + 1], in_=hc_[:, 255:256], func=AF.Copy), reads=[khc], writes=[("s0", 0, cc)], c=0.2)
                        A("act", lambda e, cc=cc, hs_=hs_: e.activation(out=csb[:, cc * 2 + 1:cc * 2 + 2], in_=hs_[:, 2047:2048], func=AF.Copy),
                          reads=[khs], writes=[("csb", 0, cc, 1)], c=0.2)
                    else:
                        A("dve", lambda e, rg_=rg_, ig_=ig_, hc_=hc_: e.tensor_tensor_scan(out=hc_[:, ::-1], data0=rg_[:, 2309:2053:-1], data1=ig_[:, 2309:2053:-1], initial=0.0,
                                                                 op0=ALU.mult, op1=ALU.add), reads=[krg, kig], writes=[khc], c=0.7)
                        A("dve", lambda e, rg_=rg_, ig_=ig_, hs_=hs_: e.tensor_tensor_scan(out=hs_[:, ::-1], data0=rg_[:, 2047::-1], data1=ig_[:, 2047::-1], initial=0.0,
                                                                 op0=ALU.mult, op1=ALU.add), reads=[krg, kig], writes=[khs], c=4.4)
                        A("pool", lambda e, cc=cc, hc_=hc_: e.tensor_tensor(out=ylru_ctx[:, cc, :], in0=ylru_ctx[:, cc, :], in1=hc_[:], op=ALU.add),
                          reads=[khc, ("ylc", cc)], writes=[("ylc", cc)], c=0.6)
                        A("act", lambda e, cc=cc, hc_=hc_: e.activation(out=s0[:, 4 + cc:5 + cc], in_=hc_[:, 0:1], func=AF.Copy), reads=[khc], writes=[("s0", 1, cc)], c=0.2)
                        A("act", lambda e, cc=cc, hs_=hs_: e.activation(out=csb[:, 8 + cc * 2 + 1:8 + cc * 2 + 2], in_=hs_[:, 0:1], func=AF.Copy),
                          reads=[khs], writes=[("csb", 1, cc, 1)], c=0.2)
                    cmb = d * 4 + cc
                    A("sp", lambda e, cmb=cmb, rg_=rg_: e.dma_start(out=au[(cmb * 2) * 128:(cmb * 2 + 1) * 128, :], in_=rg_[:, 0:2048]), reads=[krg], writes=[("au", cmb, 0)], dma=True, c=5.0)
                    A("sp", lambda e, cmb=cmb, ig_=ig_: e.dma_start(out=au[(cmb * 2 + 1) * 128:(cmb * 2 + 2) * 128, :], in_=ig_[:, 0:2048]), reads=[kig], writes=[("au", cmb, 1)], dma=True, c=5.0)
                conv1(cc)
            csb_keys = [("csb", d, cc, j) for d in range(2) for cc in range(4) for j in range(2)]
            A("sp", lambda e: e.dma_start(out=csend[:, :], in_=csb[:]), reads=csb_keys, writes=["csend"], dma=True, ne=True)
            A("pool", lambda e: e.collective_compute("AllGather", ALU.bypass, replica_groups=RG, ins=[csend.opt()], outs=[cgath.opt()]),
              reads=["csend"], writes=["cgath"], dma="cc", ne=True)
            A("sp", lambda e: e.dma_start(out=cg[:], in_=cgath.rearrange("(r p) f -> p r f", p=128)), reads=["cgath"], writes=["cg"], dma=True, ne=True)
            cgv = cg[:].rearrange("p r (d c j) -> p r d c j", d=2, c=4)
            A("dve", lambda e: e.tensor_copy(out=chain[:, 0, 0, :], in_=s0[:, 0:4]), reads=[("s0", 0, c_) for c_ in range(4)], writes=[("chain", 0, 0)], ne=True)
            for r in range(3):
                A("dve", lambda e, r=r: e.tensor_tensor(out=chain[:, 0, r + 1, :], in0=chain[:, 0, r, :], in1=cgv[:, r, 0, :, 0], op=ALU.mult),
                  reads=[("chain", 0, r), "cg"], writes=[("chain", 0, r + 1)], ne=True)
                A("dve", lambda e, r=r: e.tensor_tensor(out=chain[:, 0, r + 1, :], in0=chain[:, 0, r + 1, :], in1=cgv[:, r, 0, :, 1], op=ALU.add),
                  reads=[("chain", 0, r + 1), "cg"], writes=[("chain", 0, r + 1)], ne=True)
            A("dve", lambda e: e.tensor_copy(out=chain[:, 1, 3, :], in_=s0[:, 4:8]), reads=[("s0", 1, c_) for c_ in range(4)], writes=[("chain", 1, 3)], ne=True)
            for r in (3, 2, 1):
                A("dve", lambda e, r=r: e.tensor_tensor(out=chain[:, 1, r - 1, :], in0=chain[:, 1, r, :], in1=cgv[:, r, 1, :, 0], op=ALU.mult),
                  reads=[("chain", 1, r), "cg"], writes=[("chain", 1, r - 1)], ne=True)
                A("dve", lambda e, r=r: e.tensor_tensor(out=chain[:, 1, r - 1, :], in0=chain[:, 1, r - 1, :], in1=cgv[:, r, 1, :, 1], op=ALU.add),
                  reads=[("chain", 1, r - 1), "cg"], writes=[("chain", 1, r - 1)], ne=True)
            for d in range(2):
                ck = [("chain", d, r) for r in range(4)]
                A("dve", lambda e, d=d: e.tensor_scalar(out=carry[:, d * 4:d * 4 + 4], in0=chain[:, d, 0, :], scalar1=flags[:, 2:3], scalar2=None, op0=ALU.mult),
                  reads=ck + ["flags"], writes=[("carry", d)], ne=True)
                for r in range(1, 4):
                    A("dve", lambda e, d=d, r=r: e.scalar_tensor_tensor(out=carry[:, d * 4:d * 4 + 4], in0=chain[:, d, r, :], scalar=flags[:, 2 + r:3 + r],
                                                                    in1=carry[:, d * 4:d * 4 + 4], op0=ALU.mult, op1=ALU.add),
                      reads=ck + ["flags", ("carry", d)], writes=[("carry", d)], ne=True)

            chk("lru1")
            barrier()
            sa = Alloc(nc, conv1_end, LIM)
            sq = [sa.t([128, 512], F32) for _ in range(2)]
            mean_t = [sa.t([128, 512], F32) for _ in range(2)]
            msq = [sa.t([128, 512], F32) for _ in range(2)]
            var_t = [sa.t([128, 512], F32) for _ in range(2)]
            tt = [sa.t([128, 512], F32) for _ in range(4)]
            cvk = [("cv", cc) for cc in range(4)]
            for ri, (n0, n1) in enumerate(full_r):
                n = n1 - n0
                rp = ri % 2
                mean_r, msq_r, var_r = mean_t[rp], msq[rp], var_t[rp]
                kme, kms, kva = ("mean_t", rp), ("msq", rp), ("var_t", rp)
                b1 = nb()

                def f(e, b1=b1, n0=n0, n1=n1):
                    for cc in range(4):
                        ins = e.matmul(ps[b1][:, 0:n1 - n0], lhsT=onesf[:], rhs=cv[:, cc, n0:n1], start=(cc == 0), stop=(cc == 3))
                    return ins
                A("pe", f, reads=cvk + ["onesf"], writes=[("ps", b1)], c=4.5)
                b2 = nb()
                for cc in range(4):
                    qi = cc % 2
                    A("act", lambda e, qi=qi, cc=cc, n0=n0, n1=n1: e.activation(out=sq[qi][:, 0:n1 - n0], in_=cv[:, cc, n0:n1], func=AF.Square),
                      reads=[("cv", cc)], writes=[("sq", qi)])
                    A("pe", lambda e, qi=qi, cc=cc, b2=b2, n=n: e.matmul(ps[b2][:, 0:n], lhsT=onesf[:], rhs=sq[qi][:, 0:n], start=(cc == 0), stop=(cc == 3)),
                      reads=[("sq", qi), "onesf"], writes=[("ps", b2)], c=1.2)
                fence([("ps", b1), ("ps", b2)])
                A("act", lambda e, b1=b1, n=n, mean_r=mean_r: e.activation(out=mean_r[:, 0:n], in_=ps[b1][:, 0:n], func=AF.Copy, scale=1.0 / 512), reads=[("ps", b1)], writes=[kme])
                A("pool", lambda e, n=n, mean_r=mean_r, msq_r=msq_r: e.tensor_tensor(out=msq_r[:, 0:n], in0=mean_r[:, 0:n], in1=mean_r[:, 0:n], op=ALU.mult), reads=[kme], writes=[kms])
                A("dve", lambda e, b2=b2, n=n, var_r=var_r, msq_r=msq_r: e.scalar_tensor_tensor(out=var_r[:, 0:n], in0=ps[b2][:, 0:n], scalar=1.0 / 512, in1=msq_r[:, 0:n], op0=ALU.mult, op1=ALU.subtract),
                  reads=[("ps", b2), kms], writes=[kva])
                A("dve", lambda e, n=n, var_r=var_r: e.tensor_scalar(out=var_r[:, 0:n], in0=var_r[:, 0:n], scalar1=EPS, scalar2=None, op0=ALU.add), reads=[kva], writes=[kva])
                A("act", lambda e, n=n, var_r=var_r: e.activation(out=var_r[:, 0:n], in_=var_r[:, 0:n], func=AF.Sqrt), reads=[kva], writes=[kva])
                A("dve", lambda e, n=n, var_r=var_r: e.reciprocal(out=var_r[:, 0:n], in_=var_r[:, 0:n]), reads=[kva], writes=[kva])
                for cc in range(4):
                    ti_ = cc
                    A("dve", lambda e, ti_=ti_, cc=cc, n0=n0, n1=n1, n=n, mean_r=mean_r: e.tensor_tensor(out=tt[ti_][:, 0:n], in0=cv[:, cc, n0:n1], in1=mean_r[:, 0:n], op=ALU.subtract),
                      reads=[("cv", cc), kme], writes=[("tt", ti_)])
                    A("dve", lambda e, ti_=ti_, cc=cc, n=n, var_r=var_r: e.scalar_tensor_tensor(out=tt[ti_][:, 0:n], in0=tt[ti_][:, 0:n], scalar=pcols[:, PB + cc:PB + cc + 1], in1=var_r[:, 0:n],
                                                                               op0=ALU.mult, op1=ALU.mult),
                      reads=[("tt", ti_), kva, "pcols"], writes=[("tt", ti_)])
                    A("act", lambda e, ti_=ti_, cc=cc, n0=n0, n1=n1, n=n: e.activation(out=Y[:, 4 + cc, n0:n1], in_=tt[ti_][:, 0:n], func=AF.Silu, bias=pcols[:, PB + 4 + cc:PB + 5 + cc], scale=1.0),
                      reads=[("tt", ti_), "pcols"], writes=[("Y", 4 + cc)])
            gt = [sa.t([128, 512], F32) for _ in range(2)]

            def gate_branch(bidx, col0):
                for cc in range(4):
                    buf, bkey = load_w(w_in[l], col0 + cc * 128, 128)
                    for (n0, n1) in full_r:
                        n = n1 - n0
                        b = proj_fm(buf, bkey, 0, n0, n1)
                        gi_ = b % 2
                        A("act", lambda e, b=b, gi_=gi_, n=n: e.activation(out=gt[gi_][:, 0:n], in_=ps[b][:, 0:n], func=AF.Silu), reads=[("ps", b)], writes=[("gt", gi_)])
                        A("dve", lambda e, gi_=gi_, n0=n0, n1=n1, n=n, cc=cc: e.tensor_tensor(out=Y[:, bidx * 4 + cc, n0:n1], in0=Y[:, bidx * 4 + cc, n0:n1], in1=gt[gi_][:, 0:n], op=ALU.mult),
                          reads=[("gt", gi_), ("Y", bidx * 4 + cc)], writes=[("Y", bidx * 4 + cc)])
            gate_branch(1, C_GCONV)

            chk("conv")
            barrier()
            sa = Alloc(nc, SCR, LIM)
            gt = [sa.t([128, 512], F32) for _ in range(2)]
            qT = sa.t([128, 4, NFULL], BF16)
            kT = sa.t([128, 2, NT], BF16)
            vS = sa.t([128, 20, 2, 128], BF16)
            cs_off = [sa.off, sa.off + NT * 4]
            cosT = sa.t([128, NT], F32); sinT = sa.t([128, NT], F32)
            E = [nc.alloc_sbuf_tensor_at("E%d_%d" % (i_, l), [128, 5, 2, 2, 2, 128], BF16, offset=cs_off[i_]) for i_ in range(2)]
            ekeys_all = [[("E", i_, j0, hp, g) for j0 in (0, 2, 4) for hp in range(2) for g in range(2)] + [("Em", i_, j) for j in range(5)] for i_ in range(2)]
            t1 = [sa.t([128, 512], F32) for _ in range(3)]
            qb = [sa.t([128, 512], BF16) for _ in range(3)]
            rpi = [0]
            den = sa.t([128, 512], F32)
            sinkbc = sa.t([128, 4, 128], F32)
            sk = sa.t([128, 8], F32)
            A("sp", lambda e: e.dma_start(out=cosT[:], in_=cos_d[:, :]), writes=["cosT"] + ekeys_all[0], dma=True)
            A("sp", lambda e: e.dma_start(out=sinT[:], in_=sin_d[:, :]), writes=["sinT"] + ekeys_all[1], dma=True)
            A("sp", lambda e: e.dma_start(out=sk[:], in_=attn_sink[l:l + 1, :].partition_broadcast(128)), writes=["sk"], dma=True)
            A("act", lambda e: e.activation(out=sk[:], in_=sk[:], func=AF.Exp), reads=["sk"], writes=["sk"])

            def f(e):
                for c_ in range(4):
                    for hp in range(2):
                        ins = e.tensor_scalar(out=sinkbc[hp * 64:(hp + 1) * 64, c_, :], in0=zeros[hp * 64:(hp + 1) * 64, :],
                                              scalar1=sk[hp * 64:(hp + 1) * 64, 2 * c_ + hp:2 * c_ + hp + 1], scalar2=None, op0=ALU.add)
                return ins
            A("dve", f, reads=["sk", "zeros"], writes=["sinkbc"])
            A("pool", lambda e: e.memset(vS[:], 0.0), writes=["vS"])

            def rope(b, dst, n0, n1):
                n = n1 - n0
                rpi[0] += 1
                i = rpi[0] % 3
                A("act", lambda e: e.activation(out=qb[i][:, 0:n], in_=ps[b][:, 0:n], func=AF.Copy), reads=[("ps", b)], writes=[("qb", i)])
                A("dve", lambda e: e.tensor_tensor(out=t1[i][:, 0:n], in0=ps[b][:, 0:n], in1=cosT[:, n0:n1], op=ALU.mult), reads=[("ps", b), "cosT", ("qb", i)], writes=[("t1", i)])
                b2 = nb()
                A("pe", lambda e: e.matmul(ps[b2][:, 0:n], lhsT=perm[:], rhs=qb[i][:, 0:n], start=True, stop=True), reads=[("qb", i), "perm"], writes=[("ps", b2)])
                A("dve", lambda e: e.tensor_tensor(out=qb[i][:, 0:n], in0=ps[b2][:, 0:n], in1=sinT[:, n0:n1], op=ALU.mult), reads=[("ps", b2), "sinT"], writes=[("qb", i)])
                A("pool", lambda e: e.tensor_tensor(out=dst, in0=qb[i][:, 0:n], in1=t1[i][:, 0:n], op=ALU.add), reads=[("qb", i), ("t1", i)], writes=["qkT"])
            for c_ in range(4):
                buf, bkey = load_w(w_in[l], C_Q + c_ * 128, 128)
                for (n0, n1) in full_r:
                    b = proj_fm(buf, bkey, 0, n0, n1)
                    rope(b, qT[:, c_, n0:n1], n0, n1)
            for g in range(2):
                for dup in range(2):
                    A("pool", lambda e, g=g, dup=dup: e.dma_start(out=wk[:, :, g * 128 + dup * 64:g * 128 + dup * 64 + 64],
                                                                in_=w_in[l][:, C_K + g * 64:C_K + g * 64 + 64].rearrange("(kc p) c -> p kc c", p=128)),
                      writes=[("wk", g, dup)], dma=True)
            for g in range(2):
                for (n0, n1) in FULL_R + [(LH0, NT)]:
                    b = nb()

                    def f(e, b=b, g=g, n0=n0, n1=n1):
                        for kc in range(KC):
                            ins = e.matmul(ps[b][:, 0:n1 - n0], lhsT=wk[:, kc, g * 128:(g + 1) * 128], rhs=hT[:, kc, n0:n1], start=(kc == 0), stop=(kc == KC - 1))
                        return ins
                    A("pe", f, reads=[("wk", g, 0), ("wk", g, 1), "hT"], writes=[("ps", b)])
                    rope(b, kT[:, g, n0:n1], n0, n1)
            bufv, kv = load_w(w_in[l], C_V, 128)
            for ti in range(20):
                b = nb()

                def f(e, b=b, ti=ti):
                    for kc in range(KC):
                        ins = e.matmul(ps[b][:, 0:128], lhsT=hT[:, kc, ti * 128:(ti + 1) * 128], rhs=bufv[:, kc, 0:128], start=(kc == 0), stop=(kc == KC - 1))
                    return ins
                A("pe", f, reads=[kv, "hT"], writes=[("ps", b)])
                A("act", lambda e, b=b, ti=ti: e.activation(out=vS[:, ti, :, 0:64], in_=ps[b][:, 0:128].rearrange("p (g d) -> p g d", g=2), func=AF.Copy),
                  reads=[("ps", b), "vS"], writes=[("vS", ti)])
            vO = sa.t([128, 20, 2, 128], BF16)
            A("pool", lambda e: e.memset(vO[:], 0.0), writes=["vO"])
            for ti in range(20):
                A("pool", lambda e, ti=ti: e.tensor_copy(out=vO[:, ti, :, 64:128], in_=vS[:, ti, :, 0:64]), reads=[("vS", ti), "vO"], writes=[("vO", ti)])
            onesE = sa.t([128, 2, 128], BF16)
            A("pool", lambda e: e.memset(onesE[:], 0.0), writes=["onesE0"])
            A("pool", lambda e: e.memset(onesE[:, 0, 0:64], 1.0), reads=["onesE0"], writes=["onesE1"])
            A("pool", lambda e: e.memset(onesE[:, 1, 64:128], 1.0), reads=["onesE0"], writes=["onesE2"])
            onesEk = ["onesE1", "onesE2"]
            sbk = [0]
            for qt in range(ntile_full):
                ei = qt % 2
                if qt < 16:
                    kl = [(LH0 // 128 if qt == 0 else qt - 1, 2 if qt == 0 else 0), (qt, None),
                          (RH0 // 128 if qt == 15 else qt + 1, 3 if qt == 15 else 1), (16, None), (17, None)]
                else:
                    kl = [(16, None), (17, None)]
                q0 = qt * 128
                for g in range(2):
                    for hp in range(2):
                        for j0 in range(0, len(kl), 2):
                            js = list(range(j0, min(j0 + 2, len(kl))))
                            sbk[0] = (sbk[0] + 1) % 3
                            b = sbk[0]

                            def f(e, b=b, js=js, g=g, hp=hp, kl=kl, q0=q0):
                                for jj, j in enumerate(js):
                                    kt = kl[j][0]
                                    ins = e.matmul(ps[b][:, jj * 256:(jj + 1) * 256], lhsT=kT[hp * 64:(hp + 1) * 64, g, kt * 128:(kt + 1) * 128],
                                                   rhs=qT[hp * 64:(hp + 1) * 64, 2 * g:2 * g + 2, q0:q0 + 128], start=True, stop=True)
                                return ins
                            A("pe", f, reads=["qkT"], writes=[("ps", b)], c=0.1 + 0.14 * len(js))
                            A("act", lambda e, b=b, js=js, g=g, hp=hp, ei=ei, j0=j0: e.activation(
                                out=E[ei][:, j0:j0 + len(js), hp, g, :, :], in_=ps[b][:, 0:256 * len(js)].rearrange("p (j c q) -> p j c q", j=len(js), c=2),
                                func=AF.Exp, scale=0.125), reads=[("ps", b)], writes=[("E", ei, j0, hp, g)])
                ek = [("E", ei, j0, hp, g) for j0 in (0, 2, 4) for hp in range(2) for g in range(2)]
                for j, (kt, mi) in enumerate(kl):
                    if mi is not None:
                        A("pool", lambda e, ei=ei, j=j, mi=mi: e.tensor_tensor(out=E[ei][:, j].rearrange("p a b c q -> p (a b c) q"),
                                                                              in0=E[ei][:, j].rearrange("p a b c q -> p (a b c) q"),
                                                                              in1=masks[:, mi:mi + 1, :].to_broadcast([128, 8, 128]), op=ALU.mult),
                          reads=ek + ["masks"], writes=[("Em", ei, j)], c=1.6)
                emk = [("Em", ei, j) for j in range(5)]
                bn_ = 3
                bd_ = 4

                def f(e, ei=ei, kl=kl, bn_=bn_, bd_=bd_):
                    cnt = len(kl) * 2
                    for g in range(2):
                        i = 0
                        for j, (kt, mi) in enumerate(kl):
                            for hp in range(2):
                                vv = vS if hp == 0 else vO
                                e.matmul(ps[bn_][:, g * 256:(g + 1) * 256], lhsT=vv[:, kt, g, :], rhs=E[ei][:, j, hp, g, :, :],
                                         start=(i == 0), stop=(i == cnt - 1))
                                i += 1
                    for g in range(2):
                        i = 0
                        for j, (kt, mi) in enumerate(kl):
                            for hp in range(2):
                                ins = e.matmul(ps[bd_][:, g * 256:(g + 1) * 256], lhsT=onesE[:, hp, :], rhs=E[ei][:, j, hp, g, :, :],
                                               start=(i == 0), stop=(i == cnt - 1))
                                i += 1
                    return ins
                A("pe", f, reads=ek + emk + [("vS", kt) for kt, _ in kl] + [("vO", kt) for kt, _ in kl] + onesEk, writes=[("ps", bn_), ("ps", bd_)], c=0.1 + 0.14 * 8 * len(kl))
                A("dve", lambda e, bd_=bd_: e.tensor_tensor(out=den[:], in0=ps[bd_][:], in1=sinkbc[:].rearrange("p c q -> p (c q)"), op=ALU.add),
                  reads=[("ps", bd_), "sinkbc"], writes=["den"])
                A("dve", lambda e: e.reciprocal(out=den[:], in_=den[:]), reads=["den"], writes=["den"])
                A("dve", lambda e, bn_=bn_, q0=q0: e.tensor_tensor(out=Y[:, 0:4, q0:q0 + 128], in0=ps[bn_][:].rearrange("p (c q) -> p c q", c=4),
                                                                in1=den[:].rearrange("p (c q) -> p c q", c=4), op=ALU.mult),
                  reads=[("ps", bn_), "den"], writes=[("Y", c_) for c_ in range(4)])
            gate_branch(0, C_GATT)

            chk("att")
            barrier()
            sa = Alloc(nc, SCR, LIM)
            gt = [sa.t([128, 512], F32) for _ in range(2)]
            a2 = [sa.t([128, 2048], F32) for _ in range(2)]
            u2 = [sa.t([128, 2048], F32) for _ in range(2)]
            h2 = [sa.t([128, 2048], F32) for _ in range(2)]
            for cc in range(4):
                for d in range(2):
                    cmb = d * 4 + cc
                    A("sp", lambda e, cmb=cmb, d=d: e.dma_start(out=a2[d][:], in_=au[(cmb * 2) * 128:(cmb * 2 + 1) * 128, :]), reads=[("au", cmb, 0)], writes=[("a2", d)], dma=True)
                    A("sp", lambda e, cmb=cmb, d=d: e.dma_start(out=u2[d][:], in_=au[(cmb * 2 + 1) * 128:(cmb * 2 + 2) * 128, :]), reads=[("au", cmb, 1)], writes=[("u2", d)], dma=True)
                A("dve", lambda e, cc=cc: e.tensor_tensor_scan(out=h2[0][:], data0=a2[0][:], data1=u2[0][:], initial=carry[:, cc:cc + 1], op0=ALU.mult, op1=ALU.add),
                  reads=[("a2", 0), ("u2", 0), ("carry", 0)], writes=[("h2", 0)])
                A("dve", lambda e, cc=cc: e.tensor_tensor_scan(out=h2[1][:, ::-1], data0=a2[1][:, ::-1], data1=u2[1][:, ::-1], initial=carry[:, 4 + cc:5 + cc],
                                                             op0=ALU.mult, op1=ALU.add),
                  reads=[("a2", 1), ("u2", 1), ("carry", 1)], writes=[("h2", 1)])
                A("pool", lambda e, cc=cc: e.tensor_tensor(out=Y[:, 8 + cc, 0:2048], in0=h2[0][:], in1=h2[1][:], op=ALU.add), reads=[("h2", 0), ("h2", 1)], writes=[("Y", 8 + cc)])
                if not last:
                    A("pool", lambda e, cc=cc: e.tensor_copy(out=Y[:, 8 + cc, 2048:2304], in_=ylru_ctx[:, cc, :]), reads=[("ylc", cc)], writes=[("Y", 8 + cc)])
            gate_branch(2, C_GLRU)

            chk("lru2")
            barrier()
            sa = Alloc(nc, SCR, LIM)
            mT = sa.t([128, KC, NFULL], BF16)
            wout = sa.t([128, KC, D], BF16)
            mrg_off = sa.off
            wg = [sa.t([128, KC, 3, 128], BF16) for _ in range(2)]
            wbr = [sa.t([128, 3, 4, 128], BF16) for _ in range(2)]
            Gt = [sa.t([128, 512], F32) for _ in range(2)]
            acc = sa.t([128, 512], F32)
            tmp = sa.t([128, 512], F32)
            for kc in range(KC):
                A("pool", lambda e, kc=kc: e.dma_start(out=wout[:, kc, :], in_=w_out[l][kc * 128:(kc + 1) * 128, :]), writes=[("wout", kc)], dma=True)
            for dc in range(8):
                wi = dc % 2
                for b_ in range(3):
                    A("pool", lambda e, wi=wi, dc=dc, b_=b_: e.dma_start(out=wg[wi][:, :, b_, :],
                                                                     in_=w_gate[l][:, b_ * D + dc * 128:b_ * D + (dc + 1) * 128].rearrange("(kc p) c -> p kc c", p=128)),
                      writes=[("wg", wi, b_)], dma=True)
                    A("pool", lambda e, wi=wi, dc=dc, b_=b_: e.dma_start(out=wbr[wi][:, b_, :, :],
                                                                     in_=w_branch[l, b_][:, dc * 128:(dc + 1) * 128].rearrange("(kc p) c -> p kc c", p=128)),
                      writes=[("wbr", wi, b_)], dma=True)
                for (n0, n1) in full_r:
                    n = n1 - n0
                    for b_ in range(3):
                        bg = nb()

                        def f(e, bg=bg, wi=wi, b_=b_, n0=n0, n1=n1):
                            for kc in range(KC):
                                ins = e.matmul(ps[bg][:, 0:n1 - n0], lhsT=wg[wi][:, kc, b_, :], rhs=hT[:, kc, n0:n1], start=(kc == 0), stop=(kc == KC - 1))
                            return ins
                        A("pe", f, reads=[("wg", wi, b_), "hT"], writes=[("ps", bg)], c=0.1 + 8 * 0.27 * n / 512.0)
                        gi_ = bg % 2
                        A("act", lambda e, bg=bg, gi_=gi_, n=n, b_=b_, dc=dc: e.activation(out=Gt[gi_][:, 0:n], in_=ps[bg][:, 0:n], func=AF.Sigmoid,
                                                                                     bias=pcols[:, PB + 72 + b_ * 8 + dc:PB + 73 + b_ * 8 + dc], scale=1.0),
                          reads=[("ps", bg), "pcols"], writes=[("Gt", gi_)])
                        bp = nb()

                        def f(e, bp=bp, wi=wi, b_=b_, n0=n0, n1=n1):
                            for kc in range(4):
                                ins = e.matmul(ps[bp][:, 0:n1 - n0], lhsT=wbr[wi][:, b_, kc, :], rhs=Y[:, b_ * 4 + kc, n0:n1], start=(kc == 0), stop=(kc == 3))
                            return ins
                        A("pe", f, reads=[("wbr", wi, b_)] + [("Y", b_ * 4 + kc) for kc in range(4)], writes=[("ps", bp)], c=0.1 + 4 * 0.27 * n / 512.0)
                        if b_ == 0:
                            A("dve", lambda e, bp=bp, gi_=gi_, n=n: e.tensor_tensor(out=acc[:, 0:n], in0=ps[bp][:, 0:n], in1=Gt[gi_][:, 0:n], op=ALU.mult),
                              reads=[("ps", bp), ("Gt", gi_)], writes=["acc"])
                        else:
                            A("dve", lambda e, bp=bp, gi_=gi_, n=n: e.tensor_tensor(out=tmp[:, 0:n], in0=ps[bp][:, 0:n], in1=Gt[gi_][:, 0:n], op=ALU.mult),
                              reads=[("ps", bp), ("Gt", gi_)], writes=["tmp"])
                            if b_ == 1:
                                A("pool", lambda e, n=n: e.tensor_tensor(out=acc[:, 0:n], in0=acc[:, 0:n], in1=tmp[:, 0:n], op=ALU.add), reads=["acc", "tmp"], writes=["acc"])
                            else:
                                A("pool", lambda e, n=n, dc=dc, n0=n0, n1=n1: e.tensor_tensor(out=mT[:, dc, n0:n1], in0=acc[:, 0:n], in1=tmp[:, 0:n], op=ALU.add),
                                  reads=["acc", "tmp"], writes=[("mT", dc)])

            chk("merge")
            barrier()
            sa = Alloc(nc, mrg_off, LIM)
            gbc = [sa.t([128, D], F32) for _ in range(1 if last else 2)]
            lng = sa.t([128, D], F32); lnb = sa.t([128, D], F32)
            A("sp", lambda e: e.dma_start(out=lng[:], in_=ln_g[l:l + 1, :].partition_broadcast(128)), writes=["lng"], dma=True)
            A("sp", lambda e: e.dma_start(out=lnb[:], in_=ln_b[l:l + 1, :].partition_broadcast(128)), writes=["lnb"], dma=True)
            for r in range(len(gbc)):
                for h_ in range(2):
                    b = nb()
                    A("pe", lambda e, b=b, r=r, h_=h_: e.matmul(ps[b][:, 0:512], lhsT=sel[0:2, r * 128:(r + 1) * 128], rhs=grow[0:2, h_ * 512:(h_ + 1) * 512], start=True, stop=True),
                      reads=["sel", "grow"], writes=[("ps", b)])
                    fence([("ps", b)])
                    A("act", lambda e, b=b, r=r, h_=h_: e.activation(out=gbc[r][:, h_ * 512:(h_ + 1) * 512], in_=ps[b][:, 0:512], func=AF.Copy), reads=[("ps", b)], writes=[("gbc", r, h_)])
            mk = [("mT", dc) for dc in range(8)]
            wk_ = [("wout", kc) for kc in range(8)]
            ya = Alloc(nc, Y_OFF, Y_END)
            G4 = 3
            xr = [ya.t([128, D], F32) for _ in range(2 * G4)]
            zt = [ya.t([128, D], F32) for _ in range(2 * G4)]
            st2 = [sa.t([128, G4, 2, 6], F32) for _ in range(2)]
            mv2 = [sa.t([128, G4, 2], F32) for _ in range(2)]
            rs2 = [sa.t([128, G4], F32) for _ in range(2)]
            all_tiles = list(range(ntile_full))
            groups = [all_tiles[i:i + G4] for i in range(0, ntile_full, G4)]
            final = last or l == nlayers - 1
            for gi, grp in enumerate(groups):
                s_ = gi % 2
                ng = len(grp)
                for j, ti in enumerate(grp):
                    bi = s_ * G4 + j
                    r = 1 if ti >= 16 else 0
                    if l == 0:
                        src = x_in[ti * 128:(ti + 1) * 128, :] if ti < 16 else ctx_in[(ti - 16) * 128:(ti - 15) * 128, :]
                    else:
                        src = xs[ti * 128:(ti + 1) * 128, :]
                    A("sp", lambda e, bi=bi, src=src: e.dma_start(out=xr[bi][:], in_=src), reads=[("xs", ti)], writes=[("xr", bi)], dma=True)
                    for h_ in range(2):
                        b = nb()

                        def f(e, b=b, ti=ti, h_=h_):
                            for dc in range(8):
                                ins = e.matmul(ps[b][:, 0:512], lhsT=mT[:, dc, ti * 128:(ti + 1) * 128], rhs=wout[:, dc, h_ * 512:(h_ + 1) * 512], start=(dc == 0), stop=(dc == 7))
                            return ins
                        A("pe", f, reads=mk + wk_, writes=[("ps", b)], c=2.3)
                        A("dve", lambda e, b=b, bi=bi, r=r, h_=h_: e.tensor_tensor(out=zt[bi][:, h_ * 512:(h_ + 1) * 512], in0=ps[b][:, 0:512], in1=gbc[r][:, h_ * 512:(h_ + 1) * 512], op=ALU.mult),
                          reads=[("ps", b), ("gbc", r, h_)], writes=[("zt", bi, h_)])
                        A("pool", lambda e, bi=bi, h_=h_: e.tensor_tensor(out=zt[bi][:, h_ * 512:(h_ + 1) * 512], in0=zt[bi][:, h_ * 512:(h_ + 1) * 512], in1=xr[bi][:, h_ * 512:(h_ + 1) * 512], op=ALU.add),
                          reads=[("zt", bi, h_), ("xr", bi)], writes=[("zt", bi, h_)])
                for j in range(ng):
                    bi = s_ * G4 + j
                    for h_ in range(2):
                        A("dve", lambda e, bi=bi, j=j, h_=h_, s_=s_: e.bn_stats(out=st2[s_][:, j, h_, :], in_=zt[bi][:, h_ * 512:(h_ + 1) * 512]),
                          reads=[("zt", bi, h_)], writes=[("st2", s_, j, h_)])
                for j in range(ng):
                    A("dve", lambda e, j=j, s_=s_: e.bn_aggr(out=mv2[s_][:, j, :], in_=st2[s_][:, j, :, :]),
                      reads=[("st2", s_, j, 0), ("st2", s_, j, 1)], writes=[("mv2", s_, j)])
                mvk = [("mv2", s_, j) for j in range(ng)]
                A("dve", lambda e, s_=s_, ng=ng: e.tensor_scalar(out=rs2[s_][:, 0:ng], in0=mv2[s_][:, 0:ng, 1], scalar1=EPS / (ALPHA * ALPHA), scalar2=None, op0=ALU.add),
                  reads=mvk, writes=[("rs2", s_)])
                A("act", lambda e, s_=s_, ng=ng: e.activation(out=rs2[s_][:, 0:ng], in_=rs2[s_][:, 0:ng], func=AF.Sqrt), reads=[("rs2", s_)], writes=[("rs2", s_)])
                A("dve", lambda e, s_=s_, ng=ng: e.reciprocal(out=rs2[s_][:, 0:ng], in_=rs2[s_][:, 0:ng]), reads=[("rs2", s_)], writes=[("rs2", s_)])
                for j in range(ng):
                    bi = s_ * G4 + j
                    zk = [("zt", bi, 0), ("zt", bi, 1)]
                    A("dve", lambda e, bi=bi, j=j, s_=s_: e.scalar_tensor_tensor(out=zt[bi][:], in0=zt[bi][:], scalar=mv2[s_][:, j, 0:1], in1=lng[:], op0=ALU.subtract, op1=ALU.mult),
                      reads=zk + [("mv2", s_, j), "lng"], writes=zk)
                for j in range(ng):
                    bi = s_ * G4 + j
                    zk = [("zt", bi, 0), ("zt", bi, 1)]
                    A("dve", lambda e, bi=bi, j=j, s_=s_: e.scalar_tensor_tensor(out=zt[bi][:], in0=zt[bi][:], scalar=rs2[s_][:, j:j + 1], in1=lnb[:], op0=ALU.mult, op1=ALU.add),
                      reads=zk + [("rs2", s_), "lnb"], writes=zk)
                for j, ti in enumerate(grp):
                    bi = s_ * G4 + j
                    zk = [("zt", bi, 0), ("zt", bi, 1)]
                    if final:
                        if ti < 16:
                            A("sp", lambda e, bi=bi, ti=ti: e.dma_start(out=out[ti * 128:(ti + 1) * 128, :], in_=zt[bi][:]), reads=zk, writes=[("out", ti)], dma=True)
                    else:
                        A("sp", lambda e, bi=bi, ti=ti: e.dma_start(out=xs[ti * 128:(ti + 1) * 128, :], in_=zt[bi][:]), reads=zk, writes=[("xs", ti)], dma=True)
                        if ti == 0:
                            A("sp", lambda e, bi=bi: e.dma_start(out=send[0:128, :], in_=zt[bi][:]), reads=zk, writes=[("send", 0)], dma=True)
                        if ti == 15:
                            A("sp", lambda e, bi=bi: e.dma_start(out=send[128:256, :], in_=zt[bi][:]), reads=zk, writes=[("send", 1)], dma=True)
            if not (last or l == nlayers - 1):
                A("pool", lambda e: e.collective_compute("AllGather", ALU.bypass, replica_groups=RG, ins=[send.opt()], outs=[gath.opt()]),
                  reads=[("send", 0), ("send", 1)], writes=["gath"], dma="cc", ne=True)
        try:
            for l_ in range(nlayers):
                layer(l_)
        except _Stop:
            pass
        if dbg:
            dbg_h = nc.dram_tensor("dbg_h", [128, KC * NT], BF16, kind="ExternalOutput").ap()
            dbg_y = nc.dram_tensor("dbg_y", [128, 12 * NFULL], BF16, kind="ExternalOutput").ap()
            A("sp", lambda e: e.dma_start(out=dbg_h[:, :], in_=hT[:].rearrange("p a b -> p (a b)")), reads=["hT"], writes=["dbg_h"], dma=True)
            A("sp", lambda e: e.dma_start(out=dbg_y[:, :], in_=Y[:].rearrange("p a b -> p (a b)")), reads=[("Y", i) for i in range(12)], writes=["dbg_y"], dma=True)
            A("sp", None, reads=["dbg_h", "dbg_y"])
        A("sp", None, reads=[("out", ti) for ti in range(16)])
        S.run()
    return nc


def _rope_tables(pos):
    quarter = 16
    inv = (10000.0 ** (-np.arange(quarter, dtype=np.float32) / quarter)).astype(np.float32)
    row = (pos // 64).astype(np.float32)
    col = (pos % 64).astype(np.float32)
    ang_r = row[:, None] * inv[None, :]
    ang_c = col[:, None] * inv[None, :]
    ang = np.concatenate([ang_r, ang_r, ang_c, ang_c], -1).astype(np.float32)
    return np.cos(ang).astype(np.float32), np.sin(ang).astype(np.float32)


def make_in_maps(inputs):
    x = np.ascontiguousarray(inputs["x"], dtype=np.float32)
    B, Sq, _ = x.shape
    bf = ml_dtypes.bfloat16
    perm = np.zeros((128, 128), np.float32)
    for hd in range(2):
        o = hd * 64
        for m in range(64):
            seg, i = divmod(m, 16)
            if seg == 0:
                perm[o + m + 16, o + m] = -1.0
            elif seg == 1:
                perm[o + m - 16, o + m] = 1.0
            elif seg == 2:
                perm[o + m + 16, o + m] = -1.0
            else:
                perm[o + m - 16, o + m] = 1.0
    jj = np.arange(128)[:, None]
    qq = np.arange(128)[None, :]
    maskP = (jj >= qq).astype(np.float32)
    maskN = (jj <= qq).astype(np.float32)
    sel = np.zeros((2, 256), np.float32)
    sel[0, 0:128] = 1.0
    sel[1, 128:256] = 1.0
    wnames = ["w_ada", "b_ada", "w_in", "attn_sink", "conv_dw_w", "conv_dw_b", "conv_ln_g", "conv_ln_b", "lru_conv_w", "lru_conv_b",
              "lru_w_a", "lru_b_a", "lru_w_x", "lru_b_x", "lru_lambda", "w_branch", "w_gate", "b_gate", "w_out", "ln_g", "ln_b"]
    shared = {k: np.ascontiguousarray(inputs[k], dtype=np.float32) for k in wnames}
    shared["perm"] = perm.astype(bf)
    shared["identb"] = np.eye(128, dtype=np.float32).astype(bf)
    shared["identf"] = np.eye(128, dtype=np.float32)
    shared["sel"] = sel
    maps = []
    for c in range(8):
        b, r = divmod(c, 4)
        t0 = r * T_OWN
        m = dict(shared)
        m["x_in"] = x[b, t0:t0 + T_OWN]
        xh = np.zeros((256, D), np.float32)
        if r > 0:
            xh[0:128] = x[b, t0 - 128:t0]
        if r < 3:
            xh[128:256] = x[b, t0 + T_OWN:t0 + T_OWN + 128]
        m["xh_in"] = xh
        m["ctx_in"] = np.ascontiguousarray(inputs["ctx"][b], dtype=np.float32)
        m["c_in"] = np.stack([inputs["c"][b], inputs["c_ctx"]]).astype(np.float32)
        pos = np.zeros(NT, np.int64)
        pos[0:T_OWN] = t0 + np.arange(T_OWN)
        pos[LH0:LH0 + 128] = np.clip(t0 - 128 + np.arange(128), 0, Sq - 1)
        pos[RH0:RH0 + 128] = np.clip(t0 + T_OWN + np.arange(128), 0, Sq - 1)
        cos, sin = _rope_tables(pos)
        cos[CTX0:CTX0 + 256] = 1.0
        sin[CTX0:CTX0 + 256] = 0.0
        m["cosT"] = np.ascontiguousarray(np.concatenate([cos.T, cos.T], 0))
        m["sinT"] = np.ascontiguousarray(np.concatenate([sin.T, sin.T], 0))
        hl = 1.0 if r > 0 else 0.0
        hr = 1.0 if r < 3 else 0.0
        m["masks"] = np.concatenate([maskP, maskN, maskP * hl, maskN * hr], 1).astype(bf)
        fl = np.zeros((128, 8), np.float32)
        fl[:, 0] = hl
        fl[:, 1] = hr
        fl[:, 2 + r] = 1.0
        m["flags"] = fl
        maps.append(m)
    return maps


_NC_CACHE = {}


def kernel(**inputs):
    if "nc" not in _NC_CACHE:
        _NC_CACHE["nc"] = build_nc()
    nc = _NC_CACHE["nc"]
    maps = make_in_maps(inputs)
    res = run_bass_kernel_spmd(nc, maps, core_ids=list(range(8)))
    x = inputs["x"]
    outp = np.zeros(x.shape, np.float32)
    for c in range(8):
        b, r = divmod(c, 4)
        outp[b, r * T_OWN:(r + 1) * T_OWN] = res.results[c]["out"]
    return outp
```

```python
import numpy as np
import ml_dtypes
from contextlib import ExitStack
import concourse.bass as bass
import concourse.mybir as mybir
from concourse.bass_utils import run_bass_kernel_spmd

F32 = mybir.dt.float32
BF16 = mybir.dt.bfloat16
ALU = mybir.AluOpType
AF = mybir.ActivationFunctionType

DEPTH = 4
D = 1024
KC = 8
T_OWN = 2048
NT = 2560
NFULL = 2304
CTX0 = 2048
LH0 = 2304
RH0 = 2432
ALPHA = (2 * DEPTH) ** 0.25
EPS = 1e-6
C_Q, C_K, C_V, C_GATT, C_CA, C_CG, C_GCONV, C_XLRU, C_GLRU = 0, 512, 640, 768, 1280, 1792, 2304, 2816, 3328
FULL_R = [(0, 512), (512, 1024), (1024, 1536), (1536, 2048), (2048, 2304)]

ENG_NAMES = ("pe", "act", "dve", "pool", "sp")
SAME_ENG_SYNC = {"act", "dve", "pool"}


class Op:
    __slots__ = ("eng", "fn", "deps", "sig", "tok_sem", "tok_val", "dma", "idx", "cost", "lat", "seq", "alldeps")

    DEF_COST = {"pe": 1.5, "act": 0.5, "dve": 0.6, "pool": 1.0, "sp": 0.1}

    def __init__(self, eng, fn, dma, c=None):
        self.eng = eng
        self.fn = fn
        self.dma = dma
        if dma == "cc":
            self.cost, self.lat = 1.0, 100.0
        elif dma:
            self.cost = 0.1 if eng != "pool" else 0.8
            self.lat = self.cost + (c if c is not None else 3.0)
        else:
            self.cost = c if c is not None else self.DEF_COST[eng]
            self.lat = self.cost + 0.15
        if fn is None:
            self.cost = self.lat = 0.0
        self.deps = set()
        self.sig = False
        self.tok_sem = None
        self.tok_val = 0
        self.idx = -1


class Sched:
    NDMA = {"sp": 8, "pool": 6, "act": 4}

    def __init__(self, nc, stack):
        self.nc = nc
        self.ops = {e: [] for e in ENG_NAMES}
        self.all_ops = []
        self.reorder = True
        self.last_w = {}
        self.readers = {}
        self.esem = {}
        for e in ("pe", "act", "dve", "pool"):
            self.esem[e] = stack.enter_context(nc.semaphore("s_" + e))
        self.dsem = {}
        for e, n in self.NDMA.items():
            self.dsem[e] = [stack.enter_context(nc.semaphore("d_%s%d" % (e, i))) for i in range(n)]
        self.ccsem = stack.enter_context(nc.semaphore("s_cc"))
        self.ccval = 0
        self.dcount = {e: 0 for e in self.NDMA}
        self.dlast = {e: [None] * n for e, n in self.NDMA.items()}
        self.dval = {e: [0] * n for e, n in self.NDMA.items()}

    def add(self, eng, fn, reads=(), writes=(), dma=False, c=None, ne=False):
        if "EPOCH" not in writes and not ne:
            reads = list(reads) + ["EPOCH"]
        if dma == "cc":
            reads = list(reads) + ["CCORDER"]
            writes = list(writes) + ["CCORDER"]
        op = Op(eng, fn, dma, c)
        op.seq = len(self.all_ops)
        self.all_ops.append(op)
        deps = op.deps
        for k in reads:
            w = self.last_w.get(k)
            if w is not None:
                deps.add(w)
        for k in writes:
            w = self.last_w.get(k)
            if w is not None:
                deps.add(w)
            for r in self.readers.get(k, ()):
                deps.add(r)
        for k in reads:
            self.readers.setdefault(k, []).append(op)
        for k in writes:
            self.last_w[k] = op
            self.readers[k] = []
        if dma == "cc":
            self.ccval += 1
            op.tok_sem = self.ccsem
            op.tok_val = self.ccval
            op.sig = True
        elif dma:
            n = self.dcount[eng]
            self.dcount[eng] = n + 1
            slot = n % self.NDMA[eng]
            prev = self.dlast[eng][slot]
            if prev is not None:
                deps.add(prev)
            self.dlast[eng][slot] = op
            self.dval[eng][slot] += 16
            op.tok_sem = self.dsem[eng][slot]
            op.tok_val = self.dval[eng][slot]
            op.sig = True
        deps.discard(op)
        op.idx = len(self.ops[eng])
        self.ops[eng].append(op)
        return op

    def list_schedule(self):
        import heapq
        ops = self.all_ops
        ndep = [0] * len(ops)
        users = [[] for _ in ops]
        for op in ops:
            ndep[op.seq] = len(op.deps)
            for d in op.deps:
                users[d.seq].append(op)
        ready_t = [0.0] * len(ops)
        waiting = {e: [] for e in ENG_NAMES}
        avail = {e: [] for e in ENG_NAMES}
        free = {e: 0.0 for e in ENG_NAMES}
        for op in ops:
            if ndep[op.seq] == 0:
                heapq.heappush(waiting[op.eng], (0.0, op.seq))
        newq = {e: [] for e in ENG_NAMES}
        left = len(ops)
        while left:
            best = None
            for e in ENG_NAMES:
                w, a = waiting[e], avail[e]
                while w and w[0][0] <= free[e]:
                    heapq.heappush(a, heapq.heappop(w)[1])
                if a:
                    cand = (free[e], a[0], e, True)
                elif w:
                    cand = (w[0][0], w[0][1], e, False)
                else:
                    continue
                if best is None or cand[:2] < best[:2]:
                    best = cand
            t, sq, e, from_avail = best
            if from_avail:
                heapq.heappop(avail[e])
            else:
                heapq.heappop(waiting[e])
            op = ops[sq]
            newq[e].append(op)
            free[e] = t + op.cost
            done = t + op.lat
            left -= 1
            for u in users[sq]:
                if done > ready_t[u.seq]:
                    ready_t[u.seq] = done
                ndep[u.seq] -= 1
                if ndep[u.seq] == 0:
                    heapq.heappush(waiting[u.eng], (ready_t[u.seq], u.seq))
        for e in ENG_NAMES:
            assert len(newq[e]) == len(self.ops[e])
            self.ops[e] = newq[e]
            for i, op in enumerate(newq[e]):
                op.idx = i
        self.est_total = max(free.values())

    def finalize(self):
        if self.reorder:
            self.list_schedule()
        for e in ENG_NAMES:
            for op in self.ops[e]:
                best = {}
                keep = set()
                for d in op.deps:
                    if d.dma:
                        keep.add(d)
                    else:
                        if d.eng == op.eng and not op.dma and d.eng not in SAME_ENG_SYNC:
                            continue
                        b = best.get(d.eng)
                        if b is None or d.idx > b.idx:
                            best[d.eng] = d
                keep.update(best.values())
                op.deps = keep
                for d in keep:
                    d.sig = True
        for e in ("pe", "act", "dve", "pool"):
            c = 0
            for op in self.ops[e]:
                if op.dma:
                    continue
                if op.sig:
                    c += 1
                    op.tok_sem = self.esem[e]
                    op.tok_val = c

    def replay(self, e, eng):
        waited = {}
        for op in self.ops[e]:
            for d in sorted(op.deps, key=lambda d: (d.eng, d.idx)):
                key = id(d.tok_sem)
                if waited.get(key, 0) < d.tok_val:
                    eng.wait_ge(d.tok_sem, d.tok_val)
                    waited[key] = d.tok_val
            if op.fn is None:
                continue
            ins = op.fn(eng)
            if op.sig:
                if op.dma == "cc":
                    ins.then_inc(op.tok_sem)
                else:
                    ins.then_inc(op.tok_sem, 16 if op.dma else 1)

    def run(self):
        self.finalize()
        with self.nc.Block() as block:
            @block.tensor
            def _(eng):
                self.replay("pe", eng)

            @block.scalar
            def _(eng):
                self.replay("act", eng)

            @block.vector
            def _(eng):
                self.replay("dve", eng)

            @block.gpsimd
            def _(eng):
                self.replay("pool", eng)

            @block.sync
            def _(eng):
                self.replay("sp", eng)


class Alloc:
    def __init__(self, nc, base, limit):
        self.nc = nc
        self.off = base
        self.limit = limit
        self.n = 0

    def t(self, shape, dtype):
        esz = 2 if dtype == BF16 else 4
        per = esz
        for s in shape[1:]:
            per *= s
        per = (per + 63) // 64 * 64
        h = self.nc.alloc_sbuf_tensor_at("t%d_%d" % (self.n, self.off), list(shape), dtype, offset=self.off)
        self.n += 1
        self.off += per
        assert self.off <= self.limit, ("sbuf overflow", self.off, self.limit)
        return h


class _Stop(Exception):
    pass


def build_nc(nlayers=DEPTH, dbg=False, stop=None):
    def chk(name):
        if stop == name:
            raise _Stop()
    nc = bass.Bass("TRN2", target_bir_lowering=False)
    inp = lambda n, s, d=F32: nc.dram_tensor(n, list(s), d, kind="ExternalInput").ap()
    x_in = inp("x_in", [T_OWN, D])
    xh_in = inp("xh_in", [256, D])
    ctx_in = inp("ctx_in", [256, D])
    c_in = inp("c_in", [2, D])
    w_ada = inp("w_ada", [DEPTH, D, 3 * D]); b_ada = inp("b_ada", [DEPTH, 3 * D])
    w_in = inp("w_in", [DEPTH, D, 3840]); attn_sink = inp("attn_sink", [DEPTH, 8])
    conv_dw_w = inp("conv_dw_w", [DEPTH, 31, 512]); conv_dw_b = inp("conv_dw_b", [DEPTH, 512])
    conv_ln_g = inp("conv_ln_g", [DEPTH, 512]); conv_ln_b = inp("conv_ln_b", [DEPTH, 512])
    lru_conv_w = inp("lru_conv_w", [DEPTH, 2, 4, 512]); lru_conv_b = inp("lru_conv_b", [DEPTH, 2, 512])
    lru_w_a = inp("lru_w_a", [DEPTH, 2, 8, 64, 64]); lru_b_a = inp("lru_b_a", [DEPTH, 2, 512])
    lru_w_x = inp("lru_w_x", [DEPTH, 2, 8, 64, 64]); lru_b_x = inp("lru_b_x", [DEPTH, 2, 512])
    lru_lambda = inp("lru_lambda", [DEPTH, 2, 512])
    w_branch = inp("w_branch", [DEPTH, 3, 512, D]); w_gate = inp("w_gate", [DEPTH, D, 3 * D])
    b_gate = inp("b_gate", [DEPTH, 3 * D]); w_out = inp("w_out", [DEPTH, D, D])
    ln_g = inp("ln_g", [DEPTH, D]); ln_b = inp("ln_b", [DEPTH, D])
    cos_d = inp("cosT", [128, NT]); sin_d = inp("sinT", [128, NT])
    masks_d = inp("masks", [128, 512], BF16)
    perm_d = inp("perm", [128, 128], BF16)
    identb_d = inp("identb", [128, 128], BF16)
    identf_d = inp("identf", [128, 128])
    flags_d = inp("flags", [128, 8])
    sel_d = inp("sel", [2, 256])
    out = nc.dram_tensor("out", [T_OWN, D], F32, kind="ExternalOutput").ap()

    xs = nc.dram_tensor("xs", [NFULL, D], F32).ap()
    send = nc.dram_tensor("send", [256, D], F32).ap()
    gath = nc.dram_tensor("gath", [4 * 256, D], F32).ap()
    au = nc.dram_tensor("au", [16 * 128, 2048], F32).ap()
    csend = nc.dram_tensor("csend", [128, 16], F32).ap()
    cgath = nc.dram_tensor("cgath", [4 * 128, 16], F32).ap()
    RG = [[0, 1, 2, 3], [4, 5, 6, 7]]

    with ExitStack() as st:
        S = Sched(nc, st)
        A = S.add
        BASE = 16640
        LIM = 229376
        pa = Alloc(nc, BASE, LIM)
        hT = pa.t([128, KC, NT], BF16)
        Y_OFF = pa.off
        Y = pa.t([128, 12, NFULL], BF16)
        Y_END = pa.off
        identb = pa.t([128, 128], BF16); identf = pa.t([128, 128], F32); onesf = pa.t([128, 128], F32)
        onesb = pa.t([128, 128], BF16)
        perm = pa.t([128, 128], BF16); masks = pa.t([128, 4, 128], BF16)
        flags = pa.t([128, 8], F32); sel = pa.t([2, 256], F32)
        scT = pa.t([128, KC, 2], BF16); cT = pa.t([128, KC, 2], F32)
        grow = pa.t([2, D], F32)
        modc = pa.t([128, 16, 2], F32)
        pcols = pa.t([128, 256], F32)
        cA = pa.t([128, 8], F32)
        carry = pa.t([128, 8], F32)
        s0 = pa.t([128, 8], F32)
        ylru_ctx = pa.t([128, 4, 256], F32)
        WG = 256
        wb = [pa.t([128, KC, WG], BF16) for _ in range(2)]
        wk = pa.t([128, KC, 256], BF16)
        zeros = pa.t([128, 128], F32)
        jn_t = pa.t([128, 16], F32)
        bar_t = pa.t([128, 16], F32)
        csb = pa.t([128, 16], F32)
        cg = pa.t([128, 4, 16], F32)
        chain = pa.t([128, 2, 4, 4], F32)
        SCR = pa.off
        ps = [st.enter_context(nc.psum_tensor("ps%d" % i, [128, 512], F32)) for i in range(6)]
        psTs = [st.enter_context(nc.psum_tensor("psT%d" % i, [128, 1024], BF16)) for i in range(2)]
        bank = [0]
        dyn = {}

        def nb():
            bank[0] = (bank[0] + 1) % 5
            return bank[0]

        def barrier():
            A("pool", lambda e: e.memset(bar_t[:], 0.0), writes=["EPOCH"])

        bpool = {}

        def nbp(name, banks):
            i = bpool.get(name, 0)
            bpool[name] = i + 1
            return banks[i % len(banks)]

        def fence(keys):
            A("pe", lambda e: e.matmul(ps[5][:, 0:2], lhsT=onesb[:, 0:128], rhs=onesb[:, 0:2], start=True, stop=True),
              reads=list(keys) + ["onesb"], writes=list(keys))
        wbi = [0]

        for (t, d, k) in ((identb, identb_d, "identb"), (identf, identf_d, "identf"), (perm, perm_d, "perm"),
                          (flags, flags_d, "flags")):
            A("sp", lambda e, t=t, d=d: e.dma_start(out=t[:], in_=d[:, :]), writes=[k], dma=True)
        A("sp", lambda e: e.dma_start(out=masks[:], in_=masks_d.rearrange("p (m q) -> p m q", m=4)), writes=["masks"], dma=True)
        A("sp", lambda e: e.dma_start(out=sel[:], in_=sel_d[:, :]), writes=["sel"], dma=True)
        A("pool", lambda e: e.memset(onesf[:], 1.0), writes=["onesf"])
        A("pool", lambda e: e.memset(onesb[:], 1.0), writes=["onesb"])
        A("pool", lambda e: e.memset(zeros[:], 0.0), writes=["zeros"])

        for r_ in range(2):
            def f_cT(e, r_=r_):
                with nc.allow_non_contiguous_dma(reason="tiny one-off transpose load of c"):
                    return e.dma_start(out=cT[:, :, r_], in_=c_in[r_].rearrange("(kc p) -> p kc", p=128))
            A("sp", f_cT, writes=[("cT", r_)], dma=True)
        A("act", lambda e: e.activation(out=scT[:], in_=cT[:], func=AF.Silu), reads=[("cT", 0), ("cT", 1)], writes=["scT"])

        def load_w(src, c0, w):
            i = wbi[0] % 2
            wbi[0] += 1
            buf = wb[i]
            A("pool", lambda e: e.dma_start(out=buf[:, :, 0:w], in_=src[:, c0:c0 + w].rearrange("(kc p) c -> p kc c", p=128)),
              writes=[("wb", i)], dma=True, ne=True)
            return buf, ("wb", i)

        def proj_fm(buf, bkey, mc, n0, n1, extra_reads=(), pool=None):
            b = nb() if pool is None else nbp(*pool)

            def f(e):
                for kc in range(KC):
                    ins = e.matmul(ps[b][:, 0:n1 - n0], lhsT=buf[:, kc, mc * 128:(mc + 1) * 128], rhs=hT[:, kc, n0:n1],
                                   start=(kc == 0), stop=(kc == KC - 1))
                return ins
            A("pe", f, reads=[bkey, "hT"] + list(extra_reads), writes=[("ps", b)], c=0.1 + 8 * 0.27 * (n1 - n0) / 512.0, ne=True)
            return b

        def layer(l):
            last = (l == DEPTH - 1)
            full_r = FULL_R[:4] if last else FULL_R
            ntile_full = 16 if last else 18
            barrier()
            sa = Alloc(nc, SCR, LIM)
            modrows = sa.t([2, 2 * D], F32)
            brow = sa.t([2, 2 * D], F32)
            A("sp", lambda e: e.dma_start(out=brow[:], in_=b_ada[l:l + 1, 0:2 * D].partition_broadcast(2)), writes=["brow"], dma=True)
            for g in range(2 * D // WG):
                buf, bkey = load_w(w_ada[l], g * WG, WG)
                b = nb()

                def f(e, buf=buf, b=b):
                    for kc in range(KC):
                        ins = e.matmul(ps[b][0:2, 0:WG], lhsT=scT[:, kc, :], rhs=buf[:, kc, 0:WG], start=(kc == 0), stop=(kc == KC - 1))
                    return ins
                A("pe", f, reads=[bkey, "scT"], writes=[("ps", b)])
                A("dve", lambda e, b=b, g=g: e.tensor_tensor(out=modrows[:, g * WG:(g + 1) * WG], in0=ps[b][0:2, 0:WG],
                                                               in1=brow[:, g * WG:(g + 1) * WG], op=ALU.add),
                  reads=[("ps", b), "brow"], writes=[("modrows", g)])
            mr_all = [("modrows", g) for g in range(2 * D // WG)]
            b = nb()

            def f(e, b=b):
                for j in range(16):
                    ins = e.matmul(ps[b][:, 2 * j:2 * j + 2], lhsT=modrows[0:2, j * 128:(j + 1) * 128], rhs=identf[0:2, 0:2],
                                   start=True, stop=True)
                return ins
            A("pe", f, reads=mr_all + ["identf"], writes=[("ps", b)])
            fence([("ps", b)])
            A("dve", lambda e, b=b: e.tensor_copy(out=modc[:].rearrange("p a r -> p (a r)"), in_=ps[b][:, 0:32]),
              reads=[("ps", b)], writes=["modc"])
            A("dve", lambda e: e.tensor_scalar(out=modc[:, 8:16, :], in0=modc[:, 8:16, :], scalar1=1.0, scalar2=None, op0=ALU.add),
              reads=["modc"], writes=["modc"])
            prow = sa.t([128, 2, 128], F32)
            A("pool", lambda e: e.memset(prow[:], 0.0), writes=["prow"])
            plist = [
                (0, 0, conv_dw_w[l].rearrange("k (cc p) -> (k cc) p", p=128), 124),
                (0, 124, conv_dw_b[l].rearrange("(cc p) -> cc p", p=128), 4),
                (1, 0, conv_ln_g[l].rearrange("(cc p) -> cc p", p=128), 4),
                (1, 4, conv_ln_b[l].rearrange("(cc p) -> cc p", p=128), 4),
                (1, 8, lru_conv_w[l].rearrange("d k (cc p) -> (d k cc) p", p=128), 32),
                (1, 40, lru_conv_b[l].rearrange("d (cc p) -> (d cc) p", p=128), 8),
                (1, 48, lru_b_a[l].rearrange("d (cc p) -> (d cc) p", p=128), 8),
                (1, 56, lru_b_x[l].rearrange("d (cc p) -> (d cc) p", p=128), 8),
                (1, 64, lru_lambda[l].rearrange("d (cc p) -> (d cc) p", p=128), 8),
                (1, 72, b_gate[l].rearrange("(r p) -> r p", p=128), 24),
            ]
            for (s_, r0, src, n) in plist:
                A("sp", lambda e, s_=s_, r0=r0, src=src, n=n: e.dma_start(out=prow[r0:r0 + n, s_, :], in_=src),
                  reads=["prow"], writes=[("prow", s_, r0)], dma=True)
            b = nb()

            def f(e, b=b):
                for s_ in range(2):
                    ins = e.matmul(ps[b][:, s_ * 128:(s_ + 1) * 128], lhsT=prow[:, s_, :], rhs=identf[:], start=True, stop=True)
                return ins
            A("pe", f, reads=[("prow", s_, r0) for (s_, r0, _, _) in plist] + ["identf"], writes=[("ps", b)])
            fence([("ps", b)])
            A("dve", lambda e, b=b: e.tensor_copy(out=pcols[:], in_=ps[b][:, 0:256]), reads=[("ps", b)], writes=["pcols"])
            PB = 128
            A("act", lambda e: e.activation(out=cA[:], in_=pcols[:, PB + 64:PB + 72], func=AF.Exp, scale=-1.0), reads=["pcols"], writes=["cA"])
            A("act", lambda e: e.activation(out=cA[:], in_=cA[:], func=AF.Ln, bias=1.0, scale=1.0), reads=["cA"], writes=["cA"])
            A("dve", lambda e: e.tensor_scalar(out=cA[:], in0=cA[:], scalar1=-8.0, scalar2=None, op0=ALU.mult), reads=["cA"], writes=["cA"])

            if dbg and l == 0:
                dbg_mr = nc.dram_tensor("dbg_mr", [2, 3 * D], F32, kind="ExternalOutput").ap()
                dbg_mc = nc.dram_tensor("dbg_mc", [128, 32], F32, kind="ExternalOutput").ap()
                dbg_ct = nc.dram_tensor("dbg_ct", [128, 16], F32, kind="ExternalOutput").ap()
                dbg_pc = nc.dram_tensor("dbg_pc", [128, 256], F32, kind="ExternalOutput").ap()
                A("sp", lambda e: e.dma_start(out=dbg_mr[:, :], in_=modrows[:]), reads=mr_all, writes=["dbg_mr"], dma=True)
                A("sp", lambda e: e.dma_start(out=dbg_mc[:, :], in_=modc[:].rearrange("p a r -> p (a r)")), reads=["modc"], writes=["dbg_mc"], dma=True)
                A("sp", lambda e: e.dma_start(out=dbg_ct[:, :], in_=cT[:].rearrange("p a r -> p (a r)")), reads=["scT"], writes=["dbg_ct"], dma=True)
                A("sp", lambda e: e.dma_start(out=dbg_pc[:, :], in_=pcols[:]), reads=["pcols", "cA"], writes=["dbg_pc"], dma=True)
                A("sp", None, reads=["dbg_mr", "dbg_mc", "dbg_ct", "dbg_pc"])
            chk("adaln")
            ya = Alloc(nc, Y_OFF, Y_END)
            G = 5
            xt = [ya.t([128, D], F32) for _ in range(2 * G)]
            xn = [sa.t([128, D], BF16) for _ in range(2 * G)]
            stats = [sa.t([128, G, 2, 6], F32) for _ in range(2)]
            mv = [sa.t([128, G, 2], F32) for _ in range(2)]
            rstd = [sa.t([128, G], F32) for _ in range(2)]
            nbias = [sa.t([128, G], F32) for _ in range(2)]
            pTi = [0]
            act_tiles = []
            for gi in range(4):
                s_ = gi % 2
                tiles = list(range(gi * G, gi * G + G))
                for j, ti in enumerate(tiles):
                    bi = s_ * G + j
                    if ti < 16:
                        src = (x_in if l == 0 else xs)[ti * 128:(ti + 1) * 128, :]
                        rk = [("xs", ti)]
                    elif ti < 18:
                        src = (ctx_in[(ti - 16) * 128:(ti - 15) * 128, :] if l == 0 else xs[ti * 128:(ti + 1) * 128, :])
                        rk = [("xs", ti)]
                    else:
                        rk = ["gath"]
                        src = xh_in[(ti - 18) * 128:(ti - 17) * 128, :] if l == 0 else None
                    if src is not None:
                        A("sp", lambda e, bi=bi, src=src: e.dma_start(out=xt[bi][:], in_=src), reads=rk, writes=[("xt", bi)], dma=True)
                    else:
                        def f(e, bi=bi, ti=ti):
                            if "L" not in dyn:
                                pid = e.partition_id()
                                dyn["L"] = ((pid + 3) % 4) * 256 + 128
                                dyn["R"] = ((pid + 1) % 4) * 256
                            row = dyn["L"] if ti == 18 else dyn["R"]
                            return e.dma_start(out=xt[bi][:], in_=gath[bass.ds(row, 128), :])
                        A("sp", f, reads=rk, writes=[("xt", bi)], dma=True)
                for j in range(G):
                    bi = s_ * G + j
                    for h_ in range(2):
                        A("dve", lambda e, bi=bi, j=j, h_=h_, s_=s_: e.bn_stats(out=stats[s_][:, j, h_, :], in_=xt[bi][:, h_ * 512:(h_ + 1) * 512]),
                          reads=[("xt", bi)], writes=[("st", s_, j, h_)])
                for j in range(G):
                    A("dve", lambda e, j=j, s_=s_: e.bn_aggr(out=mv[s_][:, j, :], in_=stats[s_][:, j, :, :]),
                      reads=[("st", s_, j, 0), ("st", s_, j, 1)], writes=[("mv", s_, j)])
                mvk = [("mv", s_, j) for j in range(G)]
                A("dve", lambda e, s_=s_: e.tensor_scalar(out=rstd[s_][:], in0=mv[s_][:, :, 1], scalar1=EPS, scalar2=None, op0=ALU.add),
                  reads=mvk, writes=[("rstd", s_)])
                A("act", lambda e, s_=s_: e.activation(out=rstd[s_][:], in_=rstd[s_][:], func=AF.Sqrt), reads=[("rstd", s_)], writes=[("rstd", s_)])
                A("dve", lambda e, s_=s_: e.reciprocal(out=rstd[s_][:], in_=rstd[s_][:]), reads=[("rstd", s_)], writes=[("rstd", s_)])
                A("dve", lambda e, s_=s_: e.scalar_tensor_tensor(out=nbias[s_][:], in0=mv[s_][:, :, 0], scalar=-1.0, in1=rstd[s_][:], op0=ALU.mult, op1=ALU.mult),
                  reads=mvk + [("rstd", s_)], writes=[("nbias", s_)], c=0.2)
                for j in range(G):
                    bi = s_ * G + j
                    A("act", lambda e, bi=bi, j=j, s_=s_: e.activation(out=xn[bi][:], in_=xt[bi][:], func=AF.Identity, scale=rstd[s_][:, j:j + 1], bias=nbias[s_][:, j:j + 1]),
                      reads=[("xt", bi), ("nbias", s_), ("rstd", s_)], writes=[("xn", bi)], c=1.0)
                for j, ti in enumerate(tiles):
                    bi = s_ * G + j
                    pTi[0] += 1
                    pi = pTi[0] % 2
                    psT = psTs[pi]

                    def f(e, bi=bi, psT=psT):
                        for kc in range(KC):
                            ins = e.transpose(out=psT[:, kc * 128:(kc + 1) * 128], in_=xn[bi][:, kc * 128:(kc + 1) * 128], identity=identb[:])
                        return ins
                    A("pe", f, reads=[("xn", bi), "identb"], writes=[("psT", pi)], c=0.9)
                    r = 1 if 16 <= ti < 18 else 0

                    if pi == 0:
                        A("act", lambda e, ti=ti, psT=psT: e.activation(out=hT[:, :, ti * 128:(ti + 1) * 128], in_=psT[:].rearrange("p (k t) -> p k t", k=KC), func=AF.Copy),
                          reads=[("psT", pi)], writes=[("hTe", ti)], c=1.1)
                    else:
                        A("dve", lambda e, ti=ti, psT=psT: e.tensor_copy(out=hT[:, :, ti * 128:(ti + 1) * 128], in_=psT[:].rearrange("p (k t) -> p k t", k=KC)),
                          reads=[("psT", pi)], writes=[("hTe", ti)], c=0.8)

            A("dve", lambda e: e.memset(jn_t[:], 0.0), reads=[("hTe", t_) for t_ in range(20)], writes=["hTraw"], c=0.1)
            mkeys = []
            for kc in range(KC):
                for ri_, (t0_, t1_, r_) in enumerate(((0, 2048, 0), (2304, 2560, 0), (2048, 2304, 1))):
                    mk_ = ("hTm", kc, ri_)
                    mkeys.append(mk_)
                    if kc % 2 == 0:
                        A("dve", lambda e, kc=kc, t0_=t0_, t1_=t1_, r_=r_: e.tensor_scalar(out=hT[:, kc, t0_:t1_], in0=hT[:, kc, t0_:t1_],
                                                                                         scalar1=modc[:, 8 + kc, r_:r_ + 1], scalar2=modc[:, kc, r_:r_ + 1], op0=ALU.mult, op1=ALU.add),
                          reads=["hTraw", "modc"], writes=[mk_], c=0.2 + (t1_ - t0_) / 1500.0)
                    else:
                        A("act", lambda e, kc=kc, t0_=t0_, t1_=t1_, r_=r_: e.activation(out=hT[:, kc, t0_:t1_], in_=hT[:, kc, t0_:t1_], func=AF.Identity,
                                                                                      scale=modc[:, 8 + kc, r_:r_ + 1], bias=modc[:, kc, r_:r_ + 1]),
                          reads=["hTraw", "modc"], writes=[mk_], c=0.3 + (t1_ - t0_) / 1100.0)
            A("dve", lambda e: e.memset(jn_t[:], 0.0), reads=mkeys, writes=["hT"], c=0.1)
            A("sp", lambda e: e.dma_start(out=grow[:], in_=b_ada[l:l + 1, 2 * D:3 * D].partition_broadcast(2)), writes=["grow"], dma=True, ne=True)
            for g in range(2 * D // WG, 3 * D // WG):
                buf, bkey = load_w(w_ada[l], g * WG, WG)
                b = nb()

                def f(e, buf=buf, b=b):
                    for kc in range(KC):
                        ins = e.matmul(ps[b][0:2, 0:WG], lhsT=scT[:, kc, :], rhs=buf[:, kc, 0:WG], start=(kc == 0), stop=(kc == KC - 1))
                    return ins
                A("pe", f, reads=[bkey, "scT"], writes=[("ps", b)], ne=True, c=1.2)
                gs_ = (g - 2 * D // WG) * WG
                A("dve", lambda e, b=b, gs_=gs_: e.tensor_tensor(out=grow[:, gs_:gs_ + WG], in0=ps[b][0:2, 0:WG], in1=grow[:, gs_:gs_ + WG], op=ALU.add),
                  reads=[("ps", b), "grow"], writes=["grow"], ne=True, c=0.3)
            A("act", lambda e: e.activation(out=grow[:], in_=grow[:], func=AF.Copy, scale=1.0 / ALPHA), reads=["grow"], writes=["grow"], ne=True, c=0.5)
            chk("ln1")
            barrier()
            sa = Alloc(nc, SCR, LIM)
            ya = Alloc(nc, Y_OFF, Y_END)
            GW = 2364
            cv = sa.t([128, 4, NFULL], F32)
            glu = [sa.t([128, GW], BF16) for _ in range(2)]
            Dg = sa.t([128, 31, 128], BF16)
            sg_t = [sa.t([128, 512], F32) for _ in range(2)]
            conv1_end = sa.off
            XLW = 2316
            NO = 2310
            xl_pads = [sa.t([128, XLW], BF16) for _ in range(2)]
            al = [ya, sa]
            DgL = [al[d].t([128, 4, 128], BF16) for d in range(2)]
            xc = [al[d].t([128, NO + 2], F32) for d in range(2)]
            xcb = [al[d].t([128, NO + 2], BF16) for d in range(2)]
            rg = [ya.t([128, NO + 2], F32)] * 2
            ig = [ya.t([128, NO + 2], F32)] * 2
            tq = [ya.t([128, NO + 2], F32)] * 2
            hs = [ya.t([128, 2048], F32)] * 2
            hc = [ya.t([128, 256], F32)] * 2
            wbd = [[al[d].t([128, 128], BF16) for _ in range(2)] for d in range(2)]
            sumr = [ya.t([128, 1], F32)] * 2
            LB = ("lru", (3, 4))
            CB = ("cv1", (0, 1, 2))
            for i in range(2):
                A("pool", lambda e, i=i: e.memset(glu[i][:], 0.0), writes=[("glu", i)], c=2.0)
            sic = [0]
            CO_R = [(0, 512, 0), (512, 1024, 512), (1024, 1536, 1024), (1536, 2048, 1536), (2078, 2334, 2048)]

            def conv1(cc):
                gi = cc % 2
                bufa, ka = load_w(w_in[l], C_CA + cc * 128, 128)
                bufg, kg = load_w(w_in[l], C_CG + cc * 128, 128)
                ranges = [(n0, n1, (15 + n0 if n0 < CTX0 else 2093), None) for (n0, n1) in FULL_R]
                ranges.append((LH0 + 113, LH0 + 128, 0, 0))
                ranges.append((RH0, RH0 + 15, 2063, 1))
                for (n0, n1, p0, fl) in ranges:
                    n = n1 - n0
                    bg = proj_fm(bufg, kg, 0, n0, n1, pool=CB)
                    sic[0] += 1
                    si = sic[0] % 2
                    A("act", lambda e, bg=bg, si=si, n=n: e.activation(out=sg_t[si][:, 0:n], in_=ps[bg][:, 0:n], func=AF.Sigmoid),
                      reads=[("ps", bg)], writes=[("sg_t", si)])
                    ba = proj_fm(bufa, ka, 0, n0, n1, pool=CB)
                    if fl is None:
                        A("dve", lambda e, ba=ba, si=si, n=n, p0=p0, gi=gi: e.tensor_tensor(out=glu[gi][:, p0:p0 + n], in0=ps[ba][:, 0:n], in1=sg_t[si][:, 0:n], op=ALU.mult),
                          reads=[("ps", ba), ("sg_t", si)], writes=[("glu", gi)])
                    else:
                        A("dve", lambda e, ba=ba, si=si, n=n, p0=p0, gi=gi, fl=fl: e.scalar_tensor_tensor(out=glu[gi][:, p0:p0 + n], in0=ps[ba][:, 0:n], scalar=flags[:, fl:fl + 1],
                                                                                                 in1=sg_t[si][:, 0:n], op0=ALU.mult, op1=ALU.mult),
                          reads=[("ps", ba), ("sg_t", si), "flags"], writes=[("glu", gi)])

                def f(e, cc=cc):
                    for k in range(31):
                        ins = e.tensor_scalar(out=Dg[:, k, :], in0=identb[:], scalar1=pcols[:, k * 4 + cc:k * 4 + cc + 1], scalar2=None, op0=ALU.mult)
                    return ins
                A("dve", f, reads=["identb", "pcols"], writes=["Dg"], c=4.0)
                for (o0, o1, t0) in CO_R:
                    b = nbp(*CB)

                    def f(e, b=b, o0=o0, o1=o1, gi=gi):
                        for k in range(31):
                            ins = e.matmul(ps[b][:, 0:o1 - o0], lhsT=Dg[:, k, :], rhs=glu[gi][:, o0 + k:o1 + k], start=(k == 0), stop=(k == 30))
                        return ins
                    A("pe", f, reads=["Dg", ("glu", gi)], writes=[("ps", b)], c=0.1 + 31 * 0.27 * (o1 - o0) / 512.0)
                    A("act", lambda e, b=b, o0=o0, o1=o1, t0=t0, cc=cc: e.activation(out=cv[:, cc, t0:t0 + o1 - o0], in_=ps[b][:, 0:o1 - o0], func=AF.Identity,
                                                                              bias=pcols[:, 124 + cc:125 + cc], scale=1.0),
                      reads=[("ps", b), "pcols"], writes=[("cv", cc)])

            for i_ in range(2):
                A("pool", lambda e, i_=i_: e.memset(xl_pads[i_][:], 0.0), writes=[("xl_pad", i_)], c=4.0)
            O_R = [(0, 512), (512, 1024), (1024, 1536), (1536, 2048), (2054, 2310)]
            for cc in range(4):
                xi = cc % 2
                xl_pad = xl_pads[xi]
                xk = ("xl_pad", xi)
                buf, bkey = load_w(w_in[l], C_XLRU + cc * 128, 128)
                for (n0, n1) in FULL_R:
                    b = proj_fm(buf, bkey, 0, n0, n1, pool=LB)
                    dst = xl_pad[:, 3 + n0:3 + n1] if n0 < CTX0 else xl_pad[:, 2057:2313]
                    A("act", lambda e, b=b, dst=dst, n=n1 - n0: e.activation(out=dst, in_=ps[b][:, 0:n], func=AF.Copy),
                      reads=[("ps", b)], writes=[xk])
                b = proj_fm(buf, bkey, 0, LH0 + 125, LH0 + 131, pool=LB)
                A("dve", lambda e, b=b, xl_pad=xl_pad: e.tensor_scalar(out=xl_pad[:, 0:3], in0=ps[b][:, 0:3], scalar1=flags[:, 0:1], scalar2=None, op0=ALU.mult),
                  reads=[("ps", b), "flags"], writes=[xk], c=0.2)
                A("dve", lambda e, b=b, xl_pad=xl_pad: e.tensor_scalar(out=xl_pad[:, 2051:2054], in0=ps[b][:, 3:6], scalar1=flags[:, 1:2], scalar2=None, op0=ALU.mult),
                  reads=[("ps", b), "flags"], writes=[xk], c=0.2)
                for d in range(2):
                    sh = 0 if d == 0 else 3
                    wcols = [pcols[:, PB + 8 + d * 16 + k * 4 + cc:PB + 9 + d * 16 + k * 4 + cc] for k in range(4)]
                    bcol = pcols[:, PB + 40 + d * 4 + cc:PB + 41 + d * 4 + cc]
                    xc_, xcb_, rg_, ig_, tq_, hs_, hc_, sumr_ = xc[d], xcb[d], rg[d], ig[d], tq[d], hs[d], hc[d], sumr[d]
                    kxc, kxcb, krg, kig, ktq, khs, khc, ksr = ("xc", d), ("xcb", d), ("rg", 0), ("ig", 0), ("tq", 0), ("hs", 0), ("hc", 0), ("sumr", 0)
                    dg_ = DgL[d]

                    def f(e, dg_=dg_, wcols=wcols):
                        for k in range(4):
                            ins = e.tensor_scalar(out=dg_[:, k, :], in0=identb[:], scalar1=wcols[k], scalar2=None, op0=ALU.mult)
                        return ins
                    A("dve", f, reads=["identb", "pcols"], writes=[("DgL", d)], c=0.6)
                    for (o0, o1) in O_R:
                        b = nbp(*LB)

                        def f(e, b=b, o0=o0, o1=o1, sh=sh, dg_=dg_, xl_pad=xl_pad):
                            for k in range(4):
                                ins = e.matmul(ps[b][:, 0:o1 - o0], lhsT=dg_[:, k, :], rhs=xl_pad[:, sh + o0 + k:sh + o1 + k], start=(k == 0), stop=(k == 3))
                            return ins
                        A("pe", f, reads=[("DgL", d), xk], writes=[("ps", b)], c=0.1 + 4 * 0.27 * (o1 - o0) / 512.0)
                        A("act", lambda e, b=b, o0=o0, o1=o1, xc_=xc_, bcol=bcol: e.activation(out=xc_[:, o0:o1], in_=ps[b][:, 0:o1 - o0], func=AF.Identity, bias=bcol, scale=1.0),
                          reads=[("ps", b), "pcols"], writes=[kxc], c=0.6)
                    A("dve", lambda e, xc_=xc_, xcb_=xcb_: e.tensor_copy(out=xcb_[:, 0:NO], in_=xc_[:, 0:NO]), reads=[kxc], writes=[kxcb], c=1.5)
                    for wi, (wsrc, dstg, kdst, bo) in enumerate(((lru_w_a, rg_, krg, 48), (lru_w_x, ig_, kig, 56))):
                        wt = wbd[d][wi]
                        A("pool", lambda e, wt=wt: e.memset(wt[:], 0.0), writes=[("wbd", d, wi), ("wbdd", d, wi, 0), ("wbdd", d, wi, 1)], c=0.3)
                        for hb in range(2):
                            A("pool", lambda e, wt=wt, hb=hb, wsrc=wsrc, d=d, cc=cc: e.dma_start(out=wt[hb * 64:(hb + 1) * 64, hb * 64:(hb + 1) * 64],
                                                                                          in_=wsrc[l, d, 2 * cc + hb, :, :]),
                              reads=[("wbd", d, wi)], writes=[("wbdd", d, wi, hb)], dma=True)
                        bias = pcols[:, PB + bo + d * 4 + cc:PB + bo + 1 + d * 4 + cc]
                        for (o0, o1) in O_R:
                            b = nbp(*LB)
                            A("pe", lambda e, b=b, wt=wt, o0=o0, o1=o1, xcb_=xcb_: e.matmul(ps[b][:, 0:o1 - o0], lhsT=wt[:], rhs=xcb_[:, o0:o1], start=True, stop=True),
                              reads=[("wbdd", d, wi, 0), ("wbdd", d, wi, 1), kxcb], writes=[("ps", b)], c=0.3)
                            A("act", lambda e, b=b, dstg=dstg, o0=o0, o1=o1, bias=bias: e.activation(out=dstg[:, o0:o1], in_=ps[b][:, 0:o1 - o0], func=AF.Sigmoid,
                                                                                                  bias=bias, scale=1.0),
                              reads=[("ps", b), "pcols"], writes=[kdst], c=0.6)
                    cAc = cA[:, d * 4 + cc:d * 4 + cc + 1]
                    A("dve", lambda e, rg_=rg_, sumr_=sumr_: e.reduce_sum(out=sumr_[:], in_=rg_[:, 0:2048], axis=mybir.AxisListType.X), reads=[krg], writes=[ksr], c=2.2)
                    A("act", lambda e, cAc=cAc, d=d, cc=cc, sumr_=sumr_: e.activation(out=csb[:, d * 8 + cc * 2:d * 8 + cc * 2 + 1], in_=sumr_[:], func=AF.Exp, scale=cAc),
                      reads=[ksr, "cA"], writes=[("csb", d, cc, 0)], c=0.2)
                    A("act", lambda e, cAc=cAc, rg_=rg_: e.activation(out=rg_[:, 0:NO], in_=rg_[:, 0:NO], func=AF.Exp, scale=cAc), reads=[krg, "cA"], writes=[krg], c=2.0)
                    A("pool", lambda e, rg_=rg_, tq_=tq_: e.tensor_tensor(out=tq_[:, 0:NO], in0=rg_[:, 0:NO], in1=rg_[:, 0:NO], op=ALU.mult), reads=[krg], writes=[ktq], c=4.5)
                    A("act", lambda e, tq_=tq_: e.activation(out=tq_[:, 0:NO], in_=tq_[:, 0:NO], func=AF.Sqrt, scale=-1.0, bias=1.0), reads=[ktq], writes=[ktq], c=2.0)
                    A("pool", lambda e, ig_=ig_, xc_=xc_: e.tensor_tensor(out=ig_[:, 0:NO], in0=ig_[:, 0:NO], in1=xc_[:, 0:NO], op=ALU.mult), reads=[kig, kxc], writes=[kig], c=4.5)
                    A("dve", lambda e, ig_=ig_, tq_=tq_: e.tensor_tensor(out=ig_[:, 0:NO], in0=ig_[:, 0:NO], in1=tq_[:, 0:NO], op=ALU.mult), reads=[kig, ktq], writes=[kig], c=2.5)
                    if d == 0:
                        A("dve", lambda e, rg_=rg_, ig_=ig_, hc_=hc_: e.tensor_tensor_scan(out=hc_[:], data0=rg_[:, 2054:2310], data1=ig_[:, 2054:2310], initial=0.0,
                                                                 op0=ALU.mult, op1=ALU.add), reads=[krg, kig], writes=[khc], c=0.7)
                        A("dve", lambda e, rg_=rg_, ig_=ig_, hs_=hs_: e.tensor_tensor_scan(out=hs_[:], data0=rg_[:, 0:2048], data1=ig_[:, 0:2048], initial=0.0,
                                                                 op0=ALU.mult, op1=ALU.add), reads=[krg, kig], writes=[khs], c=4.4)
                        A("pool", lambda e, cc=cc, hc_=hc_: e.tensor_copy(out=ylru_ctx[:, cc, :], in_=hc_[:]), reads=[khc], writes=[("ylc", cc)], c=0.6)
                        A("act", lambda e, cc=cc, hc_=hc_: e.activation(out=s0[:, cc:cc + 1], in_=hc_[:, 255:256], func=AF.Copy), reads=[khc], writes=[("s0", 0, cc)], c=0.2)
                        A("act", lambda e, cc=cc, hs_=hs_: e.activation(out=csb[:, cc * 2 + 1:cc * 2 + 2], in_=hs_[:, 2047:2048], func=AF.Copy),
                          reads=[khs], writes=[("csb", 0, cc, 1)], c=0.2)
                    else:
                        A("dve", lambda e, rg_=rg_, ig_=ig_, hc_=hc_: e.tensor_tensor_scan(out=hc_[:, ::-1], data0=rg_[:, 2309:2053:-1], data1=ig_[:, 2309:2053:-1], initial=0.0,
                                                                 op0=ALU.mult, op1=ALU.add), reads=[krg, kig], writes=[khc], c=0.7)
                        A("dve", lambda e, rg_=rg_, ig_=ig_, hs_=hs_: e.tensor_tensor_scan(out=hs_[:, ::-1], data0=rg_[:, 2047::-1], data1=ig_[:, 2047::-1], initial=0.0,
                                                                 op0=ALU.mult, op1=ALU.add), reads=[krg, kig], writes=[khs], c=4.4)
                        A("pool", lambda e, cc=cc, hc_=hc_: e.tensor_tensor(out=ylru_ctx[:, cc, :], in0=ylru_ctx[:, cc, :], in1=hc_[:], op=ALU.add),
                          reads=[khc, ("ylc", cc)], writes=[("ylc", cc)], c=0.6)
                        A("act", lambda e, cc=cc, hc_=hc_: e.activation(out=s0[:, 4 + cc:5 + cc], in_=hc_[:, 0:1], func=AF.Copy), reads=[khc], writes=[("s0", 1, cc)], c=0.2)
                        A("act", lambda e, cc=cc, hs_=hs_: e.activation(out=csb[:, 8 + cc * 2 + 1:8 + cc * 2 + 2], in_=hs_[:, 0:1], func=AF.Copy),
                          reads=[khs], writes=[("csb", 1, cc, 1)], c=0.2)
                    cmb = d * 4 + cc
                    A("sp", lambda e, cmb=cmb, rg_=rg_: e.dma_start(out=au[(cmb * 2) * 128:(cmb * 2 + 1) * 128, :], in_=rg_[:, 0:2048]), reads=[krg], writes=[("au", cmb, 0)], dma=True, c=5.0)
                    A("sp", lambda e, cmb=cmb, ig_=ig_: e.dma_start(out=au[(cmb * 2 + 1) * 128:(cmb * 2 + 2) * 128, :], in_=ig_[:, 0:2048]), reads=[kig], writes=[("au", cmb, 1)], dma=True, c=5.0)
                conv1(cc)
            csb_keys = [("csb", d, cc, j) for d in range(2) for cc in range(4) for j in range(2)]
            A("sp", lambda e: e.dma_start(out=csend[:, :], in_=csb[:]), reads=csb_keys, writes=["csend"], dma=True, ne=True)
            A("pool", lambda e: e.collective_compute("AllGather", ALU.bypass, replica_groups=RG, ins=[csend.opt()], outs=[cgath.opt()]),
              reads=["csend"], writes=["cgath"], dma="cc", ne=True)
            A("sp", lambda e: e.dma_start(out=cg[:], in_=cgath.rearrange("(r p) f -> p r f", p=128)), reads=["cgath"], writes=["cg"], dma=True, ne=True)
            cgv = cg[:].rearrange("p r (d c j) -> p r d c j", d=2, c=4)
            A("dve", lambda e: e.tensor_copy(out=chain[:, 0, 0, :], in_=s0[:, 0:4]), reads=[("s0", 0, c_) for c_ in range(4)], writes=[("chain", 0, 0)], ne=True)
            for r in range(3):
                A("dve", lambda e, r=r: e.tensor_tensor(out=chain[:, 0, r + 1, :], in0=chain[:, 0, r, :], in1=cgv[:, r, 0, :, 0], op=ALU.mult),
                  reads=[("chain", 0, r), "cg"], writes=[("chain", 0, r + 1)], ne=True)
                A("dve", lambda e, r=r: e.tensor_tensor(out=chain[:, 0, r + 1, :], in0=chain[:, 0, r + 1, :], in1=cgv[:, r, 0, :, 1], op=ALU.add),
                  reads=[("chain", 0, r + 1), "cg"], writes=[("chain", 0, r + 1)], ne=True)
            A("dve", lambda e: e.tensor_copy(out=chain[:, 1, 3, :], in_=s0[:, 4:8]), reads=[("s0", 1, c_) for c_ in range(4)], writes=[("chain", 1, 3)], ne=True)
            for r in (3, 2, 1):
                A("dve", lambda e, r=r: e.tensor_tensor(out=chain[:, 1, r - 1, :], in0=chain[:, 1, r, :], in1=cgv[:, r, 1, :, 0], op=ALU.mult),
                  reads=[("chain", 1, r), "cg"], writes=[("chain", 1, r - 1)], ne=True)
                A("dve", lambda e, r=r: e.tensor_tensor(out=chain[:, 1, r - 1, :], in0=chain[:, 1, r - 1, :], in1=cgv[:, r, 1, :, 1], op=ALU.add),
                  reads=[("chain", 1, r - 1), "cg"], writes=[("chain", 1, r - 1)], ne=True)
            for d in range(2):
                ck = [("chain", d, r) for r in range(4)]
                A("dve", lambda e, d=d: e.tensor_scalar(out=carry[:, d * 4:d * 4 + 4], in0=chain[:, d, 0, :], scalar1=flags[:, 2:3], scalar2=None, op0=ALU.mult),
                  reads=ck + ["flags"], writes=[("carry", d)], ne=True)
                for r in range(1, 4):
                    A("dve", lambda e, d=d, r=r: e.scalar_tensor_tensor(out=carry[:, d * 4:d * 4 + 4], in0=chain[:, d, r, :], scalar=flags[:, 2 + r:3 + r],
                                                                    in1=carry[:, d * 4:d * 4 + 4], op0=ALU.mult, op1=ALU.add),
                      reads=ck + ["flags", ("carry", d)], writes=[("carry", d)], ne=True)

            chk("lru1")
            barrier()
            sa = Alloc(nc, conv1_end, LIM)
            sq = [sa.t([128, 512], F32) for _ in range(2)]
            mean_t = [sa.t([128, 512], F32) for _ in range(2)]
            msq = [sa.t([128, 512], F32) for _ in range(2)]
            var_t = [sa.t([128, 512], F32) for _ in range(2)]
            tt = [sa.t([128, 512], F32) for _ in range(4)]
            cvk = [("cv", cc) for cc in range(4)]
            for ri, (n0, n1) in enumerate(full_r):
                n = n1 - n0
                rp = ri % 2
                mean_r, msq_r, var_r = mean_t[rp], msq[rp], var_t[rp]
                kme, kms, kva = ("mean_t", rp), ("msq", rp), ("var_t", rp)
                b1 = nb()

                def f(e, b1=b1, n0=n0, n1=n1):
                    for cc in range(4):
                        ins = e.matmul(ps[b1][:, 0:n1 - n0], lhsT=onesf[:], rhs=cv[:, cc, n0:n1], start=(cc == 0), stop=(cc == 3))
                    return ins
                A("pe", f, reads=cvk + ["onesf"], writes=[("ps", b1)], c=4.5)
                b2 = nb()
                for cc in range(4):
                    qi = cc % 2
                    A("act", lambda e, qi=qi, cc=cc, n0=n0, n1=n1: e.activation(out=sq[qi][:, 0:n1 - n0], in_=cv[:, cc, n0:n1], func=AF.Square),
                      reads=[("cv", cc)], writes=[("sq", qi)])
                    A("pe", lambda e, qi=qi, cc=cc, b2=b2, n=n: e.matmul(ps[b2][:, 0:n], lhsT=onesf[:], rhs=sq[qi][:, 0:n], start=(cc == 0), stop=(cc == 3)),
                      reads=[("sq", qi), "onesf"], writes=[("ps", b2)], c=1.2)
                fence([("ps", b1), ("ps", b2)])
                A("act", lambda e, b1=b1, n=n, mean_r=mean_r: e.activation(out=mean_r[:, 0:n], in_=ps[b1][:, 0:n], func=AF.Copy, scale=1.0 / 512), reads=[("ps", b1)], writes=[kme])
                A("pool", lambda e, n=n, mean_r=mean_r, msq_r=msq_r: e.tensor_tensor(out=msq_r[:, 0:n], in0=mean_r[:, 0:n], in1=mean_r[:, 0:n], op=ALU.mult), reads=[kme], writes=[kms])
                A("dve", lambda e, b2=b2, n=n, var_r=var_r, msq_r=msq_r: e.scalar_tensor_tensor(out=var_r[:, 0:n], in0=ps[b2][:, 0:n], scalar=1.0 / 512, in1=msq_r[:, 0:n], op0=ALU.mult, op1=ALU.subtract),
                  reads=[("ps", b2), kms], writes=[kva])
                A("dve", lambda e, n=n, var_r=var_r: e.tensor_scalar(out=var_r[:, 0:n], in0=var_r[:, 0:n], scalar1=EPS, scalar2=None, op0=ALU.add), reads=[kva], writes=[kva])
                A("act", lambda e, n=n, var_r=var_r: e.activation(out=var_r[:, 0:n], in_=var_r[:, 0:n], func=AF.Sqrt), reads=[kva], writes=[kva])
                A("dve", lambda e, n=n, var_r=var_r: e.reciprocal(out=var_r[:, 0:n], in_=var_r[:, 0:n]), reads=[kva], writes=[kva])
                for cc in range(4):
                    ti_ = cc
                    A("dve", lambda e, ti_=ti_, cc=cc, n0=n0, n1=n1, n=n, mean_r=mean_r: e.tensor_tensor(out=tt[ti_][:, 0:n], in0=cv[:, cc, n0:n1], in1=mean_r[:, 0:n], op=ALU.subtract),
                      reads=[("cv", cc), kme], writes=[("tt", ti_)])
                    A("dve", lambda e, ti_=ti_, cc=cc, n=n, var_r=var_r: e.scalar_tensor_tensor(out=tt[ti_][:, 0:n], in0=tt[ti_][:, 0:n], scalar=pcols[:, PB + cc:PB + cc + 1], in1=var_r[:, 0:n],
                                                                               op0=ALU.mult, op1=ALU.mult),
                      reads=[("tt", ti_), kva, "pcols"], writes=[("tt", ti_)])
                    A("act", lambda e, ti_=ti_, cc=cc, n0=n0, n1=n1, n=n: e.activation(out=Y[:, 4 + cc, n0:n1], in_=tt[ti_][:, 0:n], func=AF.Silu, bias=pcols[:, PB + 4 + cc:PB + 5 + cc], scale=1.0),
                      reads=[("tt", ti_), "pcols"], writes=[("Y", 4 + cc)])
            gt = [sa.t([128, 512], F32) for _ in range(2)]

            def gate_branch(bidx, col0):
                for cc in range(4):
                    buf, bkey = load_w(w_in[l], col0 + cc * 128, 128)
                    for (n0, n1) in full_r:
                        n = n1 - n0
                        b = proj_fm(buf, bkey, 0, n0, n1)
                        gi_ = b % 2
                        A("act", lambda e, b=b, gi_=gi_, n=n: e.activation(out=gt[gi_][:, 0:n], in_=ps[b][:, 0:n], func=AF.Silu), reads=[("ps", b)], writes=[("gt", gi_)])
                        A("dve", lambda e, gi_=gi_, n0=n0, n1=n1, n=n, cc=cc: e.tensor_tensor(out=Y[:, bidx * 4 + cc, n0:n1], in0=Y[:, bidx * 4 + cc, n0:n1], in1=gt[gi_][:, 0:n], op=ALU.mult),
                          reads=[("gt", gi_), ("Y", bidx * 4 + cc)], writes=[("Y", bidx * 4 + cc)])
            gate_branch(1, C_GCONV)

            chk("conv")
            barrier()
            sa = Alloc(nc, SCR, LIM)
            gt = [sa.t([128, 512], F32) for _ in range(2)]
            qT = sa.t([128, 4, NFULL], BF16)
            kT = sa.t([128, 2, NT], BF16)
            vS = sa.t([128, 20, 2, 128], BF16)
            cs_off = [sa.off, sa.off + NT * 4]
            cosT = sa.t([128, NT], F32); sinT = sa.t([128, NT], F32)
            E = [nc.alloc_sbuf_tensor_at("E%d_%d" % (i_, l), [128, 5, 2, 2, 2, 128], BF16, offset=cs_off[i_]) for i_ in range(2)]
            ekeys_all = [[("E", i_, j0, hp, g) for j0 in (0, 2, 4) for hp in range(2) for g in range(2)] + [("Em", i_, j) for j in range(5)] for i_ in range(2)]
            t1 = [sa.t([128, 512], F32) for _ in range(3)]
            qb = [sa.t([128, 512], BF16) for _ in range(3)]
            rpi = [0]
            den = sa.t([128, 512], F32)
            sinkbc = sa.t([128, 4, 128], F32)
            sk = sa.t([128, 8], F32)
            A("sp", lambda e: e.dma_start(out=cosT[:], in_=cos_d[:, :]), writes=["cosT"] + ekeys_all[0], dma=True)
            A("sp", lambda e: e.dma_start(out=sinT[:], in_=sin_d[:, :]), writes=["sinT"] + ekeys_all[1], dma=True)
            A("sp", lambda e: e.dma_start(out=sk[:], in_=attn_sink[l:l + 1, :].partition_broadcast(128)), writes=["sk"], dma=True)
            A("act", lambda e: e.activation(out=sk[:], in_=sk[:], func=AF.Exp), reads=["sk"], writes=["sk"])

            def f(e):
                for c_ in range(4):
                    for hp in range(2):
                        ins = e.tensor_scalar(out=sinkbc[hp * 64:(hp + 1) * 64, c_, :], in0=zeros[hp * 64:(hp + 1) * 64, :],
                                              scalar1=sk[hp * 64:(hp + 1) * 64, 2 * c_ + hp:2 * c_ + hp + 1], scalar2=None, op0=ALU.add)
                return ins
            A("dve", f, reads=["sk", "zeros"], writes=["sinkbc"])
            A("pool", lambda e: e.memset(vS[:], 0.0), writes=["vS"])

            def rope(b, dst, n0, n1):
                n = n1 - n0
                rpi[0] += 1
                i = rpi[0] % 3
                A("act", lambda e: e.activation(out=qb[i][:, 0:n], in_=ps[b][:, 0:n], func=AF.Copy), reads=[("ps", b)], writes=[("qb", i)])
                A("dve", lambda e: e.tensor_tensor(out=t1[i][:, 0:n], in0=ps[b][:, 0:n], in1=cosT[:, n0:n1], op=ALU.mult), reads=[("ps", b), "cosT", ("qb", i)], writes=[("t1", i)])
                b2 = nb()
                A("pe", lambda e: e.matmul(ps[b2][:, 0:n], lhsT=perm[:], rhs=qb[i][:, 0:n], start=True, stop=True), reads=[("qb", i), "perm"], writes=[("ps", b2)])
                A("dve", lambda e: e.tensor_tensor(out=qb[i][:, 0:n], in0=ps[b2][:, 0:n], in1=sinT[:, n0:n1], op=ALU.mult), reads=[("ps", b2), "sinT"], writes=[("qb", i)])
                A("pool", lambda e: e.tensor_tensor(out=dst, in0=qb[i][:, 0:n], in1=t1[i][:, 0:n], op=ALU.add), reads=[("qb", i), ("t1", i)], writes=["qkT"])
            for c_ in range(4):
                buf, bkey = load_w(w_in[l], C_Q + c_ * 128, 128)
                for (n0, n1) in full_r:
                    b = proj_fm(buf, bkey, 0, n0, n1)
                    rope(b, qT[:, c_, n0:n1], n0, n1)
            for g in range(2):
                for dup in range(2):
                    A("pool", lambda e, g=g, dup=dup: e.dma_start(out=wk[:, :, g * 128 + dup * 64:g * 128 + dup * 64 + 64],
                                                                in_=w_in[l][:, C_K + g * 64:C_K + g * 64 + 64].rearrange("(kc p) c -> p kc c", p=128)),
                      writes=[("wk", g, dup)], dma=True)
            for g in range(2):
                for (n0, n1) in FULL_R + [(LH0, NT)]:
                    b = nb()

                    def f(e, b=b, g=g, n0=n0, n1=n1):
                        for kc in range(KC):
                            ins = e.matmul(ps[b][:, 0:n1 - n0], lhsT=wk[:, kc, g * 128:(g + 1) * 128], rhs=hT[:, kc, n0:n1], start=(kc == 0), stop=(kc == KC - 1))
                        return ins
                    A("pe", f, reads=[("wk", g, 0), ("wk", g, 1), "hT"], writes=[("ps", b)])
                    rope(b, kT[:, g, n0:n1], n0, n1)
            bufv, kv = load_w(w_in[l], C_V, 128)
            for ti in range(20):
                b = nb()

                def f(e, b=b, ti=ti):
                    for kc in range(KC):
                        ins = e.matmul(ps[b][:, 0:128], lhsT=hT[:, kc, ti * 128:(ti + 1) * 128], rhs=bufv[:, kc, 0:128], start=(kc == 0), stop=(kc == KC - 1))
                    return ins
                A("pe", f, reads=[kv, "hT"], writes=[("ps", b)])
                A("act", lambda e, b=b, ti=ti: e.activation(out=vS[:, ti, :, 0:64], in_=ps[b][:, 0:128].rearrange("p (g d) -> p g d", g=2), func=AF.Copy),
                  reads=[("ps", b), "vS"], writes=[("vS", ti)])
            vO = sa.t([128, 20, 2, 128], BF16)
            A("pool", lambda e: e.memset(vO[:], 0.0), writes=["vO"])
            for ti in range(20):
                A("pool", lambda e, ti=ti: e.tensor_copy(out=vO[:, ti, :, 64:128], in_=vS[:, ti, :, 0:64]), reads=[("vS", ti), "vO"], writes=[("vO", ti)])
            onesE = sa.t([128, 2, 128], BF16)
            A("pool", lambda e: e.memset(onesE[:], 0.0), writes=["onesE0"])
            A("pool", lambda e: e.memset(onesE[:, 0, 0:64], 1.0), reads=["onesE0"], writes=["onesE1"])
            A("pool", lambda e: e.memset(onesE[:, 1, 64:128], 1.0), reads=["onesE0"], writes=["onesE2"])
            onesEk = ["onesE1", "onesE2"]
            sbk = [0]
            for qt in range(ntile_full):
                ei = qt % 2
                if qt < 16:
                    kl = [(LH0 // 128 if qt == 0 else qt - 1, 2 if qt == 0 else 0), (qt, None),
                          (RH0 // 128 if qt == 15 else qt + 1, 3 if qt == 15 else 1), (16, None), (17, None)]
                else:
                    kl = [(16, None), (17, None)]
                q0 = qt * 128
                for g in range(2):
                    for hp in range(2):
                        for j0 in range(0, len(kl), 2):
                            js = list(range(j0, min(j0 + 2, len(kl))))
                            sbk[0] = (sbk[0] + 1) % 3
                            b = sbk[0]

                            def f(e, b=b, js=js, g=g, hp=hp, kl=kl, q0=q0):
                                for jj, j in enumerate(js):
                                    kt = kl[j][0]
                                    ins = e.matmul(ps[b][:, jj * 256:(jj + 1) * 256], lhsT=kT[hp * 64:(hp + 1) * 64, g, kt * 128:(kt + 1) * 128],
                                                   rhs=qT[hp * 64:(hp + 1) * 64, 2 * g:2 * g + 2, q0:q0 + 128], start=True, stop=True)
                                return ins
                            A("pe", f, reads=["qkT"], writes=[("ps", b)], c=0.1 + 0.14 * len(js))
                            A("act", lambda e, b=b, js=js, g=g, hp=hp, ei=ei, j0=j0: e.activation(
                                out=E[ei][:, j0:j0 + len(js), hp, g, :, :], in_=ps[b][:, 0:256 * len(js)].rearrange("p (j c q) -> p j c q", j=len(js), c=2),
                                func=AF.Exp, scale=0.125), reads=[("ps", b)], writes=[("E", ei, j0, hp, g)])
                ek = [("E", ei, j0, hp, g) for j0 in (0, 2, 4) for hp in range(2) for g in range(2)]
                for j, (kt, mi) in enumerate(kl):
                    if mi is not None:
                        A("pool", lambda e, ei=ei, j=j, mi=mi: e.tensor_tensor(out=E[ei][:, j].rearrange("p a b c q -> p (a b c) q"),
                                                                              in0=E[ei][:, j].rearrange("p a b c q -> p (a b c) q"),
                                                                              in1=masks[:, mi:mi + 1, :].to_broadcast([128, 8, 128]), op=ALU.mult),
                          reads=ek + ["masks"], writes=[("Em", ei, j)], c=1.6)
                emk = [("Em", ei, j) for j in range(5)]
                bn_ = 3
                bd_ = 4

                def f(e, ei=ei, kl=kl, bn_=bn_, bd_=bd_):
                    cnt = len(kl) * 2
                    for g in range(2):
                        i = 0
                        for j, (kt, mi) in enumerate(kl):
                            for hp in range(2):
                                vv = vS if hp == 0 else vO
                                e.matmul(ps[bn_][:, g * 256:(g + 1) * 256], lhsT=vv[:, kt, g, :], rhs=E[ei][:, j, hp, g, :, :],
                                         start=(i == 0), stop=(i == cnt - 1))
                                i += 1
                    for g in range(2):
                        i = 0
                        for j, (kt, mi) in enumerate(kl):
                            for hp in range(2):
                                ins = e.matmul(ps[bd_][:, g * 256:(g + 1) * 256], lhsT=onesE[:, hp, :], rhs=E[ei][:, j, hp, g, :, :],
                                               start=(i == 0), stop=(i == cnt - 1))
                                i += 1
                    return ins
                A("pe", f, reads=ek + emk + [("vS", kt) for kt, _ in kl] + [("vO", kt) for kt, _ in kl] + onesEk, writes=[("ps", bn_), ("ps", bd_)], c=0.1 + 0.14 * 8 * len(kl))
                A("dve", lambda e, bd_=bd_: e.tensor_tensor(out=den[:], in0=ps[bd_][:], in1=sinkbc[:].rearrange("p c q -> p (c q)"), op=ALU.add),
                  reads=[("ps", bd_), "sinkbc"], writes=["den"])
                A("dve", lambda e: e.reciprocal(out=den[:], in_=den[:]), reads=["den"], writes=["den"])
                A("dve", lambda e, bn_=bn_, q0=q0: e.tensor_tensor(out=Y[:, 0:4, q0:q0 + 128], in0=ps[bn_][:].rearrange("p (c q) -> p c q", c=4),
                                                                in1=den[:].rearrange("p (c q) -> p c q", c=4), op=ALU.mult),
                  reads=[("ps", bn_), "den"], writes=[("Y", c_) for c_ in range(4)])
            gate_branch(0, C_GATT)

            chk("att")
            barrier()
            sa = Alloc(nc, SCR, LIM)
            gt = [sa.t([128, 512], F32) for _ in range(2)]
            a2 = [sa.t([128, 2048], F32) for _ in range(2)]
            u2 = [sa.t([128, 2048], F32) for _ in range(2)]
            h2 = [sa.t([128, 2048], F32) for _ in range(2)]
            for cc in range(4):
                for d in range(2):
                    cmb = d * 4 + cc
                    A("sp", lambda e, cmb=cmb, d=d: e.dma_start(out=a2[d][:], in_=au[(cmb * 2) * 128:(cmb * 2 + 1) * 128, :]), reads=[("au", cmb, 0)], writes=[("a2", d)], dma=True)
                    A("sp", lambda e, cmb=cmb, d=d: e.dma_start(out=u2[d][:], in_=au[(cmb * 2 + 1) * 128:(cmb * 2 + 2) * 128, :]), reads=[("au", cmb, 1)], writes=[("u2", d)], dma=True)
                A("dve", lambda e, cc=cc: e.tensor_tensor_scan(out=h2[0][:], data0=a2[0][:], data1=u2[0][:], initial=carry[:, cc:cc + 1], op0=ALU.mult, op1=ALU.add),
                  reads=[("a2", 0), ("u2", 0), ("carry", 0)], writes=[("h2", 0)])
                A("dve", lambda e, cc=cc: e.tensor_tensor_scan(out=h2[1][:, ::-1], data0=a2[1][:, ::-1], data1=u2[1][:, ::-1], initial=carry[:, 4 + cc:5 + cc],
                                                             op0=ALU.mult, op1=ALU.add),
                  reads=[("a2", 1), ("u2", 1), ("carry", 1)], writes=[("h2", 1)])
                A("pool", lambda e, cc=cc: e.tensor_tensor(out=Y[:, 8 + cc, 0:2048], in0=h2[0][:], in1=h2[1][:], op=ALU.add), reads=[("h2", 0), ("h2", 1)], writes=[("Y", 8 + cc)])
                if not last:
                    A("pool", lambda e, cc=cc: e.tensor_copy(out=Y[:, 8 + cc, 2048:2304], in_=ylru_ctx[:, cc, :]), reads=[("ylc", cc)], writes=[("Y", 8 + cc)])
            gate_branch(2, C_GLRU)

            chk("lru2")
            barrier()
            sa = Alloc(nc, SCR, LIM)
            mT = sa.t([128, KC, NFULL], BF16)
            wout = sa.t([128, KC, D], BF16)
            mrg_off = sa.off
            wg = [sa.t([128, KC, 3, 128], BF16) for _ in range(2)]
            wbr = [sa.t([128, 3, 4, 128], BF16) for _ in range(2)]
            Gt = [sa.t([128, 512], F32) for _ in range(2)]
            acc = sa.t([128, 512], F32)
            tmp = sa.t([128, 512], F32)
            for kc in range(KC):
                A("pool", lambda e, kc=kc: e.dma_start(out=wout[:, kc, :], in_=w_out[l][kc * 128:(kc + 1) * 128, :]), writes=[("wout", kc)], dma=True)
            for dc in range(8):
                wi = dc % 2
                for b_ in range(3):
                    A("pool", lambda e, wi=wi, dc=dc, b_=b_: e.dma_start(out=wg[wi][:, :, b_, :],
                                                                     in_=w_gate[l][:, b_ * D + dc * 128:b_ * D + (dc + 1) * 128].rearrange("(kc p) c -> p kc c", p=128)),
                      writes=[("wg", wi, b_)], dma=True)
                    A("pool", lambda e, wi=wi, dc=dc, b_=b_: e.dma_start(out=wbr[wi][:, b_, :, :],
                                                                     in_=w_branch[l, b_][:, dc * 128:(dc + 1) * 128].rearrange("(kc p) c -> p kc c", p=128)),
                      writes=[("wbr", wi, b_)], dma=True)
                for (n0, n1) in full_r:
                    n = n1 - n0
                    for b_ in range(3):
                        bg = nb()

                        def f(e, bg=bg, wi=wi, b_=b_, n0=n0, n1=n1):
                            for kc in range(KC):
                                ins = e.matmul(ps[bg][:, 0:n1 - n0], lhsT=wg[wi][:, kc, b_, :], rhs=hT[:, kc, n0:n1], start=(kc == 0), stop=(kc == KC - 1))
                            return ins
                        A("pe", f, reads=[("wg", wi, b_), "hT"], writes=[("ps", bg)], c=0.1 + 8 * 0.27 * n / 512.0)
                        gi_ = bg % 2
                        A("act", lambda e, bg=bg, gi_=gi_, n=n, b_=b_, dc=dc: e.activation(out=Gt[gi_][:, 0:n], in_=ps[bg][:, 0:n], func=AF.Sigmoid,
                                                                                     bias=pcols[:, PB + 72 + b_ * 8 + dc:PB + 73 + b_ * 8 + dc], scale=1.0),
                          reads=[("ps", bg), "pcols"], writes=[("Gt", gi_)])
                        bp = nb()

                        def f(e, bp=bp, wi=wi, b_=b_, n0=n0, n1=n1):
                            for kc in range(4):
                                ins = e.matmul(ps[bp][:, 0:n1 - n0], lhsT=wbr[wi][:, b_, kc, :], rhs=Y[:, b_ * 4 + kc, n0:n1], start=(kc == 0), stop=(kc == 3))
                            return ins
                        A("pe", f, reads=[("wbr", wi, b_)] + [("Y", b_ * 4 + kc) for kc in range(4)], writes=[("ps", bp)], c=0.1 + 4 * 0.27 * n / 512.0)
                        if b_ == 0:
                            A("dve", lambda e, bp=bp, gi_=gi_, n=n: e.tensor_tensor(out=acc[:, 0:n], in0=ps[bp][:, 0:n], in1=Gt[gi_][:, 0:n], op=ALU.mult),
                              reads=[("ps", bp), ("Gt", gi_)], writes=["acc"])
                        else:
                            A("dve", lambda e, bp=bp, gi_=gi_, n=n: e.tensor_tensor(out=tmp[:, 0:n], in0=ps[bp][:, 0:n], in1=Gt[gi_][:, 0:n], op=ALU.mult),
                              reads=[("ps", bp), ("Gt", gi_)], writes=["tmp"])
                            if b_ == 1:
                                A("pool", lambda e, n=n: e.tensor_tensor(out=acc[:, 0:n], in0=acc[:, 0:n], in1=tmp[:, 0:n], op=ALU.add), reads=["acc", "tmp"], writes=["acc"])
                            else:
                                A("pool", lambda e, n=n, dc=dc, n0=n0, n1=n1: e.tensor_tensor(out=mT[:, dc, n0:n1], in0=acc[:, 0:n], in1=tmp[:, 0:n], op=ALU.add),
                                  reads=["acc", "tmp"], writes=[("mT", dc)])

            chk("merge")
            barrier()
            sa = Alloc(nc, mrg_off, LIM)
            gbc = [sa.t([128, D], F32) for _ in range(1 if last else 2)]
            lng = sa.t([128, D], F32); lnb = sa.t([128, D], F32)
            A("sp", lambda e: e.dma_start(out=lng[:], in_=ln_g[l:l + 1, :].partition_broadcast(128)), writes=["lng"], dma=True)
            A("sp", lambda e: e.dma_start(out=lnb[:], in_=ln_b[l:l + 1, :].partition_broadcast(128)), writes=["lnb"], dma=True)
            for r in range(len(gbc)):
                for h_ in range(2):
                    b = nb()
                    A("pe", lambda e, b=b, r=r, h_=h_: e.matmul(ps[b][:, 0:512], lhsT=sel[0:2, r * 128:(r + 1) * 128], rhs=grow[0:2, h_ * 512:(h_ + 1) * 512], start=True, stop=True),
                      reads=["sel", "grow"], writes=[("ps", b)])
                    fence([("ps", b)])
                    A("act", lambda e, b=b, r=r, h_=h_: e.activation(out=gbc[r][:, h_ * 512:(h_ + 1) * 512], in_=ps[b][:, 0:512], func=AF.Copy), reads=[("ps", b)], writes=[("gbc", r, h_)])
            mk = [("mT", dc) for dc in range(8)]
            wk_ = [("wout", kc) for kc in range(8)]
            ya = Alloc(nc, Y_OFF, Y_END)
            G4 = 3
            xr = [ya.t([128, D], F32) for _ in range(2 * G4)]
            zt = [ya.t([128, D], F32) for _ in range(2 * G4)]
            st2 = [sa.t([128, G4, 2, 6], F32) for _ in range(2)]
            mv2 = [sa.t([128, G4, 2], F32) for _ in range(2)]
            rs2 = [sa.t([128, G4], F32) for _ in range(2)]
            all_tiles = list(range(ntile_full))
            groups = [all_tiles[i:i + G4] for i in range(0, ntile_full, G4)]
            final = last or l == nlayers - 1
            for gi, grp in enumerate(groups):
                s_ = gi % 2
                ng = len(grp)
                for j, ti in enumerate(grp):
                    bi = s_ * G4 + j
                    r = 1 if ti >= 16 else 0
                    if l == 0:
                        src = x_in[ti * 128:(ti + 1) * 128, :] if ti < 16 else ctx_in[(ti - 16) * 128:(ti - 15) * 128, :]
                    else:
                        src = xs[ti * 128:(ti + 1) * 128, :]
                    A("sp", lambda e, bi=bi, src=src: e.dma_start(out=xr[bi][:], in_=src), reads=[("xs", ti)], writes=[("xr", bi)], dma=True)
                    for h_ in range(2):
                        b = nb()

                        def f(e, b=b, ti=ti, h_=h_):
                            for dc in range(8):
                                ins = e.matmul(ps[b][:, 0:512], lhsT=mT[:, dc, ti * 128:(ti + 1) * 128], rhs=wout[:, dc, h_ * 512:(h_ + 1) * 512], start=(dc == 0), stop=(dc == 7))
                            return ins
                        A("pe", f, reads=mk + wk_, writes=[("ps", b)], c=2.3)
                        A("dve", lambda e, b=b, bi=bi, r=r, h_=h_: e.tensor_tensor(out=zt[bi][:, h_ * 512:(h_ + 1) * 512], in0=ps[b][:, 0:512], in1=gbc[r][:, h_ * 512:(h_ + 1) * 512], op=ALU.mult),
                          reads=[("ps", b), ("gbc", r, h_)], writes=[("zt", bi, h_)])
                        A("pool", lambda e, bi=bi, h_=h_: e.tensor_tensor(out=zt[bi][:, h_ * 512:(h_ + 1) * 512], in0=zt[bi][:, h_ * 512:(h_ + 1) * 512], in1=xr[bi][:, h_ * 512:(h_ + 1) * 512], op=ALU.add),
                          reads=[("zt", bi, h_), ("xr", bi)], writes=[("zt", bi, h_)])
                for j in range(ng):
                    bi = s_ * G4 + j
                    for h_ in range(2):
                        A("dve", lambda e, bi=bi, j=j, h_=h_, s_=s_: e.bn_stats(out=st2[s_][:, j, h_, :], in_=zt[bi][:, h_ * 512:(h_ + 1) * 512]),
                          reads=[("zt", bi, h_)], writes=[("st2", s_, j, h_)])
                for j in range(ng):
                    A("dve", lambda e, j=j, s_=s_: e.bn_aggr(out=mv2[s_][:, j, :], in_=st2[s_][:, j, :, :]),
                      reads=[("st2", s_, j, 0), ("st2", s_, j, 1)], writes=[("mv2", s_, j)])
                mvk = [("mv2", s_, j) for j in range(ng)]
                A("dve", lambda e, s_=s_, ng=ng: e.tensor_scalar(out=rs2[s_][:, 0:ng], in0=mv2[s_][:, 0:ng, 1], scalar1=EPS / (ALPHA * ALPHA), scalar2=None, op0=ALU.add),
                  reads=mvk, writes=[("rs2", s_)])
                A("act", lambda e, s_=s_, ng=ng: e.activation(out=rs2[s_][:, 0:ng], in_=rs2[s_][:, 0:ng], func=AF.Sqrt), reads=[("rs2", s_)], writes=[("rs2", s_)])
                A("dve", lambda e, s_=s_, ng=ng: e.reciprocal(out=rs2[s_][:, 0:ng], in_=rs2[s_][:, 0:ng]), reads=[("rs2", s_)], writes=[("rs2", s_)])
                for j in range(ng):
                    bi = s_ * G4 + j
                    zk = [("zt", bi, 0), ("zt", bi, 1)]
                    A("dve", lambda e, bi=bi, j=j, s_=s_: e.scalar_tensor_tensor(out=zt[bi][:], in0=zt[bi][:], scalar=mv2[s_][:, j, 0:1], in1=lng[:], op0=ALU.subtract, op1=ALU.mult),
                      reads=zk + [("mv2", s_, j), "lng"], writes=zk)
                for j in range(ng):
                    bi = s_ * G4 + j
                    zk = [("zt", bi, 0), ("zt", bi, 1)]
                    A("dve", lambda e, bi=bi, j=j, s_=s_: e.scalar_tensor_tensor(out=zt[bi][:], in0=zt[bi][:], scalar=rs2[s_][:, j:j + 1], in1=lnb[:], op0=ALU.mult, op1=ALU.add),
                      reads=zk + [("rs2", s_), "lnb"], writes=zk)
                for j, ti in enumerate(grp):
                    bi = s_ * G4 + j
                    zk = [("zt", bi, 0), ("zt", bi, 1)]
                    if final:
                        if ti < 16:
                            A("sp", lambda e, bi=bi, ti=ti: e.dma_start(out=out[ti * 128:(ti + 1) * 128, :], in_=zt[bi][:]), reads=zk, writes=[("out", ti)], dma=True)
                    else:
                        A("sp", lambda e, bi=bi, ti=ti: e.dma_start(out=xs[ti * 128:(ti + 1) * 128, :], in_=zt[bi][:]), reads=zk, writes=[("xs", ti)], dma=True)
                        if ti == 0:
                            A("sp", lambda e, bi=bi: e.dma_start(out=send[0:128, :], in_=zt[bi][:]), reads=zk, writes=[("send", 0)], dma=True)
                        if ti == 15:
                            A("sp", lambda e, bi=bi: e.dma_start(out=send[128:256, :], in_=zt[bi][:]), reads=zk, writes=[("send", 1)], dma=True)
            if not (last or l == nlayers - 1):
                A("pool", lambda e: e.collective_compute("AllGather", ALU.bypass, replica_groups=RG, ins=[send.opt()], outs=[gath.opt()]),
                  reads=[("send", 0), ("send", 1)], writes=["gath"], dma="cc", ne=True)
        try:
            for l_ in range(nlayers):
                layer(l_)
        except _Stop:
            pass
        if dbg:
            dbg_h = nc.dram_tensor("dbg_h", [128, KC * NT], BF16, kind="ExternalOutput").ap()
            dbg_y = nc.dram_tensor("dbg_y", [128, 12 * NFULL], BF16, kind="ExternalOutput").ap()
            A("sp", lambda e: e.dma_start(out=dbg_h[:, :], in_=hT[:].rearrange("p a b -> p (a b)")), reads=["hT"], writes=["dbg_h"], dma=True)
            A("sp", lambda e: e.dma_start(out=dbg_y[:, :], in_=Y[:].rearrange("p a b -> p (a b)")), reads=[("Y", i) for i in range(12)], writes=["dbg_y"], dma=True)
            A("sp", None, reads=["dbg_h", "dbg_y"])
        A("sp", None, reads=[("out", ti) for ti in range(16)])
        S.run()
    return nc


def _rope_tables(pos):
    quarter = 16
    inv = (10000.0 ** (-np.arange(quarter, dtype=np.float32) / quarter)).astype(np.float32)
    row = (pos // 64).astype(np.float32)
    col = (pos % 64).astype(np.float32)
    ang_r = row[:, None] * inv[None, :]
    ang_c = col[:, None] * inv[None, :]
    ang = np.concatenate([ang_r, ang_r, ang_c, ang_c], -1).astype(np.float32)
    return np.cos(ang).astype(np.float32), np.sin(ang).astype(np.float32)


def make_in_maps(inputs):
    x = np.ascontiguousarray(inputs["x"], dtype=np.float32)
    B, Sq, _ = x.shape
    bf = ml_dtypes.bfloat16
    perm = np.zeros((128, 128), np.float32)
    for hd in range(2):
        o = hd * 64
        for m in range(64):
            seg, i = divmod(m, 16)
            if seg == 0:
                perm[o + m + 16, o + m] = -1.0
            elif seg == 1:
                perm[o + m - 16, o + m] = 1.0
            elif seg == 2:
                perm[o + m + 16, o + m] = -1.0
            else:
                perm[o + m - 16, o + m] = 1.0
    jj = np.arange(128)[:, None]
    qq = np.arange(128)[None, :]
    maskP = (jj >= qq).astype(np.float32)
    maskN = (jj <= qq).astype(np.float32)
    sel = np.zeros((2, 256), np.float32)
    sel[0, 0:128] = 1.0
    sel[1, 128:256] = 1.0
    wnames = ["w_ada", "b_ada", "w_in", "attn_sink", "conv_dw_w", "conv_dw_b", "conv_ln_g", "conv_ln_b", "lru_conv_w", "lru_conv_b",
              "lru_w_a", "lru_b_a", "lru_w_x", "lru_b_x", "lru_lambda", "w_branch", "w_gate", "b_gate", "w_out", "ln_g", "ln_b"]
    shared = {k: np.ascontiguousarray(inputs[k], dtype=np.float32) for k in wnames}
    shared["perm"] = perm.astype(bf)
    shared["identb"] = np.eye(128, dtype=np.float32).astype(bf)
    shared["identf"] = np.eye(128, dtype=np.float32)
    shared["sel"] = sel
    maps = []
    for c in range(8):
        b, r = divmod(c, 4)
        t0 = r * T_OWN
        m = dict(shared)
        m["x_in"] = x[b, t0:t0 + T_OWN]
        xh = np.zeros((256, D), np.float32)
        if r > 0:
            xh[0:128] = x[b, t0 - 128:t0]
        if r < 3:
            xh[128:256] = x[b, t0 + T_OWN:t0 + T_OWN + 128]
        m["xh_in"] = xh
        m["ctx_in"] = np.ascontiguousarray(inputs["ctx"][b], dtype=np.float32)
        m["c_in"] = np.stack([inputs["c"][b], inputs["c_ctx"]]).astype(np.float32)
        pos = np.zeros(NT, np.int64)
        pos[0:T_OWN] = t0 + np.arange(T_OWN)
        pos[LH0:LH0 + 128] = np.clip(t0 - 128 + np.arange(128), 0, Sq - 1)
        pos[RH0:RH0 + 128] = np.clip(t0 + T_OWN + np.arange(128), 0, Sq - 1)
        cos, sin = _rope_tables(pos)
        cos[CTX0:CTX0 + 256] = 1.0
        sin[CTX0:CTX0 + 256] = 0.0
        m["cosT"] = np.ascontiguousarray(np.concatenate([cos.T, cos.T], 0))
        m["sinT"] = np.ascontiguousarray(np.concatenate([sin.T, sin.T], 0))
        hl = 1.0 if r > 0 else 0.0
        hr = 1.0 if r < 3 else 0.0
        m["masks"] = np.concatenate([maskP, maskN, maskP * hl, maskN * hr], 1).astype(bf)
        fl = np.zeros((128, 8), np.float32)
        fl[:, 0] = hl
        fl[:, 1] = hr
        fl[:, 2 + r] = 1.0
        m["flags"] = fl
        maps.append(m)
    return maps


_NC_CACHE = {}


def kernel(**inputs):
    if "nc" not in _NC_CACHE:
        _NC_CACHE["nc"] = build_nc()
    nc = _NC_CACHE["nc"]
    maps = make_in_maps(inputs)
    res = run_bass_kernel_spmd(nc, maps, core_ids=list(range(8)))
    x = inputs["x"]
    outp = np.zeros(x.shape, np.float32)
    for c in range(8):
        b, r = divmod(c, 4)
        outp[b, r * T_OWN:(r + 1) * T_OWN] = res.results[c]["out"]
    return outp
```
